# Optimizing a Trainium2 kernel written in Bass

```python
import jax, jax.numpy as jnp
from jax import lax
import numpy as np

D_MODEL = 1024
BATCH = 16
SEQ = 2048
DEPTH = 1

CHUNK = 64
ROPE_THETA = 10000.0
RMS_EPS = 1e-6
NEG_INF = -1e30

A_HEADS = 8
A_HEAD_DIM = 64
A_WIDTH = A_HEADS * A_HEAD_DIM
IDX_HEADS = 8
IDX_DIM = 64
IDX_ROPE_DIM = 32
TOPK_MAX = 256
SPARSE_Q_BLOCK = 32

B_HEADS = 8
B_NOPE_DIM = 64
B_ROPE_DIM = 32
B_QK_DIM = B_NOPE_DIM + B_ROPE_DIM
B_V_DIM = 64
B_WIDTH = B_HEADS * B_V_DIM
Q_LORA = 384
KV_LORA = 256
DENSE_Q_BLOCK = 128

D_MIX = A_WIDTH + B_WIDTH
IN_SPLITS = (A_WIDTH, A_WIDTH, A_WIDTH, A_WIDTH, IDX_HEADS * IDX_DIM, IDX_DIM, IDX_HEADS,
             Q_LORA, KV_LORA, B_ROPE_DIM, B_WIDTH)
D_IN = 4 * A_WIDTH + IDX_HEADS * IDX_DIM + IDX_DIM + IDX_HEADS + Q_LORA + KV_LORA + B_ROPE_DIM + B_WIDTH

kernel_name = "hybrid_dsa_mla_parallel_heads"


def rms_norm(x, g):
    xf = x.astype(jnp.float32)
    y = xf * lax.rsqrt(jnp.mean(xf * xf, axis=-1, keepdims=True) + RMS_EPS)
    return (y * g.astype(jnp.float32)).astype(x.dtype)


def rope(x, pos):
    d = x.shape[-1]
    half = d // 2
    freqs = jnp.power(ROPE_THETA, -jnp.arange(half, dtype=jnp.float32) * 2.0 / d)
    ang = pos.astype(jnp.float32)[:, None] * freqs[None, :]
    cos = jnp.cos(ang)[None, :, None, :]
    sin = jnp.sin(ang)[None, :, None, :]
    xf = x.astype(jnp.float32)
    x1, x2 = xf[..., :half], xf[..., half:]
    return jnp.concatenate([x1 * cos - x2 * sin, x2 * cos + x1 * sin], axis=-1).astype(x.dtype)


def chunk_limit(pos):
    return (pos // CHUNK + 1) * CHUNK


def indexer_sparse_attention(q, k, v, q_idx, k_idx, w_idx, pos):
    B, S, H, dh = q.shape
    topk = min(TOPK_MAX, S // 4)
    nblk = S // SPARSE_Q_BLOCK
    key_pos = jnp.arange(S, dtype=jnp.int32)
    idx_scale = (IDX_DIM * IDX_HEADS) ** -0.5
    att_scale = dh ** -0.5

    def to_blocks(a):
        return jnp.moveaxis(a.reshape(B, nblk, SPARSE_Q_BLOCK, *a.shape[2:]), 1, 0)

    def block(args):
        qb, qib, wb, pb = args
        limit = chunk_limit(pb)
        rel = jax.nn.relu(jnp.einsum('bqhd,bsd->bqhs', qib, k_idx).astype(jnp.float32))
        score = jnp.einsum('bqhs,bqh->bqs', rel, wb.astype(jnp.float32)) * idx_scale
        admissible = key_pos[None, :] < limit[:, None]
        score = jnp.where(admissible[None], score, NEG_INF)
        _, sel = lax.top_k(score, topk)
        valid = sel < limit[None, :, None]
        k_sel = jax.vmap(lambda kb, ib: kb[ib])(k, sel)
        v_sel = jax.vmap(lambda vb, ib: vb[ib])(v, sel)
        logits = jnp.einsum('bqhd,bqkhd->bhqk', qb, k_sel).astype(jnp.float32) * att_scale
        logits = jnp.where(valid[:, None], logits, NEG_INF)
        p = jax.nn.softmax(logits, axis=-1).astype(v.dtype)
        return jnp.einsum('bhqk,bqkhd->bqhd', p, v_sel)

    out = lax.map(block, (to_blocks(q), to_blocks(q_idx), to_blocks(w_idx),
                          pos.reshape(nblk, SPARSE_Q_BLOCK)))
    return jnp.moveaxis(out, 0, 1).reshape(B, S, H, dh)


def chunk_causal_attention(q, k, v, pos):
    B, S, H, dq = q.shape
    dv = v.shape[-1]
    nblk = S // DENSE_Q_BLOCK
    scale = dq ** -0.5

    def block(args):
        qb, pb = args
        logits = jnp.einsum('bqhd,bshd->bhqs', qb, k).astype(jnp.float32) * scale
        mask = pos[None, :] < chunk_limit(pb)[:, None]
        logits = jnp.where(mask[None, None], logits, NEG_INF)
        p = jax.nn.softmax(logits, axis=-1).astype(v.dtype)
        return jnp.einsum('bhqs,bshd->bqhd', p, v)

    qbl = jnp.moveaxis(q.reshape(B, nblk, DENSE_Q_BLOCK, H, dq), 1, 0)
    out = lax.map(block, (qbl, pos.reshape(nblk, DENSE_Q_BLOCK)))
    return jnp.moveaxis(out, 0, 1).reshape(B, S, H, dv)


def setup_inputs(seed: int = 0) -> dict:
    key = jax.random.key(seed)
    ks = jax.random.split(key, 12)
    f32 = jnp.float32

    def gain(k, n):
        return 1.0 + 0.02 * jax.random.normal(k, (DEPTH, n), f32)

    x = jax.random.normal(ks[0], (BATCH, SEQ, D_MODEL), f32)
    norm_gain = gain(ks[1], D_MODEL)
    w_in = jax.random.normal(ks[2], (DEPTH, D_MODEL, D_IN), f32) * D_MODEL ** -0.5
    a_q_norm = gain(ks[3], A_HEAD_DIM)
    a_k_norm = gain(ks[4], A_HEAD_DIM)
    b_q_latent_norm = gain(ks[5], Q_LORA)
    b_kv_latent_norm = gain(ks[6], KV_LORA)
    w_uq = jax.random.normal(ks[7], (DEPTH, Q_LORA, B_HEADS * B_QK_DIM), f32) * Q_LORA ** -0.5
    w_ukv = jax.random.normal(ks[8], (DEPTH, KV_LORA, B_HEADS * (B_NOPE_DIM + B_V_DIM)), f32) * KV_LORA ** -0.5
    b_q_norm = gain(ks[9], B_QK_DIM)
    b_k_norm = gain(ks[10], B_QK_DIM)
    w_out = jax.random.normal(ks[11], (DEPTH, D_MIX, D_MODEL), f32) * D_MIX ** -0.5
    return {"x": x, "norm_gain": norm_gain, "w_in": w_in, "a_q_norm": a_q_norm,
            "a_k_norm": a_k_norm, "b_q_latent_norm": b_q_latent_norm,
            "b_kv_latent_norm": b_kv_latent_norm, "w_uq": w_uq, "w_ukv": w_ukv,
            "b_q_norm": b_q_norm, "b_k_norm": b_k_norm, "w_out": w_out}


def reference(x, norm_gain, w_in, a_q_norm, a_k_norm, b_q_latent_norm, b_kv_latent_norm,
              w_uq, w_ukv, b_q_norm, b_k_norm, w_out):
    B, S, _ = x.shape
    pos = jnp.arange(S, dtype=jnp.int32)
    offsets = [int(o) for o in np.cumsum(IN_SPLITS)[:-1]]
    h = x
    for l in range(DEPTH):
        xn = rms_norm(h, norm_gain[l])
        proj = xn @ w_in[l]
        (q_a, k_a, v_a, g_a, q_i, k_i, w_i,
         c_q, c_kv, k_rope, g_b) = jnp.split(proj, offsets, axis=-1)

        q_a = rope(rms_norm(q_a.reshape(B, S, A_HEADS, A_HEAD_DIM), a_q_norm[l]), pos)
        k_a = rope(rms_norm(k_a.reshape(B, S, A_HEADS, A_HEAD_DIM), a_k_norm[l]), pos)
        v_a = v_a.reshape(B, S, A_HEADS, A_HEAD_DIM)
        q_i = q_i.reshape(B, S, IDX_HEADS, IDX_DIM)
        q_i = jnp.concatenate([rope(q_i[..., :IDX_ROPE_DIM], pos), q_i[..., IDX_ROPE_DIM:]], axis=-1)
        k_i = k_i[:, :, None, :]
        k_i = jnp.concatenate([rope(k_i[..., :IDX_ROPE_DIM], pos), k_i[..., IDX_ROPE_DIM:]], axis=-1)[:, :, 0, :]
        o_a = indexer_sparse_attention(q_a, k_a, v_a, q_i, k_i, w_i, pos)

        q_b = (rms_norm(c_q, b_q_latent_norm[l]) @ w_uq[l]).reshape(B, S, B_HEADS, B_QK_DIM)
        kv = (rms_norm(c_kv, b_kv_latent_norm[l]) @ w_ukv[l]).reshape(B, S, B_HEADS, B_NOPE_DIM + B_V_DIM)
        k_nope, v_b = kv[..., :B_NOPE_DIM], kv[..., B_NOPE_DIM:]
        k_rope_h = jnp.broadcast_to(k_rope[:, :, None, :], (B, S, B_HEADS, B_ROPE_DIM))
        k_b = jnp.concatenate([k_nope, k_rope_h], axis=-1)
        q_b = rms_norm(q_b, b_q_norm[l])
        k_b = rms_norm(k_b, b_k_norm[l])
        q_b = jnp.concatenate([q_b[..., :B_NOPE_DIM], rope(q_b[..., B_NOPE_DIM:], pos)], axis=-1)
        k_b = jnp.concatenate([k_b[..., :B_NOPE_DIM], rope(k_b[..., B_NOPE_DIM:], pos)], axis=-1)
        o_b = chunk_causal_attention(q_b, k_b, v_b, pos)

        mixed = jnp.concatenate([o_a.reshape(B, S, A_WIDTH) * jax.nn.silu(g_a),
                                 o_b.reshape(B, S, B_WIDTH) * jax.nn.silu(g_b)], axis=-1)
        h = h + mixed @ w_out[l]
    return h
```

```python
import numpy as np
import os
VAR = os.environ.get('KVAR', '')
from contextlib import ExitStack
import concourse.bass as bass
import concourse.mybir as mybir
from concourse.bass_utils import run_bass_kernel_spmd

F32 = mybir.dt.float32
BF16 = mybir.dt.bfloat16
I32 = mybir.dt.int32
ALU = mybir.AluOpType
AF = mybir.ActivationFunctionType
AX = mybir.AxisListType

S = 2048
DM = 1024
NCORES = 8
SEQ_PER_CORE = 2
EPS = 1e-6
NIT = 24
TOPK = 256
NEG = -30000.0
PI = float(np.pi)

ENGS = ("pe", "act", "dve", "pool", "sp")


class Prog:
    uid = 0

    def __init__(self, nc, st):
        Prog.uid += 1
        self.nc = nc
        self.st = st
        self.sems = {("eng", e): st.enter_context(nc.semaphore(f"se_{e}_{Prog.uid}")) for e in ENGS}
        self.cnt = {e: 0 for e in ENGS}
        self.ops = {e: [] for e in ENGS}
        self.waited = {e: {} for e in ENGS}
        self.lw = {}
        self.rd = {}
        self.chcnt = {}

    def _deps(self, e, reads, writes):
        deps = {}

        def add(s, v):
            if deps.get(s, 0) < v:
                deps[s] = v
        for k in reads:
            t = self.lw.get(k)
            if t:
                add(*t)
        for k in writes:
            t = self.lw.get(k)
            if t:
                add(*t)
            for s, v in self.rd.get(k, {}).items():
                add(s, v)
        out = []
        for s, v in deps.items():
            if e == "pe" and s == ("eng", "pe"):
                continue
            if self.waited[e].get(s, 0) >= v:
                continue
            self.waited[e][s] = v
            out.append((s, v))
        return out

    def _commit(self, tok, reads, writes):
        for k in reads:
            d = self.rd.setdefault(k, {})
            if d.get(tok[0], 0) < tok[1]:
                d[tok[0]] = tok[1]
        for k in writes:
            self.lw[k] = tok
            self.rd[k] = {}

    def op(self, e, fn, reads=(), writes=()):
        waits = self._deps(e, reads, writes)
        self.cnt[e] += 1
        tok = (("eng", e), self.cnt[e])
        self.ops[e].append((waits, fn, tok[0], 1))
        self._commit(tok, reads, writes)

    def dma(self, e, chan, fn, reads=(), writes=()):
        waits = self._deps(e, reads, writes)
        key = ("ch", chan)
        if key not in self.sems:
            self.sems[key] = self.st.enter_context(self.nc.semaphore(f"sc_{chan}_{Prog.uid}"))
            self.chcnt[key] = 0
        self.chcnt[key] += 16
        tok = (key, self.chcnt[key])
        self.ops[e].append((waits, fn, key, 16))
        self._commit(tok, reads, writes)

    def finish(self):
        waits = []
        for key, v in self.chcnt.items():
            if self.waited["sp"].get(key, 0) < v:
                waits.append((key, v))
        self.ops["sp"].append((waits, None, None, 0))

    def emit(self):
        nc = self.nc
        engobj = {"pe": nc.tensor, "act": nc.scalar, "dve": nc.vector, "pool": nc.gpsimd, "sp": nc.sync}
        with nc.Block() as block:
            def run(e):
                eng = engobj[e]
                for waits, fn, inc, amt in self.ops[e]:
                    for s, v in waits:
                        eng.wait_ge(self.sems[s], v)
                    if fn is not None:
                        ins = fn(eng)
                        ins.then_inc(self.sems[inc], amt)

            @block.tensor
            def _(t):
                run("pe")

            @block.scalar
            def _(t):
                run("act")

            @block.vector
            def _(t):
                run("dve")

            @block.gpsimd
            def _(t):
                run("pool")

            @block.sync
            def _(t):
                run("sp")


def run_pipeline(jobs):
    n = len(jobs)
    S = max(len(j) for j in jobs)
    states = [dict() for _ in jobs]
    for step in range(n + S - 1):
        for st_i in range(S):
            c = step - st_i
            if 0 <= c < n and st_i < len(jobs[c]):
                jobs[c][st_i](states[c])


class Rot:
    def __init__(self, items):
        self.items = list(items)
        self.i = 0

    def next(self):
        it = self.items[self.i % len(self.items)]
        self.i += 1
        return it


def build_nc(stage=99, dbg=None, bstage=99, skipA=False, nseq=SEQ_PER_CORE):
    dbg = dbg if dbg is not None else {}
    nc = bass.Bass("TRN2", target_bir_lowering=False)
    x_d = nc.dram_tensor("x", [SEQ_PER_CORE, S, DM], F32, kind="ExternalInput").ap()
    win_d = nc.dram_tensor("w_in", [DM, 3816], F32, kind="ExternalInput").ap()
    wuq_d = nc.dram_tensor("w_uq", [384, 768], F32, kind="ExternalInput").ap()
    wukv_d = nc.dram_tensor("w_ukv", [256, 1024], F32, kind="ExternalInput").ap()
    wout_d = nc.dram_tensor("w_out", [DM, DM], F32, kind="ExternalInput").ap()
    cols_d = nc.dram_tensor("cols", [128, 32], F32, kind="ExternalInput").ap()
    mats_d = nc.dram_tensor("mats", [128, 6, 128], F32, kind="ExternalInput").ap()
    tabs_d = nc.dram_tensor("tabs", [4, 128, 4, 512], F32, kind="ExternalInput").ap()
    out_d = nc.dram_tensor("out", [SEQ_PER_CORE, S, DM], F32, kind="ExternalOutput").ap()
    dbg_d = {}
    for name, shape in dbg.items():
        dbg_d[name] = nc.dram_tensor("dbg_" + name, list(shape), F32, kind="ExternalOutput").ap()

    top = ExitStack()
    with top:
        def sb(name, shape, dt, st=top):
            return st.enter_context(nc.sbuf_tensor("sb_" + name, list(shape), dt))

        cols = sb("cols", [128, 32], F32)
        cols2 = sb("cols2", [128, 8], F32)
        matsf = sb("matsf", [128, 6, 128], F32)
        mats = sb("matsb", [128, 6, 128], BF16)
        ident = mats[:, 0, :]
        permA = mats[:, 1, :]
        permI = mats[:, 2, :]
        ones = mats[:, 3, :]
        bdA = mats[:, 4, :]
        zeros = mats[:, 5, :]
        fin260 = mats[:].rearrange("p a b -> p (a b)")[:, 0:260]
        mixed = sb("mixed", [128, 16, 1024], BF16)
        PS = [top.enter_context(nc.psum_tensor(f"ps{i}", [128, 512], F32)) for i in range(7)]
        PST = top.enter_context(nc.psum_tensor("pst", [128, 1024], BF16))

        C_NG, C_AQ, C_AK, C_CQ, C_CKV, C_BQ, C_BK = 0, 8, 9, 10, 13, 15, 16
        C_FA, C_FI, C_NSA, C_NSI, C_NCMI, C_OMI = 17, 18, 19, 20, 21, 22

        st0 = ExitStack()
        with st0:
            p = Prog(nc, st0)
            p.dma("sp", "c0", lambda e: e.dma_start(out=cols[:], in_=cols_d[:, :]), writes=["cols"])
            p.dma("sp", "c1", lambda e: e.dma_start(out=matsf[:], in_=mats_d[:, :, :]), writes=["matsf"])
            p.op("dve", lambda e: e.tensor_copy(out=mats[:], in_=matsf[:]), reads=["matsf"], writes=["mats"])
            p.op("dve", lambda e: e.tensor_scalar(out=cols2[:, 0:1], in0=cols[:, C_AQ:C_AQ + 1], scalar1=0.125,
                                                  scalar2=None, op0=ALU.mult), reads=["cols"], writes=["cols2"])
            p.op("dve", lambda e: e.tensor_scalar(out=cols2[:, 1:2], in0=cols[:, C_BQ:C_BQ + 1],
                                                  scalar1=float(96.0 ** -0.5), scalar2=None, op0=ALU.mult),
                 reads=["cols"], writes=["cols2"])
            p.op("dve", lambda e: e.memset(cols2[:, 7:8], 1024.0 * EPS), writes=["cols2"])
            p.op("dve", lambda e: e.memset(cols2[:, 6:7], EPS), writes=["cols2"])
            p.finish()
            p.emit()

        def colap(i, lo=0, hi=128):
            return cols[lo:hi, i:i + 1]

        def common_p0(p, st, seq, b, xnT, xst_rot, tag):
            for j in range(4):
                i = 4 * b + j
                xs, xk, ch = xst_rot.next()
                p.dma("sp", ch, lambda e, xs=xs, i=i: e.dma_start(out=xs[:], in_=x_d[seq, i * 128:(i + 1) * 128, :]),
                      writes=[xk])
                p.op("act", lambda e, xs=xs: e.activation(out=p.junk[:, 0:1024], in_=xs[:], func=AF.Square,
                                                          accum_out=p.ss[:, 0:1]),
                     reads=[xk], writes=["junk", "ss"])
                p.op("act", lambda e: e.activation(out=p.rs[:, 1:2], in_=p.ss[:, 0:1], func=AF.Ln,
                                                   bias=cols2[:, 7:8], scale=1.0),
                     reads=["ss", "cols2"], writes=["rs1"])
                p.op("act", lambda e: e.activation(out=p.rs[:, 0:1], in_=p.rs[:, 1:2], func=AF.Exp, scale=-0.5),
                     reads=["rs1"], writes=["rs"])
                p.op("dve", lambda e, xs=xs: e.tensor_scalar(out=p.xb[:], in0=xs[:], scalar1=p.rs[:, 0:1],
                                                             scalar2=32.0, op0=ALU.mult, op1=ALU.mult),
                     reads=[xk, "rs"], writes=["xb"])
                for g in range(2):
                    for q in range(4):
                        kc = 4 * g + q
                        p.op("pe", lambda e, kc=kc, q=q: e.transpose(out=PST[:, q * 128:(q + 1) * 128],
                                                                      in_=p.xb[:, kc * 128:(kc + 1) * 128],
                                                                      identity=ident),
                             reads=["xb", "mats"], writes=["pst"])
                    eng = "act" if g == 0 else "dve"
                    if eng == "act":
                        p.op("act", lambda e, g=g, j=j: e.copy(
                            out=xnT[:, 4 * g:4 * g + 4, j * 128:(j + 1) * 128],
                            in_=PST[:, 0:512].rearrange("p (q t) -> p q t", t=128)),
                            reads=["pst"], writes=["xnT"])
                    else:
                        p.op("dve", lambda e, g=g, j=j: e.tensor_copy(
                            out=xnT[:, 4 * g:4 * g + 4, j * 128:(j + 1) * 128],
                            in_=PST[:, 0:512].rearrange("p (q t) -> p q t", t=128)),
                            reads=["pst"], writes=["xnT"])

        def make_tables(p, b, tabs, ntab=4):
            p.dma("sp", "tab", lambda e: e.dma_start(out=tabs[:, 0:ntab, :], in_=tabs_d[b, :, 0:ntab, :]),
                  writes=[("tab", ti) for ti in range(4)])

        def load_w(p, dst, src, nk, chan, key):
            for kc in range(nk):
                p.dma("pool", chan, lambda e, kc=kc: e.dma_start(
                    out=dst[:, kc, :], in_=src[kc * 128:(kc + 1) * 128, :], max_dma_last_dim=4096),
                    writes=[(key, kc)])
            tot = (("ch", chan), p.chcnt[("ch", chan)])
            for kc in range(nk):
                p.lw[(key, kc)] = tot

        def scale_rows(p, dst, nk, colbase, key, eng="dve"):
            for kc in range(nk):
                p.op(eng, lambda e, kc=kc: e.tensor_scalar(out=dst[:, kc, :], in0=dst[:, kc, :],
                                                           scalar1=colap(colbase + kc), scalar2=None, op0=ALU.mult),
                     reads=[(key, kc), "cols"], writes=[(key, kc)])

        def attention_units(p, b, nheads, qk_fn, exp_fn, post_fn, v_ap_fn, fin_fn, PTs, Lps, Ops):
            for _ in attention_units_gen(p, b, nheads, qk_fn, exp_fn, post_fn, v_ap_fn, fin_fn, PTs, Lps, Ops):
                pass

        def attention_units_gen(p, b, nheads, qk_fn, exp_fn, post_fn, v_ap_fn, fin_fn, PTs, Lps, Ops):
            units = [(h, k) for h in range(nheads) for k in range(4 * b + 4)]
            LOOK = min(2, max(1, len(Lps.items) - 1))
            state = {}

            def issue_qk(u):
                h, k = units[u]
                kk = k - 4 * b
                col0 = 128 * max(kk, 0)
                lp, lk = Lps.next()
                state[u] = (lp, lk, col0)
                qk_fn(h, k, col0, lp, lk)

            def issue_rest(u):
                h, k = units[u]
                lp, lk, col0 = state.pop(u)
                kk = k - 4 * b
                pt, pk = PTs.next()
                N = 512 - col0
                exp_fn(h, k, col0, lp, lk, pt, pk, N)
                if post_fn is not None and kk >= 0:
                    post_fn(h, k, col0, pt, pk)
                if k == 0:
                    state[("O", h)] = Ops.next()
                    O0, ok0 = state[("O", h)]
                    p.op("pe", lambda e, O0=O0: e.matmul(O0[:, 0:260], lhsT=zeros, rhs=fin260, start=True, stop=False),
                         reads=["mats"], writes=[ok0])
                O, ok = state[("O", h)]
                for jj in range(max(kk, 0), 4):
                    c0 = jj * 128 - col0
                    p.op("pe", lambda e, O=O, jj=jj, pt=pt, c0=c0, h=h, k=k: e.matmul(
                        O[:, jj * 65:jj * 65 + 65], lhsT=pt[:, c0:c0 + 128], rhs=v_ap_fn(k, h),
                        start=False, stop=(k == 4 * b + jj)),
                        reads=[pk, ("V", k // 4)], writes=[ok])
                if k == 4 * b + 3:
                    fin_fn(h, O, ok)

            n = len(units)
            for u in range(min(LOOK, n)):
                issue_qk(u)
            for u in range(n):
                if u + LOOK < n:
                    issue_qk(u + LOOK)
                issue_rest(u)
                yield 1.3

        for seq in range(nseq):
            stA = ExitStack()
            if not skipA:
              with stA:
                  p = Prog(nc, stA)
                  A = lambda name, shape, dt: sb(f"A{seq}_{name}", shape, dt, stA)
                  WA = A("WA", [128, 8, 2632], BF16)
                  WKI2 = A("WKI2", [128, 8, 128], BF16)
                  SC = [A(f"scores{i}", [128, 2048], F32) for i in range(2)]
                  xnT = SC[0][:].bitcast(BF16).rearrange("p (k t) -> p k t", t=512)
                  sc1b = SC[1][:].bitcast(BF16)
                  xst_rot = Rot([(SC[1][:, 0:1024], "xs0", "xs0")])
                  p.junk = sc1b[:, 3072:4096]
                  p.ss = A("ss", [128, 1], F32)
                  p.rs = A("rs", [128, 2], F32)
                  p.xb = sc1b[:, 2048:3072]
                  tabs = A("tabs", [128, 4, 512], F32)
                  KaT = A("KaT", [128, 4, 2048], BF16)
                  KiT = A("KiT", [128, 2048], BF16)
                  Va = A("Va", [128, 16, 8 * 65], BF16)
                  QaTb = [A(f"QaT{i}", [128, 4, 512], BF16) for i in range(2)]
                  QiT = A("QiT", [128, 4, 512], BF16)
                  sgAb = [A(f"sgA{i}", [128, 4, 512], BF16) for i in range(2)]
                  wi = A("wi", [128, 4, 8], F32)
                  mbuf = [A(f"mb{i}", [128, 4, 2048], BF16) for i in range(2)]
                  PTt = [A(f"PT{i}", [128, 512], BF16) for i in range(3)]
                  Rt = [A(f"R{i}", [128, 512], BF16) for i in range(4)]
                  Ra = [A(f"Ra{i}", [128, 512], BF16) for i in range(2)]
                  Dg = A("Dg", [128, 4, 128], BF16)
                  sqb = [A(f"sqb{i}", [128, 512], BF16) for i in range(2)]
                  yb = [A(f"yb{i}", [128, 512], BF16) for i in range(3)]
                  bis = A("bis", [128, 16], F32)
                  wab = A("wab", [128, 4, 8], F32)
                  wsg = A("wsg", [128, 4, 8], F32)
                  rinv = A("rinv", [128, 4], F32)
                  fdum = A("fdum", [128, 2], F32)
                  PSTf = PST[:].bitcast(F32)

                  def fence(bi):
                      keys = ["sc0", "sc1", "xnT", "xs0", "xb", "junk", "ta0", "ta1", "tb0", "tb1", "rst0", "rst1"]
                      keys += [("mb", bi, jq) for jq in range(4)]
                      p.op("dve", lambda e: e.memset(fdum[:, 0:1], 0.0), writes=keys)

                  load_w(p, WA, win_d[:, 0:2632], 8, "wa", "WA")
                  scale_rows(p, WA, 8, C_NG, "WA")
                  for kc in range(8):
                      p.op("pool", lambda e, kc=kc: e.tensor_copy(
                          out=WKI2[:, kc, :].rearrange("p (r c) -> p r c", r=2),
                          in_=WA[:, kc, 2560:2624].unsqueeze(1).to_broadcast([128, 2, 64])),
                          reads=[("WA", kc)], writes=[("WKI2", kc)])
                  p.op("pool", lambda e: e.memset(Va[:].rearrange("p i (h d) -> p (i h) d", d=65)[:, :, 64:65], 1.0),
                       writes=[("V", q) for q in range(4)])

                  Lps = Rot([(PS[0], "ps0"), (PS[1], "ps1")])

                  def blockA_p1(b):
                      t0 = 512 * b
                      bi = b % 2
                      QaT, sgA = QaTb[bi], sgAb[bi]
                      f32v = mbuf[bi][:].rearrange("p a c -> p (a c)").bitcast(F32)
                      ta = [f32v[:, 0:512], f32v[:, 512:1024]]
                      tb = [f32v[:, 1024:1536], f32v[:, 1536:2048]]
                      rst = [f32v[:, 2048:2560], f32v[:, 2560:3072]]
                      fence(bi)
                      common_p0(p, stA, seq, b, xnT, xst_rot, "A")
                      make_tables(p, b, tabs)
                      PJ = Rot([(PS[0], "ps0"), (PS[1], "ps1"), (PS[6], "ps6")])
                      YB = Rot([(yb[0], "yb0"), (yb[1], "yb1"), (yb[2], "yb2")])
                      SQ = Rot([(sqb[0], "sqb0"), (sqb[1], "sqb1")])
                      MS = Rot([(PS[2], "ps2"), (PS[4], "ps4")])
                      PR = Rot([(PS[3], "ps3"), (PS[5], "ps5")])
                      RS = Rot([(rst[0], "rst0"), (rst[1], "rst1")])
                      TA = Rot([(ta[0], "ta0"), (ta[1], "ta1")])
                      TB = Rot([(tb[0], "tb0"), (tb[1], "tb1")])

                      def proj_fm(col, wt=WA, wkey="WA", ncol=128):
                          pj, pk = PJ.next()
                          for kc in range(8):
                              p.op("pe", lambda e, kc=kc, pj=pj: e.matmul(
                                  pj[0:ncol, :], lhsT=wt[:, kc, col:col + ncol], rhs=xnT[:, kc, :],
                                  start=(kc == 0), stop=(kc == 7)),
                                  reads=[(wkey, kc), "xnT"], writes=[pk])
                          return pj, pk

                      def rope_finish(yt, yk, perm, tcos, tsin, dst_fn, dkey):
                          pr_, prk = PR.next()
                          ta_, tak = TA.next()
                          tb_, tbk = TB.next()
                          p.op("pe", lambda e, yt=yt, pr_=pr_: e.matmul(pr_[:, :], lhsT=perm, rhs=yt[:], start=True, stop=True),
                               reads=[yk, "mats"], writes=[prk])
                          p.op("dve", lambda e, yt=yt, ta_=ta_: e.tensor_tensor(out=ta_[:], in0=yt[:], in1=tabs[:, tcos, :],
                                                                                op=ALU.mult),
                               reads=[yk, ("tab", tcos)], writes=[tak])
                          p.op("dve", lambda e, pr_=pr_, tb_=tb_: e.tensor_tensor(out=tb_[:], in0=pr_[:, :], in1=tabs[:, tsin, :],
                                                                                  op=ALU.mult),
                               reads=[prk, ("tab", tsin)], writes=[tbk])
                          p.op("pool", lambda e, ta_=ta_, tb_=tb_: e.tensor_tensor(out=dst_fn(), in0=ta_[:], in1=tb_[:], op=ALU.add),
                               reads=[tak, tbk], writes=[dkey])

                      jobs = []
                      for kind in ("q", "k"):
                          for c in range(4):
                              col = (0 if kind == "q" else 512) + c * 128
                              gcol = cols2[:, 0:1] if kind == "q" else colap(C_AK)
                              if kind == "q":
                                  dst_fn, dkey = (lambda c=c: QaT[:, c, :]), ("QaT", bi, c)
                              else:
                                  dst_fn, dkey = (lambda c=c, t0=t0: KaT[:, c, t0:t0 + 512]), ("KaT", b)

                              def s0(st, col=col):
                                  st["pj"], st["pk"] = proj_fm(col)

                              def s1(st):
                                  pj, pk = st["pj"], st["pk"]
                                  sq_, sqk = SQ.next()
                                  ms_, msk = MS.next()
                                  st["ms"], st["msk"] = ms_, msk
                                  p.op("act", lambda e, pj=pj, sq_=sq_: e.activation(out=sq_[:], in_=pj[:, :],
                                                                                     func=AF.Square),
                                       reads=[pk], writes=[sqk])
                                  p.op("pe", lambda e, sq_=sq_, ms_=ms_: e.matmul(ms_[:, :], lhsT=bdA, rhs=sq_[:],
                                                                                  start=True, stop=True),
                                       reads=[sqk, "mats"], writes=[msk])

                              def s2(st, gcol=gcol, dst_fn=dst_fn, dkey=dkey):
                                  pj, pk, ms_, msk = st["pj"], st["pk"], st["ms"], st["msk"]
                                  rs_, rsk = RS.next()
                                  p.op("act", lambda e, ms_=ms_, rs_=rs_: e.activation(
                                      out=rs_[:], in_=ms_[:, :], func=AF.Ln, bias=cols2[:, 6:7], scale=1.0 / 64.0),
                                      reads=[msk, "cols2"], writes=[rsk])
                                  p.op("act", lambda e, rs_=rs_: e.activation(out=rs_[:], in_=rs_[:], func=AF.Exp,
                                                                              scale=-0.5),
                                       reads=[rsk], writes=[rsk])
                                  yt, yk = YB.next()
                                  p.op("dve", lambda e, pj=pj, yt=yt, gcol=gcol, rs_=rs_: e.scalar_tensor_tensor(
                                      out=yt[:], in0=pj[:, :], scalar=gcol, in1=rs_[:], op0=ALU.mult, op1=ALU.mult),
                                      reads=[pk, rsk, "cols", "cols2"], writes=[yk])
                                  rope_finish(yt, yk, permA, 0, 1, dst_fn, dkey)
                              jobs.append([s0, s1, s2])
                      for c in range(5):
                          if c < 4:
                              dst_fn, dkey = (lambda c=c: QiT[:, c, :]), ("QiT", c)
                          else:
                              dst_fn, dkey = (lambda t0=t0: KiT[:, t0:t0 + 512]), ("KiT", b)

                          def s0(st, c=c):
                              if c < 4:
                                  st["pj"], st["pk"] = proj_fm(2048 + c * 128)
                              else:
                                  st["pj"], st["pk"] = proj_fm(0, wt=WKI2, wkey="WKI2")

                          def s1(st):
                              pj, pk = st["pj"], st["pk"]
                              yt, yk = YB.next()
                              st["yt"], st["yk"] = yt, yk
                              p.op("act", lambda e, pj=pj, yt=yt: e.copy(out=yt[:], in_=pj[:, :]),
                                   reads=[pk], writes=[yk])

                          def s2(st, dst_fn=dst_fn, dkey=dkey):
                              rope_finish(st["yt"], st["yk"], permI, 2, 3, dst_fn, dkey)
                          jobs.append([s0, s1, s2])
                      run_pipeline(jobs)
                      for j in range(4):
                          i = 4 * b + j
                          for what in ("v", "g", "w"):
                              pj, pk = PJ.next()
                              c0, n = {"v": (1024, 512), "g": (1536, 512), "w": (2624, 8)}[what]
                              for kc in range(8):
                                  p.op("pe", lambda e, kc=kc, pj=pj, j=j, c0=c0, n=n: e.matmul(
                                      pj[:, 0:n], lhsT=xnT[:, kc, j * 128:(j + 1) * 128], rhs=WA[:, kc, c0:c0 + n],
                                      start=(kc == 0), stop=(kc == 7)),
                                      reads=[("WA", kc), "xnT"], writes=[pk])
                              if what == "v":
                                  p.op("dve", lambda e, pj=pj, i=i: e.tensor_copy(
                                      out=Va[:, i, :].rearrange("p (h d) -> p h d", d=65)[:, :, 0:64],
                                      in_=pj[:, :].rearrange("p (h d) -> p h d", d=64)),
                                      reads=[pk], writes=[("V", b)])
                              elif what == "g":
                                  p.op("act", lambda e, pj=pj, j=j: e.activation(out=sgA[:, j, :], in_=pj[:, :],
                                                                                 func=AF.Silu),
                                       reads=[pk], writes=[("sg", bi, j)])
                              else:
                                  p.op("dve", lambda e, pj=pj, j=j: e.tensor_copy(out=wi[:, j, :], in_=pj[:, 0:8]),
                                       reads=[pk], writes=[("wi", j)])
                                  p.op("dve", lambda e, j=j: e.tensor_scalar(out=wab[:, j, :], in0=wi[:, j, :], scalar1=-1.0,
                                                                             scalar2=None, op0=ALU.mult),
                                       reads=[("wi", j)], writes=[("wab", j)])
                                  p.op("dve", lambda e, j=j: e.tensor_tensor(out=wab[:, j, :], in0=wab[:, j, :],
                                                                             in1=wi[:, j, :], op=ALU.max),
                                       reads=[("wi", j), ("wab", j)], writes=[("wab", j)])
                                  p.op("dve", lambda e, j=j: e.tensor_scalar(out=wsg[:, j, :], in0=wi[:, j, :], scalar1=0.0,
                                                                             scalar2=2.0, op0=ALU.is_ge, op1=ALU.mult),
                                       reads=[("wi", j)], writes=[("wsg", j)])
                                  p.op("dve", lambda e, j=j: e.tensor_scalar(out=wsg[:, j, :], in0=wsg[:, j, :], scalar1=-1.0,
                                                                             scalar2=None, op0=ALU.add),
                                       reads=[("wsg", j)], writes=[("wsg", j)])

                      fence(bi)

                  def blockA_idx(b):
                      bi = b % 2
                      mb = mbuf[bi]
                      XP = Rot([(PS[i], f"ps{i}") for i in range(3, 7)])
                      ACC = Rot([(PSTf, "pst")])
                      RR = Rot([(Rt[i], f"R{i}") for i in range(4)])
                      RA = Rot([(Ra[i], f"Ra{i}") for i in range(2)])
                      for pr in range(2):
                          tiles = []
                          for q in range(2):
                              j = 2 * pr + q
                              i = 4 * b + j
                              L2 = 128 * (i + 1)
                              L1 = L2 - 64
                              sc, sck = SC[q], f"sc{q}"
                              tiles.append((j, i, L1, L2, sc, sck))
                              for dq, hh in enumerate((2, 3, 6, 7)):
                                  p.op("dve", lambda e, dq=dq, hh=hh, j=j: e.tensor_scalar(
                                      out=Dg[:, dq, :], in0=ident, scalar1=wsg[:, j, hh:hh + 1], scalar2=None,
                                      op0=ALU.mult),
                                      reads=["mats", ("wsg", j)], writes=[("Dg", dq)])
                              for sbk in range(b + 1):
                                  wd = 512 if sbk < b else 128 * (j + 1)
                                  acc, acck = ACC.next()
                                  pend = []

                                  def issue_x(h, j=j, sbk=sbk, wd=wd):
                                      c, base = h // 2, (h % 2) * 64
                                      xp, xk = XP.next()
                                      p.op("pe", lambda e, xp=xp, c=c, base=base: e.matmul(
                                          xp[:, 0:wd], lhsT=QiT[base:base + 64, c, j * 128:(j + 1) * 128],
                                          rhs=KiT[base:base + 64, sbk * 512:sbk * 512 + wd], start=True, stop=True),
                                          reads=[("QiT", c), ("KiT", sbk)], writes=[xk])
                                      return xp, xk
                                  for h0 in range(4):
                                      pend.append(issue_x(h0))
                                  nacc = 0
                                  for m in range(4):
                                      xpe, xke = pend.pop(0)
                                      xpo, xko = pend.pop(0)
                                      he, ho = 2 * m, 2 * m + 1
                                      if m % 2 == 0:
                                          r, rk = RR.next()
                                          r2, rk2 = RR.next()
                                          for (xp_, xk_, r_, rk_, h_) in ((xpe, xke, r, rk, he), (xpo, xko, r2, rk2, ho)):
                                              p.op("dve", lambda e, xp=xp_, r=r_, h=h_, j=j, wd=wd: e.tensor_scalar(
                                                  out=r[:, 0:wd], in0=xp[:, 0:wd], scalar1=0.0, scalar2=wi[:, j, h:h + 1],
                                                  op0=ALU.max, op1=ALU.mult),
                                                  reads=[xk_, ("wi", j)], writes=[rk_])
                                          if 2 * m + 4 < 8:
                                              pend.append(issue_x(2 * m + 4))
                                              pend.append(issue_x(2 * m + 5))
                                          p.op("pool", lambda e, r=r, r2=r2, wd=wd: e.tensor_tensor(
                                              out=r[:, 0:wd], in0=r[:, 0:wd], in1=r2[:, 0:wd], op=ALU.add),
                                              reads=[rk, rk2], writes=[rk])
                                          p.op("pe", lambda e, acc=acc, r=r, nacc=nacc, wd=wd: e.matmul(
                                              acc[:, 0:wd], lhsT=ident, rhs=r[:, 0:wd], start=(nacc == 0), stop=False),
                                              reads=[rk, "mats"], writes=[acck])
                                          nacc += 1
                                      else:
                                          r, rk = RA.next()
                                          r2, rk2 = RA.next()
                                          for (xp_, xk_, r_, rk_, h_) in ((xpe, xke, r, rk, he), (xpo, xko, r2, rk2, ho)):
                                              p.op("act", lambda e, xp=xp_, r=r_, h=h_, j=j, wd=wd: e.activation(
                                                  out=r[:, 0:wd], in_=xp[:, 0:wd], func=AF.Relu, scale=wab[:, j, h:h + 1]),
                                                  reads=[xk_, ("wab", j)], writes=[rk_])
                                          if 2 * m + 4 < 8:
                                              pend.append(issue_x(2 * m + 4))
                                              pend.append(issue_x(2 * m + 5))
                                          for (r_, rk_, h_) in ((r, rk, he), (r2, rk2, ho)):
                                              dq = (h_ // 4) * 2 + (h_ % 2)
                                              p.op("pe", lambda e, acc=acc, r=r_, nacc=nacc, wd=wd, dq=dq, m=m, h=h_: e.matmul(
                                                  acc[:, 0:wd], lhsT=Dg[:, dq, :], rhs=r[:, 0:wd], start=(nacc == 0),
                                                  stop=(h == 7)),
                                                  reads=[rk_, ("Dg", dq)], writes=[acck])
                                              nacc += 1
                                  p.op("act", lambda e, acc=acc, sbk=sbk, wd=wd, sc=sc: e.copy(
                                      out=sc[:, sbk * 512:sbk * 512 + wd], in_=acc[:, 0:wd]),
                                      reads=[acck], writes=[sck])
                                  yield 7.0 * wd / 512.0
                              p.op("pool", lambda e, L1=L1, L2=L2, sc=sc: e.memset(sc[0:64, L1:L2], -1e30),
                                   reads=[], writes=[sck])
                              if "scores" in dbg_d and seq == 0:
                                  p.dma("sp", "dbg", lambda e, i=i, L2=L2, sc=sc: e.dma_start(
                                      out=dbg_d["scores"][i, :, 0:L2], in_=sc[:, 0:L2]), reads=[sck])
                          LO2, HI2, RNG2, MID2, CNT2, GE2, T2 = [bis[:, 2 * q:2 * q + 2] for q in range(7)]
                          SBc = bis[:, 14:15]
                          if tiles[0][1] < 2:
                              p.op("dve", lambda e: e.memset(LO2, -5e29), writes=["lo0", "lo1"])
                          else:
                              for q, (j, i, L1, L2, sc, sck) in enumerate(tiles):
                                  p.op("dve", lambda e, L1=L1, sc=sc, q=q: e.tensor_reduce(
                                      out=LO2[:, q:q + 1], in_=sc[:, 0:L1], axis=AX.X, op=ALU.min),
                                      reads=[sck], writes=[f"lo{q}"])
                                  p.op("dve", lambda e, L2=L2, sc=sc, q=q: e.tensor_reduce(
                                      out=HI2[:, q:q + 1], in_=sc[:, 0:L2], axis=AX.X, op=ALU.max),
                                      reads=[sck], writes=["hi"])
                              p.op("dve", lambda e: e.tensor_tensor(out=RNG2, in0=HI2, in1=LO2, op=ALU.subtract),
                                   reads=["hi", "lo0", "lo1"], writes=["rng"])
                              (jA, iA, L1A, L2A, scA, sckA), (jB, iB, L1B, L2B, scB, sckB) = tiles
                              for it in range(1, NIT + 1):
                                  cst = float(2.0 ** (-it))
                                  p.op("dve", lambda e, cst=cst: e.scalar_tensor_tensor(
                                      out=MID2, in0=RNG2, scalar=cst, in1=LO2, op0=ALU.mult, op1=ALU.add),
                                      reads=["rng", "lo0", "lo1"], writes=["mid"])
                                  p.op("act", lambda e, jB=jB, L2B=L2B, scB=scB: e.activation(
                                      out=mb[:, jB, 0:L2B], in_=scB[:, 0:L2B], func=AF.Sign, scale=-1.0,
                                      bias=MID2[:, 1:2], accum_out=SBc),
                                      reads=[sckB, "mid"], writes=[("mb", bi, jB), "sb"])
                                  p.op("dve", lambda e, jA=jA, L2A=L2A, scA=scA: e.tensor_scalar(
                                      out=mb[:, jA, 0:L2A], in0=scA[:, 0:L2A], scalar1=MID2[:, 0:1], scalar2=0.0,
                                      op0=ALU.is_ge, op1=ALU.add, accum_out=CNT2[:, 0:1]),
                                      reads=[sckA, "mid"], writes=[("mb", bi, jA), "cnt0"])
                                  p.op("dve", lambda e, cst=cst: e.tensor_scalar(
                                      out=GE2[:, 0:1], in0=CNT2[:, 0:1], scalar1=float(TOPK) - 0.75, scalar2=cst,
                                      op0=ALU.is_ge, op1=ALU.mult),
                                      reads=["cnt0"], writes=["ge0"])
                                  p.op("dve", lambda e, cst=cst, L2B=L2B: e.tensor_scalar(
                                      out=GE2[:, 1:2], in0=SBc, scalar1=float(L2B - 2 * TOPK) + 1.5, scalar2=cst,
                                      op0=ALU.is_le, op1=ALU.mult),
                                      reads=["sb"], writes=["ge1"])
                                  p.op("dve", lambda e: e.scalar_tensor_tensor(
                                      out=LO2[:, 0:1], in0=RNG2[:, 0:1], scalar=GE2[:, 0:1], in1=LO2[:, 0:1],
                                      op0=ALU.mult, op1=ALU.add),
                                      reads=["rng", "ge0", "lo0"], writes=["lo0"])
                                  p.op("dve", lambda e: e.scalar_tensor_tensor(
                                      out=LO2[:, 1:2], in0=RNG2[:, 1:2], scalar=GE2[:, 1:2], in1=LO2[:, 1:2],
                                      op0=ALU.mult, op1=ALU.add),
                                      reads=["rng", "ge1", "lo1"], writes=["lo1"])
                                  yield 1.2 + L2B / 800.0
                          for q, (j, i, L1, L2, sc, sck) in enumerate(tiles):
                              p.op("dve", lambda e, j=j, L2=L2, sc=sc, q=q: e.tensor_scalar(
                                  out=mb[:, j, 0:L2], in0=sc[:, 0:L2], scalar1=LO2[:, q:q + 1], scalar2=NEG,
                                  op0=ALU.is_lt, op1=ALU.mult),
                                  reads=[sck, f"lo{q}"], writes=[("mb", bi, j)])
                              if "thr" in dbg_d and seq == 0:
                                  p.dma("sp", "dbg", lambda e, i=i, q=q: e.dma_start(
                                      out=dbg_d["thr"][i, :, 0:1], in_=LO2[:, q:q + 1]), reads=[f"lo{q}"])

                          yield 0.5

                  def blockA_att(b):
                      bi = b % 2
                      mb, QaT, sgA = mbuf[bi], QaTb[bi], sgAb[bi]
                      PTs = Rot([(PTt[i], f"PT{i}") for i in range(3)])
                      Ops = Rot([(PS[2], "ps2")])

                      def qk_A(h, k, col0, lp, lk):
                          c, base = h // 2, (h % 2) * 64
                          kk = k - 4 * b
                          N = 512 - col0
                          for jj in range(max(kk, 0), 4):
                              c0 = jj * 128 - col0
                              p.op("pe", lambda e, lp=lp, jj=jj, k=k, c0=c0, kk=kk: e.matmul(
                                  lp[:, c0:c0 + 128], lhsT=mb[:, jj, k * 128:(k + 1) * 128], rhs=ident,
                                  start=(jj == max(kk, 0)), stop=False),
                                  reads=[("mb", bi, jj), "mats"], writes=[lk])
                          p.op("pe", lambda e, lp=lp, k=k, c=c, base=base, col0=col0, N=N: e.matmul(
                              lp[:, 0:N], lhsT=KaT[base:base + 64, c, k * 128:(k + 1) * 128],
                              rhs=QaT[base:base + 64, c, col0:512], start=False, stop=True),
                              reads=[("KaT", k // 4), ("QaT", bi, c)], writes=[lk])

                      def exp_A(h, k, col0, lp, lk, pt, pk, N):
                          p.op("act", lambda e, lp=lp, pt=pt, N=N: e.activation(out=pt[:, 0:N], in_=lp[:, 0:N],
                                                                                 func=AF.Exp),
                               reads=[lk], writes=[pk])

                      def fin_A(h, O, ok):
                          Ov = O[:, 0:260].rearrange("p (j d) -> p j d", d=65)
                          p.op("dve", lambda e, Ov=Ov: e.reciprocal(out=rinv[:, 0:4], in_=Ov[:, :, 64]),
                               reads=[ok], writes=["rinv"])
                          for jj in range(4):
                              p.op("dve", lambda e, Ov=Ov, jj=jj, h=h, b=b: e.scalar_tensor_tensor(
                                  out=mixed[:, 4 * b + jj, h * 64:(h + 1) * 64], in0=Ov[:, jj, 0:64],
                                  scalar=rinv[:, jj:jj + 1], in1=sgA[:, jj, h * 64:(h + 1) * 64],
                                  op0=ALU.mult, op1=ALU.mult),
                                  reads=[ok, "rinv", ("sg", bi, jj)], writes=[("mixed", 4 * b + jj)])

                      yield from attention_units_gen(p, b, 8, qk_A, exp_A, None,
                                                     lambda k, h: Va[:, k, h * 65:(h + 1) * 65], fin_A, PTs, Lps, Ops)

                  def merge(g1, g2):
                      t1 = t2 = 0.0
                      a1 = a2 = True
                      while a1 or a2:
                          if a1 and (t1 <= t2 or not a2):
                              try:
                                  t1 += next(g1)
                              except StopIteration:
                                  a1 = False
                          elif a2:
                              try:
                                  t2 += next(g2)
                              except StopIteration:
                                  a2 = False

                  def drain(g):
                      for _ in g:
                          pass

                  blockA_p1(0)
                  if stage >= 2:
                      drain(blockA_idx(0))
                  for b in range(4):
                      if b + 1 < 4:
                          blockA_p1(b + 1)
                      if stage >= 3:
                          if b + 1 < 4:
                              merge(blockA_att(b), blockA_idx(b + 1))
                          else:
                              drain(blockA_att(b))
                      elif stage >= 2 and b + 1 < 4:
                          drain(blockA_idx(b + 1))

                  if seq == 0:
                      QaT, sgA = QaTb[1], sgAb[1]
                      dump = {"KaT": (KaT, [128, 4 * 2048]), "KiT": (KiT, [128, 2048]), "QaT": (QaT, [128, 4 * 512]),
                              "QiT": (QiT, [128, 4 * 512]), "sgA": (sgA, [128, 4 * 512]), "Va": (Va, [128, 16 * 520]),
                              "mixedA": (mixed, [128, 16 * 1024])}
                      for name, (tl, shp) in dump.items():
                          if name in dbg_d:
                              flat = tl[:] if len(tl.shape) == 2 else tl[:].rearrange(
                                  "p a b -> p (a b)") if len(tl.shape) == 3 else tl[:]
                              nfree = shp[1]
                              for c0 in range(0, nfree, 2048):
                                  c1 = min(nfree, c0 + 2048)
                                  p.dma("pool", "dbgp", lambda e, flat=flat, c0=c0, c1=c1, name=name: e.dma_start(
                                      out=dbg_d[name][:, c0:c1], in_=flat[:, c0:c1]),
                                      reads=list(p.lw.keys()))
                  p.finish()
                  p.emit()
            if stage <= 3:
                continue
            stB = ExitStack()
            with stB:
                p = Prog(nc, stB)
                Bt = lambda name, shape, dt: sb(f"B{seq}_{name}", shape, dt, stB)
                WB = Bt("WB", [128, 8, 1184], BF16)
                WUQ = Bt("WUQ", [128, 3, 768], BF16)
                WUKV = Bt("WUKV", [128, 2, 1024], BF16)
                xnT = Bt("xnT", [128, 8, 512], BF16)
                xst = [Bt(f"xs{i}", [128, 1024], F32) for i in range(2)]
                xst_rot = Rot([(xst[i], f"xs{i}", f"xs{i}") for i in range(2)])
                p.junk = Bt("junk", [128, 1024], BF16)
                p.ss = Bt("ss", [128, 1], F32)
                p.rs = Bt("rs", [128, 2], F32)
                p.xb = Bt("xb", [128, 1024], BF16)
                tabs = Bt("tabs", [128, 4, 512], F32)
                KbT = Bt("KbT", [128, 8, 2048], BF16)
                Vb = Bt("Vb", [128, 16, 8 * 65], BF16)
                rk = Bt("rk", [128, 16, 8], F32)
                QbT = Bt("QbT", [128, 8, 512], BF16)
                sgB = Bt("sgB", [128, 4, 512], BF16)
                cqs = Bt("cqs", [128, 3, 512], BF16)
                cqn = Bt("cqn", [128, 3, 512], BF16)
                ckvs = Bt("ckvs", [128, 2, 512], BF16)
                ckvn = Bt("ckvn", [128, 2, 512], BF16)
                sqb = [Bt(f"sqb{i}", [128, 512], BF16) for i in range(2)]
                rst = [Bt(f"rst{i}", [128, 512], F32) for i in range(2)]
                ta = [Bt(f"ta{i}", [128, 512], F32) for i in range(2)]
                tb = [Bt(f"tb{i}", [128, 512], F32) for i in range(2)]
                yb = [Bt(f"yb{i}", [128, 512], BF16) for i in range(3)]
                krs = Bt("krs", [128, 512], BF16)
                kro = Bt("kro", [128, 512], BF16)
                PTt = [Bt(f"PT{i}", [128, 512], BF16) for i in range(3)]
                ssr = Bt("ssr", [128, 4], F32)
                ssn = Bt("ssn", [128, 8], F32)
                t8 = Bt("t8", [128, 16], F32)
                sq32 = Bt("sq32", [128, 256], F32)
                rinv = Bt("rinv", [128, 4], F32)

                load_w(p, WB, win_d[:, 2632:3816], 8, "wb", "WB")
                scale_rows(p, WB, 8, C_NG, "WB")
                load_w(p, WUQ, wuq_d[:, :], 3, "wuq", "WUQ")
                scale_rows(p, WUQ, 3, C_CQ, "WUQ")
                load_w(p, WUKV, wukv_d[:, :], 2, "wukv", "WUKV")
                scale_rows(p, WUKV, 2, C_CKV, "WUKV")
                p.op("pool", lambda e: e.memset(Vb[:].rearrange("p i (h d) -> p (i h) d", d=65)[:, :, 64:65], 1.0),
                     writes=[("V", q) for q in range(4)])
                p.op("pool", lambda e: e.memset(krs[:], 0.0), writes=["krs"])

                Lps = Rot([(PS[i], f"ps{i}") for i in range(4)])
                for b in range(4):
                    t0 = 512 * b
                    if bstage <= -1:
                        continue
                    common_p0(p, stB, seq, b, xnT, xst_rot, "B")
                    if bstage <= 0:
                        continue
                    if 'notab' not in VAR:
                        p.dma("sp", "tab", lambda e, b=b: e.dma_start(out=tabs[:, 2:4, :], in_=tabs_d[b, :, 2:4, :]),
                              writes=[("tab", 2), ("tab", 3)])
                    PJ = Rot([(PS[0], "ps0"), (PS[1], "ps1"), (PS[6], "ps6")])
                    YB = Rot([(yb[0], "yb0"), (yb[1], "yb1"), (yb[2], "yb2")])
                    SQ = Rot([(sqb[0], "sqb0"), (sqb[1], "sqb1")])
                    MS = Rot([(PS[2], "ps2"), (PS[4], "ps4")])
                    PR = Rot([(PS[3], "ps3"), (PS[5], "ps5")])
                    RS = Rot([(rst[0], "rst0"), (rst[1], "rst1")])
                    TA = Rot([(ta[0], "ta0"), (ta[1], "ta1")])
                    TB = Rot([(tb[0], "tb0"), (tb[1], "tb1")])

                    def proj_fm(col, ncol=128):
                        pj, pk = PJ.next()
                        for kc in range(8):
                            p.op("pe", lambda e, kc=kc, pj=pj: e.matmul(
                                pj[0:ncol, :], lhsT=WB[:, kc, col:col + ncol], rhs=xnT[:, kc, :],
                                start=(kc == 0), stop=(kc == 7)),
                                reads=[("WB", kc), "xnT"], writes=[pk])
                        return pj, pk

                    jobs = []
                    for (nch, cbase, raw, nrm, rawk, nrmk, dim) in ((3, 0, cqs, cqn, "cqs", "cqn", 384.0),
                                                                    (2, 384, ckvs, ckvn, "ckvs", "ckvn", 256.0)):
                        grp = {}
                        for c in range(nch):
                            def s0(st, c=c, cbase=cbase):
                                st["pj"], st["pk"] = proj_fm(cbase + c * 128)

                            def s1(st, c=c, nch=nch, grp=grp, raw=raw, rawk=rawk):
                                pj, pk = st["pj"], st["pk"]
                                if c == 0:
                                    grp["ms"], grp["msk"] = MS.next()
                                ms_, msk = grp["ms"], grp["msk"]
                                sq_, sqk = SQ.next()
                                p.op("act", lambda e, pj=pj, sq_=sq_: e.activation(out=sq_[:], in_=pj[:, :],
                                                                                   func=AF.Square),
                                     reads=[pk], writes=[sqk])
                                p.op("pe", lambda e, c=c, nch=nch, ms_=ms_, sq_=sq_: e.matmul(
                                    ms_[:, :], lhsT=ones, rhs=sq_[:], start=(c == 0), stop=(c == nch - 1)),
                                    reads=[sqk, "mats"], writes=[msk])
                                p.op("dve", lambda e, pj=pj, raw=raw, c=c: e.tensor_copy(out=raw[:, c, :], in_=pj[:, :]),
                                     reads=[pk, sqk], writes=[(rawk, c)])

                            def s2(st, nch=nch, grp=grp, raw=raw, nrm=nrm, rawk=rawk, nrmk=nrmk, dim=dim):
                                ms_, msk = grp["ms"], grp["msk"]
                                rs_, rsk = RS.next()
                                p.op("act", lambda e, dim=dim, ms_=ms_, rs_=rs_: e.activation(
                                    out=rs_[:], in_=ms_[:, :], func=AF.Ln, bias=cols2[:, 6:7], scale=1.0 / dim),
                                    reads=[msk, "cols2"], writes=[rsk])
                                p.op("act", lambda e, rs_=rs_: e.activation(out=rs_[:], in_=rs_[:], func=AF.Exp,
                                                                            scale=-0.5),
                                     reads=[rsk], writes=[rsk])
                                for cc in range(nch):
                                    p.op("pool", lambda e, raw=raw, nrm=nrm, cc=cc, rs_=rs_: e.tensor_tensor(
                                        out=nrm[:, cc, :], in0=raw[:, cc, :], in1=rs_[:], op=ALU.mult),
                                        reads=[(rawk, cc), rsk], writes=[nrmk])
                            jobs.append([s0, s1, s2] if c == nch - 1 else [s0, s1])
                    run_pipeline(jobs)
                    if bstage <= 1:
                        continue
                    pj, pk = proj_fm(576, ncol=96)
                    p.op("dve", lambda e, pj=pj: e.tensor_scalar(out=krs[64:96, :], in0=pj[64:96, :],
                                                                 scalar1=colap(C_BK, 64, 96), scalar2=None,
                                                                 op0=ALU.mult),
                         reads=[pk, "cols"], writes=["krs"])
                    pr_, prk = PR.next()
                    ta_, tak = TA.next()
                    tb_, tbk = TB.next()
                    p.op("pe", lambda e, pr_=pr_: e.matmul(pr_[0:96, :], lhsT=permI[0:96, 0:96], rhs=krs[0:96, :],
                                                           start=True, stop=True),
                         reads=["krs", "mats"], writes=[prk])
                    p.op("pool", lambda e, ta_=ta_: e.tensor_tensor(out=ta_[64:96, :], in0=krs[64:96, :],
                                                                    in1=tabs[64:96, 2, :], op=ALU.mult),
                         reads=["krs", ("tab", 2)], writes=[tak])
                    p.op("dve", lambda e, pr_=pr_, tb_=tb_: e.tensor_tensor(out=tb_[64:96, :], in0=pr_[64:96, :],
                                                                            in1=tabs[64:96, 3, :], op=ALU.mult),
                         reads=[prk, ("tab", 3)], writes=[tbk])
                    p.op("pool", lambda e, ta_=ta_, tb_=tb_: e.tensor_tensor(out=kro[64:96, :], in0=ta_[64:96, :],
                                                                             in1=tb_[64:96, :], op=ALU.add),
                         reads=[tak, tbk], writes=["kro"])
                    for hh in range(8):
                        p.op("dve", lambda e, t0=t0, hh=hh: e.tensor_copy(out=KbT[64:96, hh, t0:t0 + 512],
                                                                          in_=kro[64:96, :]),
                             reads=["kro"], writes=[("KbT", b)])
                    if bstage <= 2:
                        continue
                    for j in range(4):
                        for what in ("r", "g"):
                            pj, pk = PJ.next()
                            c0, n = {"r": (640, 32), "g": (672, 512)}[what]
                            for kc in range(8):
                                p.op("pe", lambda e, kc=kc, pj=pj, j=j, c0=c0, n=n: e.matmul(
                                    pj[:, 0:n], lhsT=xnT[:, kc, j * 128:(j + 1) * 128], rhs=WB[:, kc, c0:c0 + n],
                                    start=(kc == 0), stop=(kc == 7)),
                                    reads=[("WB", kc), "xnT"], writes=[pk])
                            if what == "r":
                                p.op("act", lambda e, pj=pj, j=j: e.activation(
                                    out=sq32[:, 0:32], in_=pj[:, 0:32], func=AF.Square, accum_out=ssr[:, j:j + 1]),
                                    reads=[pk], writes=["sq32", ("ssr", j)])
                            else:
                                p.op("act", lambda e, pj=pj, j=j: e.activation(out=sgB[:, j, :], in_=pj[:, :],
                                                                               func=AF.Silu),
                                     reads=[pk], writes=[("sg", j)])
                    if bstage <= 3:
                        continue
                    jobs = []
                    for h in range(8):
                        def s0(st, h=h):
                            pj, pk = PJ.next()
                            st["pj"], st["pk"] = pj, pk
                            for c in range(3):
                                p.op("pe", lambda e, c=c, pj=pj, h=h: e.matmul(
                                    pj[0:96, :], lhsT=WUQ[:, c, h * 96:(h + 1) * 96], rhs=cqn[:, c, :],
                                    start=(c == 0), stop=(c == 2)),
                                    reads=[("WUQ", c), "cqn"], writes=[pk])

                        def s1(st):
                            pj, pk = st["pj"], st["pk"]
                            sq_, sqk = SQ.next()
                            ms_, msk = MS.next()
                            st["ms"], st["msk"] = ms_, msk
                            p.op("act", lambda e, pj=pj, sq_=sq_: e.activation(out=sq_[0:96, :], in_=pj[0:96, :],
                                                                               func=AF.Square),
                                 reads=[pk], writes=[sqk])
                            p.op("pe", lambda e, sq_=sq_, ms_=ms_: e.matmul(ms_[0:96, :], lhsT=ones[0:96, 0:96],
                                                                            rhs=sq_[0:96, :], start=True, stop=True),
                                 reads=[sqk, "mats"], writes=[msk])

                        def s2(st, h=h):
                            pj, pk, ms_, msk = st["pj"], st["pk"], st["ms"], st["msk"]
                            rs_, rsk = RS.next()
                            pr_, prk = PR.next()
                            ta_, tak = TA.next()
                            tb_, tbk = TB.next()
                            p.op("act", lambda e, ms_=ms_, rs_=rs_: e.activation(
                                out=rs_[0:96, :], in_=ms_[0:96, :], func=AF.Ln, bias=cols2[0:96, 6:7],
                                scale=1.0 / 96.0),
                                reads=[msk, "cols2"], writes=[rsk])
                            p.op("act", lambda e, rs_=rs_: e.activation(out=rs_[0:96, :], in_=rs_[0:96, :],
                                                                        func=AF.Exp, scale=-0.5),
                                 reads=[rsk], writes=[rsk])
                            yt, yk = YB.next()
                            p.op("dve", lambda e, pj=pj, yt=yt, rs_=rs_: e.scalar_tensor_tensor(
                                out=yt[0:96, :], in0=pj[0:96, :], scalar=cols2[0:96, 1:2], in1=rs_[0:96, :],
                                op0=ALU.mult, op1=ALU.mult),
                                reads=[pk, rsk, "cols2"], writes=[yk])
                            p.op("dve", lambda e, yt=yt, h=h: e.tensor_copy(out=QbT[0:64, h, :], in_=yt[0:64, :]),
                                 reads=[yk], writes=[("QbT", h)])
                            p.op("pe", lambda e, yt=yt, pr_=pr_: e.matmul(pr_[0:96, :], lhsT=permI[0:96, 0:96],
                                                                          rhs=yt[0:96, :], start=True, stop=True),
                                 reads=[yk, "mats"], writes=[prk])
                            p.op("pool", lambda e, yt=yt, ta_=ta_: e.tensor_tensor(
                                out=ta_[64:96, :], in0=yt[64:96, :], in1=tabs[64:96, 2, :], op=ALU.mult),
                                reads=[yk, ("tab", 2)], writes=[tak])
                            p.op("dve", lambda e, pr_=pr_, tb_=tb_: e.tensor_tensor(
                                out=tb_[64:96, :], in0=pr_[64:96, :], in1=tabs[64:96, 3, :], op=ALU.mult),
                                reads=[prk, ("tab", 3)], writes=[tbk])
                            p.op("pool", lambda e, h=h, ta_=ta_, tb_=tb_: e.tensor_tensor(
                                out=QbT[64:96, h, :], in0=ta_[64:96, :], in1=tb_[64:96, :], op=ALU.add),
                                reads=[tak, tbk], writes=[("QbT", h)])
                        jobs.append([s0, s1, s2])
                    run_pipeline(jobs)
                    if bstage <= 4:
                        continue
                    for j in range(4):
                        i = 4 * b + j
                        for hf in range(2):
                            pj, pk = PJ.next()
                            for c in range(2):
                                p.op("pe", lambda e, c=c, pj=pj, j=j, hf=hf: e.matmul(
                                    pj[:, :], lhsT=ckvn[:, c, j * 128:(j + 1) * 128],
                                    rhs=WUKV[:, c, hf * 512:(hf + 1) * 512], start=(c == 0), stop=(c == 1)),
                                    reads=[("WUKV", c), "ckvn"], writes=[pk])
                            pjv = pj[:, :].rearrange("p (h d) -> p h d", d=128)
                            p.op("act", lambda e, pjv=pjv, i=i, hf=hf: e.copy(
                                out=Vb[:, i, :].rearrange("p (h d) -> p h d", d=65)[:, hf * 4:hf * 4 + 4, 0:64],
                                in_=pjv[:, :, 64:128]),
                                reads=[pk], writes=[("V", b)])
                            p.op("act", lambda e, pjv=pjv: e.activation(
                                out=sq32[:].rearrange("p (h d) -> p h d", d=64), in_=pjv[:, :, 0:64], func=AF.Square),
                                reads=[pk], writes=["sq32"])
                            p.op("dve", lambda e, hf=hf: e.tensor_reduce(
                                out=ssn[:, hf * 4:hf * 4 + 4], in_=sq32[:].rearrange("p (h d) -> p h d", d=64),
                                axis=AX.X, op=ALU.add),
                                reads=["sq32"], writes=["ssn"])
                        p.op("dve", lambda e, j=j: e.tensor_scalar(out=t8[:, 0:8], in0=ssn[:, 0:8],
                                                                   scalar1=ssr[:, j:j + 1], scalar2=1.0 / 96.0,
                                                                   op0=ALU.add, op1=ALU.mult),
                             reads=["ssn", ("ssr", j)], writes=["t8"])
                        p.op("act", lambda e: e.activation(out=t8[:, 8:16], in_=t8[:, 0:8], func=AF.Ln,
                                                           bias=cols2[:, 6:7], scale=1.0),
                             reads=["t8", "cols2"], writes=["t8b"])
                        p.op("act", lambda e, i=i: e.activation(out=rk[:, i, :], in_=t8[:, 8:16], func=AF.Exp,
                                                                scale=-0.5),
                             reads=["t8b"], writes=[("rk", b)])
                    if bstage <= 5:
                        continue
                    for h in range(8):
                        pj, pk = PJ.next()
                        for c in range(2):
                            p.op("pe", lambda e, c=c, pj=pj, h=h: e.matmul(
                                pj[0:64, :], lhsT=WUKV[:, c, h * 128:h * 128 + 64], rhs=ckvn[:, c, :],
                                start=(c == 0), stop=(c == 1)),
                                reads=[("WUKV", c), "ckvn"], writes=[pk])
                        p.op("dve", lambda e, pj=pj, h=h, t0=t0: e.tensor_scalar(
                            out=KbT[0:64, h, t0:t0 + 512], in0=pj[0:64, :], scalar1=colap(C_BK, 0, 64),
                            scalar2=None, op0=ALU.mult),
                            reads=[pk, "cols"], writes=[("KbT", b)])

                    if bstage <= 6:
                        continue
                    PTs = Rot([(PTt[i], f"PT{i}") for i in range(3)])
                    Ops = Rot([(PS[4], "ps4"), (PS[5], "ps5"), (PS[6], "ps6")])

                    def qk_B(h, k, col0, lp, lk):
                        N = 512 - col0
                        p.op("pe", lambda e, lp=lp, h=h, k=k, col0=col0, N=N: e.matmul(
                            lp[:, 0:N], lhsT=KbT[0:96, h, k * 128:(k + 1) * 128], rhs=QbT[0:96, h, col0:512],
                            start=True, stop=True),
                            reads=[("KbT", k // 4), ("QbT", h)], writes=[lk])

                    def exp_B(h, k, col0, lp, lk, pt, pk, N):
                        p.op("act", lambda e, lp=lp, pt=pt, N=N, k=k, h=h: e.activation(
                            out=pt[:, 0:N], in_=lp[:, 0:N], func=AF.Exp, scale=rk[:, k, h:h + 1]),
                            reads=[lk, ("rk", k // 4)], writes=[pk])

                    def post_B(h, k, col0, pt, pk):
                        p.op("pool", lambda e, pt=pt: e.memset(pt[64:128, 0:64], 0.0), writes=[pk])

                    def fin_B(h, O, ok, b=b):
                        Ov = O[:, 0:260].rearrange("p (j d) -> p j d", d=65)
                        p.op("dve", lambda e, Ov=Ov: e.reciprocal(out=rinv[:, 0:4], in_=Ov[:, :, 64]),
                             reads=[ok], writes=["rinv"])
                        for jj in range(4):
                            p.op("dve", lambda e, Ov=Ov, jj=jj, h=h, b=b: e.scalar_tensor_tensor(
                                out=mixed[:, 4 * b + jj, 512 + h * 64:512 + (h + 1) * 64], in0=Ov[:, jj, 0:64],
                                scalar=rinv[:, jj:jj + 1], in1=sgB[:, jj, h * 64:(h + 1) * 64],
                                op0=ALU.mult, op1=ALU.mult),
                                reads=[ok, "rinv", ("sg", jj)], writes=[("mixed", 4 * b + jj)])

                    attention_units(p, b, 8, qk_B, exp_B, post_B,
                                    lambda k, h: Vb[:, k, h * 65:(h + 1) * 65], fin_B, PTs, Lps, Ops)
                if seq == 0 and "mixedB" in dbg_d:
                    flat = mixed[:].rearrange("p a b -> p (a b)")
                    for c0 in range(0, 16384, 2048):
                        p.dma("pool", "dbgp", lambda e, c0=c0: e.dma_start(
                            out=dbg_d["mixedB"][:, c0:c0 + 2048], in_=flat[:, c0:c0 + 2048]),
                            reads=list(p.lw.keys()))
                p.finish()
                p.emit()
            if stage <= 4:
                continue
            stO = ExitStack()
            with stO:
                p = Prog(nc, stO)
                Ot = lambda name, shape, dt: sb(f"O{seq}_{name}", shape, dt, stO)
                WO = Ot("WO", [128, 8, 1024], BF16)
                mT = [Ot(f"mT{i}", [128, 8, 128], BF16) for i in range(2)]
                xst = [Ot(f"xs{i}", [128, 1024], F32) for i in range(2)]
                ost = [Ot(f"os{i}", [128, 1024], F32) for i in range(2)]
                load_w(p, WO, wout_d[:, :], 8, "wo", "WO")
                PJ = Rot([(PS[i], f"ps{i}") for i in range(4)])
                for i in range(16):
                    mt, mk = mT[i % 2], f"mT{i % 2}"
                    xs, xk = xst[i % 2], f"xs{i % 2}"
                    os_, okk = ost[i % 2], f"os{i % 2}"
                    p.dma("sp", xk, lambda e, xs=xs, i=i: e.dma_start(out=xs[:], in_=x_d[seq, i * 128:(i + 1) * 128, :]),
                          writes=[xk])
                    for g in range(2):
                        for q in range(4):
                            kc = 4 * g + q
                            p.op("pe", lambda e, kc=kc, q=q, i=i: e.transpose(
                                out=PST[:, q * 128:(q + 1) * 128], in_=mixed[:, i, kc * 128:(kc + 1) * 128],
                                identity=ident),
                                reads=["mats"], writes=["pst"])
                        if g == 0:
                            p.op("act", lambda e, g=g, mt=mt: e.copy(
                                out=mt[:, 4 * g:4 * g + 4, :], in_=PST[:, 0:512].rearrange("p (q t) -> p q t", t=128)),
                                reads=["pst"], writes=[mk])
                        else:
                            p.op("dve", lambda e, g=g, mt=mt: e.tensor_copy(
                                out=mt[:, 4 * g:4 * g + 4, :], in_=PST[:, 0:512].rearrange("p (q t) -> p q t", t=128)),
                                reads=["pst"], writes=[mk])
                    for nh in range(2):
                        pj, pk = PJ.next()
                        for kc in range(8):
                            p.op("pe", lambda e, kc=kc, pj=pj, mt=mt, nh=nh: e.matmul(
                                pj[:, :], lhsT=mt[:, kc, :], rhs=WO[:, kc, nh * 512:(nh + 1) * 512],
                                start=(kc == 0), stop=(kc == 7)),
                                reads=[mk, ("WO", kc)], writes=[pk])
                        p.op("dve", lambda e, pj=pj, os_=os_, xs=xs, nh=nh: e.tensor_tensor(
                            out=os_[:, nh * 512:(nh + 1) * 512], in0=pj[:, :], in1=xs[:, nh * 512:(nh + 1) * 512],
                            op=ALU.add),
                            reads=[pk, xk], writes=[okk])
                    p.dma("sp", "o" + okk, lambda e, os_=os_, i=i: e.dma_start(
                        out=out_d[seq, i * 128:(i + 1) * 128, :], in_=os_[:]), reads=[okk])
                p.finish()
                p.emit()
    return nc


def host_consts(inp):
    cols = np.zeros((128, 32), np.float32)
    ng = inp["norm_gain"].reshape(1024)
    cols[:, 0:8] = ng.reshape(8, 128).T
    cols[:, 8] = np.tile(inp["a_q_norm"].reshape(64), 2)
    cols[:, 9] = np.tile(inp["a_k_norm"].reshape(64), 2)
    cols[:, 10:13] = inp["b_q_latent_norm"].reshape(3, 128).T
    cols[:, 13:15] = inp["b_kv_latent_norm"].reshape(2, 128).T
    cols[:96, 15] = inp["b_q_norm"].reshape(96)
    cols[:96, 16] = inp["b_k_norm"].reshape(96)
    pidx = np.arange(128)
    cols[:, 17] = np.power(np.float32(10000.0), -(pidx % 32).astype(np.float32) * np.float32(2.0) / np.float32(64))
    cols[:, 18] = np.power(np.float32(10000.0), -(pidx % 16).astype(np.float32) * np.float32(2.0) / np.float32(32))
    m64 = pidx % 64
    cols[:, 19] = np.where(m64 < 32, 1.0, -1.0)
    cols[:, 20] = np.where(m64 < 16, 1.0, np.where(m64 < 32, -1.0, 0.0))
    mI = (m64 < 32).astype(np.float32)
    cols[:, 21] = -mI
    cols[:, 22] = 1.0 - mI
    mats = np.zeros((128, 6, 128), np.float32)
    mats[:, 0, :] = np.eye(128)
    for m in range(128):
        pm = m + 32 if m64[m] < 32 else m - 32
        mats[pm, 1, m] = 1.0
        if m64[m] < 16:
            mats[m + 16, 2, m] = 1.0
        elif m64[m] < 32:
            mats[m - 16, 2, m] = 1.0
    mats[:, 3, :] = 1.0
    mats[:, 4, :] = (pidx[:, None] // 64 == pidx[None, :] // 64).astype(np.float32)
    pos = np.arange(S, dtype=np.float32)
    tabs = np.zeros((128, 4, S), np.float32)
    angA = pos[None, :] * cols[:, 17:18]
    angI = pos[None, :] * cols[:, 18:19]
    sgnA = np.where(m64 < 32, -1.0, 1.0)[:, None]
    sgnI = np.where(m64 < 16, -1.0, np.where(m64 < 32, 1.0, 0.0))[:, None]
    tabs[:, 0] = np.cos(angA)
    tabs[:, 1] = sgnA * np.sin(angA)
    tabs[:, 2] = np.where(mI[:, None] > 0, np.cos(angI), 1.0)
    tabs[:, 3] = sgnI * np.sin(angI)
    tabs = np.ascontiguousarray(tabs.reshape(128, 4, 4, 512).transpose(2, 0, 1, 3)).astype(np.float32)
    return cols, mats, tabs


def make_in_maps(inp):
    cols, mats, tabs = host_consts(inp)
    x = np.ascontiguousarray(inp["x"], dtype=np.float32)
    maps = []
    for c in range(NCORES):
        maps.append({
            "x": np.ascontiguousarray(x[c * SEQ_PER_CORE:(c + 1) * SEQ_PER_CORE]),
            "w_in": np.ascontiguousarray(inp["w_in"][0]),
            "w_uq": np.ascontiguousarray(inp["w_uq"][0]),
            "w_ukv": np.ascontiguousarray(inp["w_ukv"][0]),
            "w_out": np.ascontiguousarray(inp["w_out"][0]),
            "cols": cols, "mats": mats, "tabs": tabs,
        })
    return maps


def kernel(**inputs):
    inp = {k: np.asarray(v) for k, v in inputs.items()}
    nc = build_nc()
    res = run_bass_kernel_spmd(nc, make_in_maps(inp), core_ids=list(range(NCORES)))
    out = np.concatenate([np.asarray(r["out"]) for r in res.results], axis=0)
    return out.astype(np.float32)
```

```python
import numpy as np
import os
VAR = os.environ.get('KVAR', '')
from contextlib import ExitStack
import concourse.bass as bass
import concourse.mybir as mybir
from concourse.bass_utils import run_bass_kernel_spmd

F32 = mybir.dt.float32
BF16 = mybir.dt.bfloat16
I32 = mybir.dt.int32
ALU = mybir.AluOpType
AF = mybir.ActivationFunctionType
AX = mybir.AxisListType

S = 2048
DM = 1024
NCORES = 8
SEQ_PER_CORE = 2
EPS = 1e-6
NIT = 24
TOPK = 256
NEG = -30000.0
PI = float(np.pi)

ENGS = ("pe", "act", "dve", "pool", "sp")


class Prog:
    uid = 0

    def __init__(self, nc, st):
        Prog.uid += 1
        self.nc = nc
        self.st = st
        self.sems = {("eng", e): st.enter_context(nc.semaphore(f"se_{e}_{Prog.uid}")) for e in ENGS}
        self.cnt = {e: 0 for e in ENGS}
        self.ops = {e: [] for e in ENGS}
        self.waited = {e: {} for e in ENGS}
        self.lw = {}
        self.rd = {}
        self.chcnt = {}

    def _deps(self, e, reads, writes):
        deps = {}

        def add(s, v):
            if deps.get(s, 0) < v:
                deps[s] = v
        for k in reads:
            t = self.lw.get(k)
            if t:
                add(*t)
        for k in writes:
            t = self.lw.get(k)
            if t:
                add(*t)
            for s, v in self.rd.get(k, {}).items():
                add(s, v)
        out = []
        for s, v in deps.items():
            if e == "pe" and s == ("eng", "pe"):
                continue
            if self.waited[e].get(s, 0) >= v:
                continue
            self.waited[e][s] = v
            out.append((s, v))
        return out

    def _commit(self, tok, reads, writes):
        for k in reads:
            d = self.rd.setdefault(k, {})
            if d.get(tok[0], 0) < tok[1]:
                d[tok[0]] = tok[1]
        for k in writes:
            self.lw[k] = tok
            self.rd[k] = {}

    def op(self, e, fn, reads=(), writes=()):
        waits = self._deps(e, reads, writes)
        self.cnt[e] += 1
        tok = (("eng", e), self.cnt[e])
        self.ops[e].append((waits, fn, tok[0], 1))
        self._commit(tok, reads, writes)

    def dma(self, e, chan, fn, reads=(), writes=()):
        waits = self._deps(e, reads, writes)
        key = ("ch", chan)
        if key not in self.sems:
            self.sems[key] = self.st.enter_context(self.nc.semaphore(f"sc_{chan}_{Prog.uid}"))
            self.chcnt[key] = 0
        self.chcnt[key] += 16
        tok = (key, self.chcnt[key])
        self.ops[e].append((waits, fn, key, 16))
        self._commit(tok, reads, writes)

    def finish(self):
        waits = []
        for key, v in self.chcnt.items():
            if self.waited["sp"].get(key, 0) < v:
                waits.append((key, v))
        self.ops["sp"].append((waits, None, None, 0))

    def emit(self):
        nc = self.nc
        engobj = {"pe": nc.tensor, "act": nc.scalar, "dve": nc.vector, "pool": nc.gpsimd, "sp": nc.sync}
        with nc.Block() as block:
            def run(e):
                eng = engobj[e]
                for waits, fn, inc, amt in self.ops[e]:
                    for s, v in waits:
                        eng.wait_ge(self.sems[s], v)
                    if fn is not None:
                        ins = fn(eng)
                        ins.then_inc(self.sems[inc], amt)

            @block.tensor
            def _(t):
                run("pe")

            @block.scalar
            def _(t):
                run("act")

            @block.vector
            def _(t):
                run("dve")

            @block.gpsimd
            def _(t):
                run("pool")

            @block.sync
            def _(t):
                run("sp")


def run_pipeline(jobs):
    n = len(jobs)
    S = max(len(j) for j in jobs)
    states = [dict() for _ in jobs]
    for step in range(n + S - 1):
        for st_i in range(S):
            c = step - st_i
            if 0 <= c < n and st_i < len(jobs[c]):
                jobs[c][st_i](states[c])


class Rot:
    def __init__(self, items):
        self.items = list(items)
        self.i = 0

    def next(self):
        it = self.items[self.i % len(self.items)]
        self.i += 1
        return it


def build_nc(stage=99, dbg=None, bstage=99, skipA=False, nseq=SEQ_PER_CORE):
    dbg = dbg if dbg is not None else {}
    nc = bass.Bass("TRN2", target_bir_lowering=False)
    x_d = nc.dram_tensor("x", [SEQ_PER_CORE, S, DM], F32, kind="ExternalInput").ap()
    win_d = nc.dram_tensor("w_in", [DM, 3816], F32, kind="ExternalInput").ap()
    wuq_d = nc.dram_tensor("w_uq", [384, 768], F32, kind="ExternalInput").ap()
    wukv_d = nc.dram_tensor("w_ukv", [256, 1024], F32, kind="ExternalInput").ap()
    wout_d = nc.dram_tensor("w_out", [DM, DM], F32, kind="ExternalInput").ap()
    cols_d = nc.dram_tensor("cols", [128, 32], F32, kind="ExternalInput").ap()
    mats_d = nc.dram_tensor("mats", [128, 6, 128], F32, kind="ExternalInput").ap()
    tabs_d = nc.dram_tensor("tabs", [4, 128, 4, 512], F32, kind="ExternalInput").ap()
    out_d = nc.dram_tensor("out", [SEQ_PER_CORE, S, DM], F32, kind="ExternalOutput").ap()
    dbg_d = {}
    for name, shape in dbg.items():
        dbg_d[name] = nc.dram_tensor("dbg_" + name, list(shape), F32, kind="ExternalOutput").ap()

    top = ExitStack()
    with top:
        def sb(name, shape, dt, st=top):
            return st.enter_context(nc.sbuf_tensor("sb_" + name, list(shape), dt))

        cols = sb("cols", [128, 32], F32)
        cols2 = sb("cols2", [128, 8], F32)
        matsf = sb("matsf", [128, 6, 128], F32)
        mats = sb("matsb", [128, 6, 128], BF16)
        ident = mats[:, 0, :]
        permA = mats[:, 1, :]
        permI = mats[:, 2, :]
        ones = mats[:, 3, :]
        bdA = mats[:, 4, :]
        zeros = mats[:, 5, :]
        fin260 = mats[:].rearrange("p a b -> p (a b)")[:, 0:260]
        mixed = sb("mixed", [128, 16, 1024], BF16)
        PS = [top.enter_context(nc.psum_tensor(f"ps{i}", [128, 512], F32)) for i in range(7)]
        PST = top.enter_context(nc.psum_tensor("pst", [128, 1024], BF16))

        C_NG, C_AQ, C_AK, C_CQ, C_CKV, C_BQ, C_BK = 0, 8, 9, 10, 13, 15, 16
        C_FA, C_FI, C_NSA, C_NSI, C_NCMI, C_OMI = 17, 18, 19, 20, 21, 22

        st0 = ExitStack()
        with st0:
            p = Prog(nc, st0)
            p.dma("sp", "c0", lambda e: e.dma_start(out=cols[:], in_=cols_d[:, :]), writes=["cols"])
            p.dma("sp", "c1", lambda e: e.dma_start(out=matsf[:], in_=mats_d[:, :, :]), writes=["matsf"])
            p.op("dve", lambda e: e.tensor_copy(out=mats[:], in_=matsf[:]), reads=["matsf"], writes=["mats"])
            p.op("dve", lambda e: e.tensor_scalar(out=cols2[:, 0:1], in0=cols[:, C_AQ:C_AQ + 1], scalar1=0.125,
                                                  scalar2=None, op0=ALU.mult), reads=["cols"], writes=["cols2"])
            p.op("dve", lambda e: e.tensor_scalar(out=cols2[:, 1:2], in0=cols[:, C_BQ:C_BQ + 1],
                                                  scalar1=float(96.0 ** -0.5), scalar2=None, op0=ALU.mult),
                 reads=["cols"], writes=["cols2"])
            p.op("dve", lambda e: e.memset(cols2[:, 7:8], 1024.0 * EPS), writes=["cols2"])
            p.op("dve", lambda e: e.memset(cols2[:, 6:7], EPS), writes=["cols2"])
            p.finish()
            p.emit()

        def colap(i, lo=0, hi=128):
            return cols[lo:hi, i:i + 1]

        def common_p0(p, st, seq, b, xnT, xst_rot, tag):
            for j in range(4):
                i = 4 * b + j
                xs, xk, ch = xst_rot.next()
                p.dma("sp", ch, lambda e, xs=xs, i=i: e.dma_start(out=xs[:], in_=x_d[seq, i * 128:(i + 1) * 128, :]),
                      writes=[xk])
                p.op("act", lambda e, xs=xs: e.activation(out=p.junk[:, 0:1024], in_=xs[:], func=AF.Square,
                                                          accum_out=p.ss[:, 0:1]),
                     reads=[xk], writes=["junk", "ss"])
                p.op("act", lambda e: e.activation(out=p.rs[:, 1:2], in_=p.ss[:, 0:1], func=AF.Ln,
                                                   bias=cols2[:, 7:8], scale=1.0),
                     reads=["ss", "cols2"], writes=["rs1"])
                p.op("act", lambda e: e.activation(out=p.rs[:, 0:1], in_=p.rs[:, 1:2], func=AF.Exp, scale=-0.5),
                     reads=["rs1"], writes=["rs"])
                p.op("dve", lambda e, xs=xs: e.tensor_scalar(out=p.xb[:], in0=xs[:], scalar1=p.rs[:, 0:1],
                                                             scalar2=32.0, op0=ALU.mult, op1=ALU.mult),
                     reads=[xk, "rs"], writes=["xb"])
                for g in range(2):
                    for q in range(4):
                        kc = 4 * g + q
                        p.op("pe", lambda e, kc=kc, q=q: e.transpose(out=PST[:, q * 128:(q + 1) * 128],
                                                                      in_=p.xb[:, kc * 128:(kc + 1) * 128],
                                                                      identity=ident),
                             reads=["xb", "mats"], writes=["pst"])
                    eng = "act" if g == 0 else "dve"
                    if eng == "act":
                        p.op("act", lambda e, g=g, j=j: e.copy(
                            out=xnT[:, 4 * g:4 * g + 4, j * 128:(j + 1) * 128],
                            in_=PST[:, 0:512].rearrange("p (q t) -> p q t", t=128)),
                            reads=["pst"], writes=["xnT"])
                    else:
                        p.op("dve", lambda e, g=g, j=j: e.tensor_copy(
                            out=xnT[:, 4 * g:4 * g + 4, j * 128:(j + 1) * 128],
                            in_=PST[:, 0:512].rearrange("p (q t) -> p q t", t=128)),
                            reads=["pst"], writes=["xnT"])

        def make_tables(p, b, tabs, ntab=4):
            p.dma("sp", "tab", lambda e: e.dma_start(out=tabs[:, 0:ntab, :], in_=tabs_d[b, :, 0:ntab, :]),
                  writes=[("tab", ti) for ti in range(4)])

        def load_w(p, dst, src, nk, chan, key):
            for kc in range(nk):
                p.dma("pool", chan, lambda e, kc=kc: e.dma_start(
                    out=dst[:, kc, :], in_=src[kc * 128:(kc + 1) * 128, :], max_dma_last_dim=4096),
                    writes=[(key, kc)])
            tot = (("ch", chan), p.chcnt[("ch", chan)])
            for kc in range(nk):
                p.lw[(key, kc)] = tot

        def scale_rows(p, dst, nk, colbase, key, eng="dve"):
            for kc in range(nk):
                p.op(eng, lambda e, kc=kc: e.tensor_scalar(out=dst[:, kc, :], in0=dst[:, kc, :],
                                                           scalar1=colap(colbase + kc), scalar2=None, op0=ALU.mult),
                     reads=[(key, kc), "cols"], writes=[(key, kc)])

        def attention_units(p, b, nheads, qk_fn, exp_fn, post_fn, v_ap_fn, fin_fn, PTs, Lps, Ops):
            for _ in attention_units_gen(p, b, nheads, qk_fn, exp_fn, post_fn, v_ap_fn, fin_fn, PTs, Lps, Ops):
                pass

        def attention_units_gen(p, b, nheads, qk_fn, exp_fn, post_fn, v_ap_fn, fin_fn, PTs, Lps, Ops):
            units = [(h, k) for h in range(nheads) for k in range(4 * b + 4)]
            LOOK = min(2, max(1, len(Lps.items) - 1))
            state = {}

            def issue_qk(u):
                h, k = units[u]
                kk = k - 4 * b
                col0 = 128 * max(kk, 0)
                lp, lk = Lps.next()
                state[u] = (lp, lk, col0)
                qk_fn(h, k, col0, lp, lk)

            def issue_rest(u):
                h, k = units[u]
                lp, lk, col0 = state.pop(u)
                kk = k - 4 * b
                pt, pk = PTs.next()
                N = 512 - col0
                exp_fn(h, k, col0, lp, lk, pt, pk, N)
                if post_fn is not None and kk >= 0:
                    post_fn(h, k, col0, pt, pk)
                if k == 0:
                    state[("O", h)] = Ops.next()
                    O0, ok0 = state[("O", h)]
                    p.op("pe", lambda e, O0=O0: e.matmul(O0[:, 0:260], lhsT=zeros, rhs=fin260, start=True, stop=False),
                         reads=["mats"], writes=[ok0])
                O, ok = state[("O", h)]
                for jj in range(max(kk, 0), 4):
                    c0 = jj * 128 - col0
                    p.op("pe", lambda e, O=O, jj=jj, pt=pt, c0=c0, h=h, k=k: e.matmul(
                        O[:, jj * 65:jj * 65 + 65], lhsT=pt[:, c0:c0 + 128], rhs=v_ap_fn(k, h),
                        start=False, stop=(k == 4 * b + 3 and jj == 3)),
                        reads=[pk, ("V", k // 4)], writes=[ok])
                if k == 4 * b + 3:
                    fin_fn(h, O, ok)

            n = len(units)
            for u in range(min(LOOK, n)):
                issue_qk(u)
            for u in range(n):
                if u + LOOK < n:
                    issue_qk(u + LOOK)
                issue_rest(u)
                yield 1.3

        for seq in range(nseq):
            stA = ExitStack()
            if not skipA:
              with stA:
                  p = Prog(nc, stA)
                  A = lambda name, shape, dt: sb(f"A{seq}_{name}", shape, dt, stA)
                  WA = A("WA", [128, 8, 2632], BF16)
                  WKI2 = A("WKI2", [128, 8, 128], BF16)
                  SC = [A(f"scores{i}", [128, 2048], F32) for i in range(2)]
                  xnT = SC[0][:].bitcast(BF16).rearrange("p (k t) -> p k t", t=512)
                  sc1b = SC[1][:].bitcast(BF16)
                  xst_rot = Rot([(SC[1][:, 0:1024], "xs0", "xs0")])
                  p.junk = sc1b[:, 3072:4096]
                  p.ss = A("ss", [128, 1], F32)
                  p.rs = A("rs", [128, 2], F32)
                  p.xb = sc1b[:, 2048:3072]
                  tabs = A("tabs", [128, 4, 512], F32)
                  KaT = A("KaT", [128, 4, 2048], BF16)
                  KiT = A("KiT", [128, 2048], BF16)
                  Va = A("Va", [128, 16, 8 * 65], BF16)
                  QaTb = [A(f"QaT{i}", [128, 4, 512], BF16) for i in range(2)]
                  QiT = A("QiT", [128, 4, 512], BF16)
                  sgAb = [A(f"sgA{i}", [128, 4, 512], BF16) for i in range(2)]
                  wi = A("wi", [128, 4, 8], F32)
                  mbuf = [A(f"mb{i}", [128, 4, 2048], BF16) for i in range(2)]
                  PTt = [A(f"PT{i}", [128, 512], BF16) for i in range(3)]
                  Rt = [A(f"R{i}", [128, 512], BF16) for i in range(4)]
                  Ra = [A(f"Ra{i}", [128, 512], BF16) for i in range(2)]
                  Dg = A("Dg", [128, 4, 128], BF16)
                  sqb = [A(f"sqb{i}", [128, 512], BF16) for i in range(2)]
                  yb = [A(f"yb{i}", [128, 512], BF16) for i in range(3)]
                  bis = A("bis", [128, 16], F32)
                  wab = A("wab", [128, 4, 8], F32)
                  wsg = A("wsg", [128, 4, 8], F32)
                  rinv = A("rinv", [128, 4], F32)
                  fdum = A("fdum", [128, 2], F32)
                  PSTf = PST[:].bitcast(F32)

                  def fence(bi):
                      keys = ["sc0", "sc1", "xnT", "xs0", "xb", "junk", "ta0", "ta1", "tb0", "tb1", "rst0", "rst1"]
                      keys += [("mb", bi, jq) for jq in range(4)]
                      p.op("dve", lambda e: e.memset(fdum[:, 0:1], 0.0), writes=keys)

                  load_w(p, WA, win_d[:, 0:2632], 8, "wa", "WA")
                  scale_rows(p, WA, 8, C_NG, "WA")
                  for kc in range(8):
                      p.op("pool", lambda e, kc=kc: e.tensor_copy(
                          out=WKI2[:, kc, :].rearrange("p (r c) -> p r c", r=2),
                          in_=WA[:, kc, 2560:2624].unsqueeze(1).to_broadcast([128, 2, 64])),
                          reads=[("WA", kc)], writes=[("WKI2", kc)])
                  p.op("pool", lambda e: e.memset(Va[:].rearrange("p i (h d) -> p (i h) d", d=65)[:, :, 64:65], 1.0),
                       writes=[("V", q) for q in range(4)])

                  Lps = Rot([(PS[0], "ps0"), (PS[1], "ps1")])

                  def blockA_p1(b):
                      t0 = 512 * b
                      bi = b % 2
                      QaT, sgA = QaTb[bi], sgAb[bi]
                      f32v = mbuf[bi][:].rearrange("p a c -> p (a c)").bitcast(F32)
                      ta = [f32v[:, 0:512], f32v[:, 512:1024]]
                      tb = [f32v[:, 1024:1536], f32v[:, 1536:2048]]
                      rst = [f32v[:, 2048:2560], f32v[:, 2560:3072]]
                      fence(bi)
                      common_p0(p, stA, seq, b, xnT, xst_rot, "A")
                      make_tables(p, b, tabs)
                      PJ = Rot([(PS[0], "ps0"), (PS[1], "ps1"), (PS[6], "ps6")])
                      YB = Rot([(yb[0], "yb0"), (yb[1], "yb1"), (yb[2], "yb2")])
                      SQ = Rot([(sqb[0], "sqb0"), (sqb[1], "sqb1")])
                      MS = Rot([(PS[2], "ps2"), (PS[4], "ps4")])
                      PR = Rot([(PS[3], "ps3"), (PS[5], "ps5")])
                      RS = Rot([(rst[0], "rst0"), (rst[1], "rst1")])
                      TA = Rot([(ta[0], "ta0"), (ta[1], "ta1")])
                      TB = Rot([(tb[0], "tb0"), (tb[1], "tb1")])

                      def proj_fm(col, wt=WA, wkey="WA", ncol=128):
                          pj, pk = PJ.next()
                          for kc in range(8):
                              p.op("pe", lambda e, kc=kc, pj=pj: e.matmul(
                                  pj[0:ncol, :], lhsT=wt[:, kc, col:col + ncol], rhs=xnT[:, kc, :],
                                  start=(kc == 0), stop=(kc == 7)),
                                  reads=[(wkey, kc), "xnT"], writes=[pk])
                          return pj, pk

                      def rope_finish(yt, yk, perm, tcos, tsin, dst_fn, dkey):
                          pr_, prk = PR.next()
                          ta_, tak = TA.next()
                          tb_, tbk = TB.next()
                          p.op("pe", lambda e, yt=yt, pr_=pr_: e.matmul(pr_[:, :], lhsT=perm, rhs=yt[:], start=True, stop=True),
                               reads=[yk, "mats"], writes=[prk])
                          p.op("dve", lambda e, yt=yt, ta_=ta_: e.tensor_tensor(out=ta_[:], in0=yt[:], in1=tabs[:, tcos, :],
                                                                                op=ALU.mult),
                               reads=[yk, ("tab", tcos)], writes=[tak])
                          p.op("dve", lambda e, pr_=pr_, tb_=tb_: e.tensor_tensor(out=tb_[:], in0=pr_[:, :], in1=tabs[:, tsin, :],
                                                                                  op=ALU.mult),
                               reads=[prk, ("tab", tsin)], writes=[tbk])
                          p.op("pool", lambda e, ta_=ta_, tb_=tb_: e.tensor_tensor(out=dst_fn(), in0=ta_[:], in1=tb_[:], op=ALU.add),
                               reads=[tak, tbk], writes=[dkey])

                      jobs = []
                      for kind in ("q", "k"):
                          for c in range(4):
                              col = (0 if kind == "q" else 512) + c * 128
                              gcol = cols2[:, 0:1] if kind == "q" else colap(C_AK)
                              if kind == "q":
                                  dst_fn, dkey = (lambda c=c: QaT[:, c, :]), ("QaT", bi, c)
                              else:
                                  dst_fn, dkey = (lambda c=c, t0=t0: KaT[:, c, t0:t0 + 512]), ("KaT", b)

                              def s0(st, col=col):
                                  st["pj"], st["pk"] = proj_fm(col)

                              def s1(st):
                                  pj, pk = st["pj"], st["pk"]
                                  sq_, sqk = SQ.next()
                                  ms_, msk = MS.next()
                                  st["ms"], st["msk"] = ms_, msk
                                  p.op("act", lambda e, pj=pj, sq_=sq_: e.activation(out=sq_[:], in_=pj[:, :],
                                                                                     func=AF.Square),
                                       reads=[pk], writes=[sqk])
                                  p.op("pe", lambda e, sq_=sq_, ms_=ms_: e.matmul(ms_[:, :], lhsT=bdA, rhs=sq_[:],
                                                                                  start=True, stop=True),
                                       reads=[sqk, "mats"], writes=[msk])

                              def s2(st, gcol=gcol, dst_fn=dst_fn, dkey=dkey):
                                  pj, pk, ms_, msk = st["pj"], st["pk"], st["ms"], st["msk"]
                                  rs_, rsk = RS.next()
                                  p.op("act", lambda e, ms_=ms_, rs_=rs_: e.activation(
                                      out=rs_[:], in_=ms_[:, :], func=AF.Ln, bias=cols2[:, 6:7], scale=1.0 / 64.0),
                                      reads=[msk, "cols2"], writes=[rsk])
                                  p.op("act", lambda e, rs_=rs_: e.activation(out=rs_[:], in_=rs_[:], func=AF.Exp,
                                                                              scale=-0.5),
                                       reads=[rsk], writes=[rsk])
                                  yt, yk = YB.next()
                                  p.op("dve", lambda e, pj=pj, yt=yt, gcol=gcol, rs_=rs_: e.scalar_tensor_tensor(
                                      out=yt[:], in0=pj[:, :], scalar=gcol, in1=rs_[:], op0=ALU.mult, op1=ALU.mult),
                                      reads=[pk, rsk, "cols", "cols2"], writes=[yk])
                                  rope_finish(yt, yk, permA, 0, 1, dst_fn, dkey)
                              jobs.append([s0, s1, s2])
                      for c in range(5):
                          if c < 4:
                              dst_fn, dkey = (lambda c=c: QiT[:, c, :]), ("QiT", c)
                          else:
                              dst_fn, dkey = (lambda t0=t0: KiT[:, t0:t0 + 512]), ("KiT", b)

                          def s0(st, c=c):
                              if c < 4:
                                  st["pj"], st["pk"] = proj_fm(2048 + c * 128)
                              else:
                                  st["pj"], st["pk"] = proj_fm(0, wt=WKI2, wkey="WKI2")

                          def s1(st):
                              pj, pk = st["pj"], st["pk"]
                              yt, yk = YB.next()
                              st["yt"], st["yk"] = yt, yk
                              p.op("act", lambda e, pj=pj, yt=yt: e.copy(out=yt[:], in_=pj[:, :]),
                                   reads=[pk], writes=[yk])

                          def s2(st, dst_fn=dst_fn, dkey=dkey):
                              rope_finish(st["yt"], st["yk"], permI, 2, 3, dst_fn, dkey)
                          jobs.append([s0, s1, s2])
                      run_pipeline(jobs)
                      for j in range(4):
                          i = 4 * b + j
                          for what in ("v", "g", "w"):
                              pj, pk = PJ.next()
                              c0, n = {"v": (1024, 512), "g": (1536, 512), "w": (2624, 8)}[what]
                              for kc in range(8):
                                  p.op("pe", lambda e, kc=kc, pj=pj, j=j, c0=c0, n=n: e.matmul(
                                      pj[:, 0:n], lhsT=xnT[:, kc, j * 128:(j + 1) * 128], rhs=WA[:, kc, c0:c0 + n],
                                      start=(kc == 0), stop=(kc == 7)),
                                      reads=[("WA", kc), "xnT"], writes=[pk])
                              if what == "v":
                                  p.op("dve", lambda e, pj=pj, i=i: e.tensor_copy(
                                      out=Va[:, i, :].rearrange("p (h d) -> p h d", d=65)[:, :, 0:64],
                                      in_=pj[:, :].rearrange("p (h d) -> p h d", d=64)),
                                      reads=[pk], writes=[("V", b)])
                              elif what == "g":
                                  p.op("act", lambda e, pj=pj, j=j: e.activation(out=sgA[:, j, :], in_=pj[:, :],
                                                                                 func=AF.Silu),
                                       reads=[pk], writes=[("sg", bi, j)])
                              else:
                                  p.op("dve", lambda e, pj=pj, j=j: e.tensor_copy(out=wi[:, j, :], in_=pj[:, 0:8]),
                                       reads=[pk], writes=[("wi", j)])
                                  p.op("dve", lambda e, j=j: e.tensor_scalar(out=wab[:, j, :], in0=wi[:, j, :], scalar1=-1.0,
                                                                             scalar2=None, op0=ALU.mult),
                                       reads=[("wi", j)], writes=[("wab", j)])
                                  p.op("dve", lambda e, j=j: e.tensor_tensor(out=wab[:, j, :], in0=wab[:, j, :],
                                                                             in1=wi[:, j, :], op=ALU.max),
                                       reads=[("wi", j), ("wab", j)], writes=[("wab", j)])
                                  p.op("dve", lambda e, j=j: e.tensor_scalar(out=wsg[:, j, :], in0=wi[:, j, :], scalar1=0.0,
                                                                             scalar2=2.0, op0=ALU.is_ge, op1=ALU.mult),
                                       reads=[("wi", j)], writes=[("wsg", j)])
                                  p.op("dve", lambda e, j=j: e.tensor_scalar(out=wsg[:, j, :], in0=wsg[:, j, :], scalar1=-1.0,
                                                                             scalar2=None, op0=ALU.add),
                                       reads=[("wsg", j)], writes=[("wsg", j)])

                      fence(bi)

                  def blockA_idx(b):
                      bi = b % 2
                      mb = mbuf[bi]
                      XP = Rot([(PS[i], f"ps{i}") for i in range(3, 7)])
                      ACC = Rot([(PSTf, "pst")])
                      RR = Rot([(Rt[i], f"R{i}") for i in range(4)])
                      RA = Rot([(Ra[i], f"Ra{i}") for i in range(2)])
                      for pr in range(2):
                          tiles = []
                          for q in range(2):
                              j = 2 * pr + q
                              i = 4 * b + j
                              L2 = 128 * (i + 1)
                              L1 = L2 - 64
                              sc, sck = SC[q], f"sc{q}"
                              tiles.append((j, i, L1, L2, sc, sck))
                              for dq, hh in enumerate((2, 3, 6, 7)):
                                  p.op("dve", lambda e, dq=dq, hh=hh, j=j: e.tensor_scalar(
                                      out=Dg[:, dq, :], in0=ident, scalar1=wsg[:, j, hh:hh + 1], scalar2=None,
                                      op0=ALU.mult),
                                      reads=["mats", ("wsg", j)], writes=[("Dg", dq)])
                              for sbk in range(b + 1):
                                  wd = 512 if sbk < b else 128 * (j + 1)
                                  acc, acck = ACC.next()
                                  pend = []

                                  def issue_x(h, j=j, sbk=sbk, wd=wd):
                                      c, base = h // 2, (h % 2) * 64
                                      xp, xk = XP.next()
                                      p.op("pe", lambda e, xp=xp, c=c, base=base: e.matmul(
                                          xp[:, 0:wd], lhsT=QiT[base:base + 64, c, j * 128:(j + 1) * 128],
                                          rhs=KiT[base:base + 64, sbk * 512:sbk * 512 + wd], start=True, stop=True),
                                          reads=[("QiT", c), ("KiT", sbk)], writes=[xk])
                                      return xp, xk
                                  for h0 in range(4):
                                      pend.append(issue_x(h0))
                                  nacc = 0
                                  for m in range(4):
                                      xpe, xke = pend.pop(0)
                                      xpo, xko = pend.pop(0)
                                      he, ho = 2 * m, 2 * m + 1
                                      if m % 2 == 0:
                                          r, rk = RR.next()
                                          r2, rk2 = RR.next()
                                          for (xp_, xk_, r_, rk_, h_) in ((xpe, xke, r, rk, he), (xpo, xko, r2, rk2, ho)):
                                              p.op("dve", lambda e, xp=xp_, r=r_, h=h_, j=j, wd=wd: e.tensor_scalar(
                                                  out=r[:, 0:wd], in0=xp[:, 0:wd], scalar1=0.0, scalar2=wi[:, j, h:h + 1],
                                                  op0=ALU.max, op1=ALU.mult),
                                                  reads=[xk_, ("wi", j)], writes=[rk_])
                                          if 2 * m + 4 < 8:
                                              pend.append(issue_x(2 * m + 4))
                                              pend.append(issue_x(2 * m + 5))
                                          p.op("pool", lambda e, r=r, r2=r2, wd=wd: e.tensor_tensor(
                                              out=r[:, 0:wd], in0=r[:, 0:wd], in1=r2[:, 0:wd], op=ALU.add),
                                              reads=[rk, rk2], writes=[rk])
                                          p.op("pe", lambda e, acc=acc, r=r, nacc=nacc, wd=wd: e.matmul(
                                              acc[:, 0:wd], lhsT=ident, rhs=r[:, 0:wd], start=(nacc == 0), stop=False),
                                              reads=[rk, "mats"], writes=[acck])
                                          nacc += 1
                                      else:
                                          r, rk = RA.next()
                                          r2, rk2 = RA.next()
                                          for (xp_, xk_, r_, rk_, h_) in ((xpe, xke, r, rk, he), (xpo, xko, r2, rk2, ho)):
                                              p.op("act", lambda e, xp=xp_, r=r_, h=h_, j=j, wd=wd: e.activation(
                                                  out=r[:, 0:wd], in_=xp[:, 0:wd], func=AF.Relu, scale=wab[:, j, h:h + 1]),
                                                  reads=[xk_, ("wab", j)], writes=[rk_])
                                          if 2 * m + 4 < 8:
                                              pend.append(issue_x(2 * m + 4))
                                              pend.append(issue_x(2 * m + 5))
                                          for (r_, rk_, h_) in ((r, rk, he), (r2, rk2, ho)):
                                              dq = (h_ // 4) * 2 + (h_ % 2)
                                              p.op("pe", lambda e, acc=acc, r=r_, nacc=nacc, wd=wd, dq=dq, m=m, h=h_: e.matmul(
                                                  acc[:, 0:wd], lhsT=Dg[:, dq, :], rhs=r[:, 0:wd], start=(nacc == 0),
                                                  stop=(h == 7)),
                                                  reads=[rk_, ("Dg", dq)], writes=[acck])
                                              nacc += 1
                                  p.op("act", lambda e, acc=acc, sbk=sbk, wd=wd, sc=sc: e.copy(
                                      out=sc[:, sbk * 512:sbk * 512 + wd], in_=acc[:, 0:wd]),
                                      reads=[acck], writes=[sck])
                                  yield 7.0 * wd / 512.0
                              p.op("pool", lambda e, L1=L1, L2=L2, sc=sc: e.memset(sc[0:64, L1:L2], -1e30),
                                   reads=[], writes=[sck])
                              if "scores" in dbg_d and seq == 0:
                                  p.dma("sp", "dbg", lambda e, i=i, L2=L2, sc=sc: e.dma_start(
                                      out=dbg_d["scores"][i, :, 0:L2], in_=sc[:, 0:L2]), reads=[sck])
                          LO2, HI2, RNG2, MID2, CNT2, GE2, T2 = [bis[:, 2 * q:2 * q + 2] for q in range(7)]
                          SBc = bis[:, 14:15]
                          if tiles[0][1] < 2:
                              p.op("dve", lambda e: e.memset(LO2, -5e29), writes=["lo0", "lo1"])
                          else:
                              for q, (j, i, L1, L2, sc, sck) in enumerate(tiles):
                                  p.op("dve", lambda e, L1=L1, sc=sc, q=q: e.tensor_reduce(
                                      out=LO2[:, q:q + 1], in_=sc[:, 0:L1], axis=AX.X, op=ALU.min),
                                      reads=[sck], writes=[f"lo{q}"])
                                  p.op("dve", lambda e, L2=L2, sc=sc, q=q: e.tensor_reduce(
                                      out=HI2[:, q:q + 1], in_=sc[:, 0:L2], axis=AX.X, op=ALU.max),
                                      reads=[sck], writes=["hi"])
                              p.op("dve", lambda e: e.tensor_tensor(out=RNG2, in0=HI2, in1=LO2, op=ALU.subtract),
                                   reads=["hi", "lo0", "lo1"], writes=["rng"])
                              (jA, iA, L1A, L2A, scA, sckA), (jB, iB, L1B, L2B, scB, sckB) = tiles
                              for it in range(1, NIT + 1):
                                  cst = float(2.0 ** (-it))
                                  p.op("dve", lambda e, cst=cst: e.scalar_tensor_tensor(
                                      out=MID2, in0=RNG2, scalar=cst, in1=LO2, op0=ALU.mult, op1=ALU.add),
                                      reads=["rng", "lo0", "lo1"], writes=["mid"])
                                  p.op("act", lambda e, jB=jB, L2B=L2B, scB=scB: e.activation(
                                      out=mb[:, jB, 0:L2B], in_=scB[:, 0:L2B], func=AF.Sign, scale=-1.0,
                                      bias=MID2[:, 1:2], accum_out=SBc),
                                      reads=[sckB, "mid"], writes=[("mb", bi, jB), "sb"])
                                  p.op("dve", lambda e, jA=jA, L2A=L2A, scA=scA: e.tensor_scalar(
                                      out=mb[:, jA, 0:L2A], in0=scA[:, 0:L2A], scalar1=MID2[:, 0:1], scalar2=0.0,
                                      op0=ALU.is_ge, op1=ALU.add, accum_out=CNT2[:, 0:1]),
                                      reads=[sckA, "mid"], writes=[("mb", bi, jA), "cnt0"])
                                  p.op("dve", lambda e, cst=cst: e.tensor_scalar(
                                      out=GE2[:, 0:1], in0=CNT2[:, 0:1], scalar1=float(TOPK) - 0.75, scalar2=cst,
                                      op0=ALU.is_ge, op1=ALU.mult),
                                      reads=["cnt0"], writes=["ge0"])
                                  p.op("dve", lambda e, cst=cst, L2B=L2B: e.tensor_scalar(
                                      out=GE2[:, 1:2], in0=SBc, scalar1=float(L2B - 2 * TOPK) + 1.5, scalar2=cst,
                                      op0=ALU.is_le, op1=ALU.mult),
                                      reads=["sb"], writes=["ge1"])
                                  p.op("dve", lambda e: e.scalar_tensor_tensor(
                                      out=LO2[:, 0:1], in0=RNG2[:, 0:1], scalar=GE2[:, 0:1], in1=LO2[:, 0:1],
                                      op0=ALU.mult, op1=ALU.add),
                                      reads=["rng", "ge0", "lo0"], writes=["lo0"])
                                  p.op("dve", lambda e: e.scalar_tensor_tensor(
                                      out=LO2[:, 1:2], in0=RNG2[:, 1:2], scalar=GE2[:, 1:2], in1=LO2[:, 1:2],
                                      op0=ALU.mult, op1=ALU.add),
                                      reads=["rng", "ge1", "lo1"], writes=["lo1"])
                                  yield 1.2 + L2B / 800.0
                          for q, (j, i, L1, L2, sc, sck) in enumerate(tiles):
                              p.op("dve", lambda e, j=j, L2=L2, sc=sc, q=q: e.tensor_scalar(
                                  out=mb[:, j, 0:L2], in0=sc[:, 0:L2], scalar1=LO2[:, q:q + 1], scalar2=NEG,
                                  op0=ALU.is_lt, op1=ALU.mult),
                                  reads=[sck, f"lo{q}"], writes=[("mb", bi, j)])
                              if "thr" in dbg_d and seq == 0:
                                  p.dma("sp", "dbg", lambda e, i=i, q=q: e.dma_start(
                                      out=dbg_d["thr"][i, :, 0:1], in_=LO2[:, q:q + 1]), reads=[f"lo{q}"])

                          yield 0.5

                  def blockA_att(b):
                      bi = b % 2
                      mb, QaT, sgA = mbuf[bi], QaTb[bi], sgAb[bi]
                      PTs = Rot([(PTt[i], f"PT{i}") for i in range(3)])
                      Ops = Rot([(PS[2], "ps2")])

                      def qk_A(h, k, col0, lp, lk):
                          c, base = h // 2, (h % 2) * 64
                          kk = k - 4 * b
                          N = 512 - col0
                          for jj in range(max(kk, 0), 4):
                              c0 = jj * 128 - col0
                              p.op("pe", lambda e, lp=lp, jj=jj, k=k, c0=c0, kk=kk: e.matmul(
                                  lp[:, c0:c0 + 128], lhsT=mb[:, jj, k * 128:(k + 1) * 128], rhs=ident,
                                  start=(jj == max(kk, 0)), stop=False),
                                  reads=[("mb", bi, jj), "mats"], writes=[lk])
                          p.op("pe", lambda e, lp=lp, k=k, c=c, base=base, col0=col0, N=N: e.matmul(
                              lp[:, 0:N], lhsT=KaT[base:base + 64, c, k * 128:(k + 1) * 128],
                              rhs=QaT[base:base + 64, c, col0:512], start=False, stop=True),
                              reads=[("KaT", k // 4), ("QaT", bi, c)], writes=[lk])

                      def exp_A(h, k, col0, lp, lk, pt, pk, N):
                          p.op("act", lambda e, lp=lp, pt=pt, N=N: e.activation(out=pt[:, 0:N], in_=lp[:, 0:N],
                                                                                 func=AF.Exp),
                               reads=[lk], writes=[pk])

                      def fin_A(h, O, ok):
                          Ov = O[:, 0:260].rearrange("p (j d) -> p j d", d=65)
                          p.op("dve", lambda e, Ov=Ov: e.reciprocal(out=rinv[:, 0:4], in_=Ov[:, :, 64]),
                               reads=[ok], writes=["rinv"])
                          for jj in range(4):
                              p.op("dve", lambda e, Ov=Ov, jj=jj, h=h, b=b: e.scalar_tensor_tensor(
                                  out=mixed[:, 4 * b + jj, h * 64:(h + 1) * 64], in0=Ov[:, jj, 0:64],
                                  scalar=rinv[:, jj:jj + 1], in1=sgA[:, jj, h * 64:(h + 1) * 64],
                                  op0=ALU.mult, op1=ALU.mult),
                                  reads=[ok, "rinv", ("sg", bi, jj)], writes=[("mixed", 4 * b + jj)])

                      yield from attention_units_gen(p, b, 8, qk_A, exp_A, None,
                                                     lambda k, h: Va[:, k, h * 65:(h + 1) * 65], fin_A, PTs, Lps, Ops)

                  def merge(g1, g2):
                      t1 = t2 = 0.0
                      a1 = a2 = True
                      while a1 or a2:
                          if a1 and (t1 <= t2 or not a2):
                              try:
                                  t1 += next(g1)
                              except StopIteration:
                                  a1 = False
                          elif a2:
                              try:
                                  t2 += next(g2)
                              except StopIteration:
                                  a2 = False

                  def drain(g):
                      for _ in g:
                          pass

                  blockA_p1(0)
                  if stage >= 2:
                      drain(blockA_idx(0))
                  for b in range(4):
                      if b + 1 < 4:
                          blockA_p1(b + 1)
                      if stage >= 3:
                          if b + 1 < 4:
                              merge(blockA_att(b), blockA_idx(b + 1))
                          else:
                              drain(blockA_att(b))
                      elif stage >= 2 and b + 1 < 4:
                          drain(blockA_idx(b + 1))

                  if seq == 0:
                      QaT, sgA = QaTb[1], sgAb[1]
                      dump = {"KaT": (KaT, [128, 4 * 2048]), "KiT": (KiT, [128, 2048]), "QaT": (QaT, [128, 4 * 512]),
                              "QiT": (QiT, [128, 4 * 512]), "sgA": (sgA, [128, 4 * 512]), "Va": (Va, [128, 16 * 520]),
                              "mixedA": (mixed, [128, 16 * 1024])}
                      for name, (tl, shp) in dump.items():
                          if name in dbg_d:
                              flat = tl[:] if len(tl.shape) == 2 else tl[:].rearrange(
                                  "p a b -> p (a b)") if len(tl.shape) == 3 else tl[:]
                              nfree = shp[1]
                              for c0 in range(0, nfree, 2048):
                                  c1 = min(nfree, c0 + 2048)
                                  p.dma("pool", "dbgp", lambda e, flat=flat, c0=c0, c1=c1, name=name: e.dma_start(
                                      out=dbg_d[name][:, c0:c1], in_=flat[:, c0:c1]),
                                      reads=list(p.lw.keys()))
                  p.finish()
                  p.emit()
            if stage <= 3:
                continue
            stB = ExitStack()
            with stB:
                p = Prog(nc, stB)
                Bt = lambda name, shape, dt: sb(f"B{seq}_{name}", shape, dt, stB)
                WB = Bt("WB", [128, 8, 1184], BF16)
                WUQ = Bt("WUQ", [128, 3, 768], BF16)
                WUKV = Bt("WUKV", [128, 2, 1024], BF16)
                xnT = Bt("xnT", [128, 8, 512], BF16)
                xst = [Bt(f"xs{i}", [128, 1024], F32) for i in range(2)]
                xst_rot = Rot([(xst[i], f"xs{i}", f"xs{i}") for i in range(2)])
                p.junk = Bt("junk", [128, 1024], BF16)
                p.ss = Bt("ss", [128, 1], F32)
                p.rs = Bt("rs", [128, 2], F32)
                p.xb = Bt("xb", [128, 1024], BF16)
                tabs = Bt("tabs", [128, 4, 512], F32)
                KbT = Bt("KbT", [128, 8, 2048], BF16)
                Vb = Bt("Vb", [128, 16, 8 * 65], BF16)
                rk = Bt("rk", [128, 16, 8], F32)
                QbT = Bt("QbT", [128, 8, 512], BF16)
                sgB = Bt("sgB", [128, 4, 512], BF16)
                cqs = Bt("cqs", [128, 3, 512], BF16)
                cqn = Bt("cqn", [128, 3, 512], BF16)
                ckvs = Bt("ckvs", [128, 2, 512], BF16)
                ckvn = Bt("ckvn", [128, 2, 512], BF16)
                sqb = [Bt(f"sqb{i}", [128, 512], BF16) for i in range(2)]
                rst = [Bt(f"rst{i}", [128, 512], F32) for i in range(2)]
                ta = [Bt(f"ta{i}", [128, 512], F32) for i in range(2)]
                tb = [Bt(f"tb{i}", [128, 512], F32) for i in range(2)]
                yb = [Bt(f"yb{i}", [128, 512], BF16) for i in range(3)]
                krs = Bt("krs", [128, 512], BF16)
                kro = Bt("kro", [128, 512], BF16)
                PTt = [Bt(f"PT{i}", [128, 512], BF16) for i in range(3)]
                ssr = Bt("ssr", [128, 4], F32)
                ssn = Bt("ssn", [128, 8], F32)
                t8 = Bt("t8", [128, 16], F32)
                sq32 = Bt("sq32", [128, 256], F32)
                rinv = Bt("rinv", [128, 4], F32)

                load_w(p, WB, win_d[:, 2632:3816], 8, "wb", "WB")
                scale_rows(p, WB, 8, C_NG, "WB")
                load_w(p, WUQ, wuq_d[:, :], 3, "wuq", "WUQ")
                scale_rows(p, WUQ, 3, C_CQ, "WUQ")
                load_w(p, WUKV, wukv_d[:, :], 2, "wukv", "WUKV")
                scale_rows(p, WUKV, 2, C_CKV, "WUKV")
                p.op("pool", lambda e: e.memset(Vb[:].rearrange("p i (h d) -> p (i h) d", d=65)[:, :, 64:65], 1.0),
                     writes=[("V", q) for q in range(4)])
                p.op("pool", lambda e: e.memset(krs[:], 0.0), writes=["krs"])

                Lps = Rot([(PS[i], f"ps{i}") for i in range(4)])
                for b in range(4):
                    t0 = 512 * b
                    if bstage <= -1:
                        continue
                    common_p0(p, stB, seq, b, xnT, xst_rot, "B")
                    if bstage <= 0:
                        continue
                    if 'notab' not in VAR:
                        p.dma("sp", "tab", lambda e, b=b: e.dma_start(out=tabs[:, 2:4, :], in_=tabs_d[b, :, 2:4, :]),
                              writes=[("tab", 2), ("tab", 3)])
                    PJ = Rot([(PS[0], "ps0"), (PS[1], "ps1"), (PS[6], "ps6")])
                    YB = Rot([(yb[0], "yb0"), (yb[1], "yb1"), (yb[2], "yb2")])
                    SQ = Rot([(sqb[0], "sqb0"), (sqb[1], "sqb1")])
                    MS = Rot([(PS[2], "ps2"), (PS[4], "ps4")])
                    PR = Rot([(PS[3], "ps3"), (PS[5], "ps5")])
                    RS = Rot([(rst[0], "rst0"), (rst[1], "rst1")])
                    TA = Rot([(ta[0], "ta0"), (ta[1], "ta1")])
                    TB = Rot([(tb[0], "tb0"), (tb[1], "tb1")])

                    def proj_fm(col, ncol=128):
                        pj, pk = PJ.next()
                        for kc in range(8):
                            p.op("pe", lambda e, kc=kc, pj=pj: e.matmul(
                                pj[0:ncol, :], lhsT=WB[:, kc, col:col + ncol], rhs=xnT[:, kc, :],
                                start=(kc == 0), stop=(kc == 7)),
                                reads=[("WB", kc), "xnT"], writes=[pk])
                        return pj, pk

                    jobs = []
                    for (nch, cbase, raw, nrm, rawk, nrmk, dim) in ((3, 0, cqs, cqn, "cqs", "cqn", 384.0),
                                                                    (2, 384, ckvs, ckvn, "ckvs", "ckvn", 256.0)):
                        grp = {}
                        for c in range(nch):
                            def s0(st, c=c, cbase=cbase):
                                st["pj"], st["pk"] = proj_fm(cbase + c * 128)

                            def s1(st, c=c, nch=nch, grp=grp, raw=raw, rawk=rawk):
                                pj, pk = st["pj"], st["pk"]
                                if c == 0:
                                    grp["ms"], grp["msk"] = MS.next()
                                ms_, msk = grp["ms"], grp["msk"]
                                sq_, sqk = SQ.next()
                                p.op("act", lambda e, pj=pj, sq_=sq_: e.activation(out=sq_[:], in_=pj[:, :],
                                                                                   func=AF.Square),
                                     reads=[pk], writes=[sqk])
                                p.op("pe", lambda e, c=c, nch=nch, ms_=ms_, sq_=sq_: e.matmul(
                                    ms_[:, :], lhsT=ones, rhs=sq_[:], start=(c == 0), stop=(c == nch - 1)),
                                    reads=[sqk, "mats"], writes=[msk])
                                p.op("dve", lambda e, pj=pj, raw=raw, c=c: e.tensor_copy(out=raw[:, c, :], in_=pj[:, :]),
                                     reads=[pk, sqk], writes=[(rawk, c)])

                            def s2(st, nch=nch, grp=grp, raw=raw, nrm=nrm, rawk=rawk, nrmk=nrmk, dim=dim):
                                ms_, msk = grp["ms"], grp["msk"]
                                rs_, rsk = RS.next()
                                p.op("act", lambda e, dim=dim, ms_=ms_, rs_=rs_: e.activation(
                                    out=rs_[:], in_=ms_[:, :], func=AF.Ln, bias=cols2[:, 6:7], scale=1.0 / dim),
                                    reads=[msk, "cols2"], writes=[rsk])
                                p.op("act", lambda e, rs_=rs_: e.activation(out=rs_[:], in_=rs_[:], func=AF.Exp,
                                                                            scale=-0.5),
                                     reads=[rsk], writes=[rsk])
                                for cc in range(nch):
                                    p.op("pool", lambda e, raw=raw, nrm=nrm, cc=cc, rs_=rs_: e.tensor_tensor(
                                        out=nrm[:, cc, :], in0=raw[:, cc, :], in1=rs_[:], op=ALU.mult),
                                        reads=[(rawk, cc), rsk], writes=[nrmk])
                            jobs.append([s0, s1, s2] if c == nch - 1 else [s0, s1])
                    run_pipeline(jobs)
                    if bstage <= 1:
                        continue
                    pj, pk = proj_fm(576, ncol=96)
                    p.op("dve", lambda e, pj=pj: e.tensor_scalar(out=krs[64:96, :], in0=pj[64:96, :],
                                                                 scalar1=colap(C_BK, 64, 96), scalar2=None,
                                                                 op0=ALU.mult),
                         reads=[pk, "cols"], writes=["krs"])
                    pr_, prk = PR.next()
                    ta_, tak = TA.next()
                    tb_, tbk = TB.next()
                    p.op("pe", lambda e, pr_=pr_: e.matmul(pr_[0:96, :], lhsT=permI[0:96, 0:96], rhs=krs[0:96, :],
                                                           start=True, stop=True),
                         reads=["krs", "mats"], writes=[prk])
                    p.op("pool", lambda e, ta_=ta_: e.tensor_tensor(out=ta_[64:96, :], in0=krs[64:96, :],
                                                                    in1=tabs[64:96, 2, :], op=ALU.mult),
                         reads=["krs", ("tab", 2)], writes=[tak])
                    p.op("dve", lambda e, pr_=pr_, tb_=tb_: e.tensor_tensor(out=tb_[64:96, :], in0=pr_[64:96, :],
                                                                            in1=tabs[64:96, 3, :], op=ALU.mult),
                         reads=[prk, ("tab", 3)], writes=[tbk])
                    p.op("pool", lambda e, ta_=ta_, tb_=tb_: e.tensor_tensor(out=kro[64:96, :], in0=ta_[64:96, :],
                                                                             in1=tb_[64:96, :], op=ALU.add),
                         reads=[tak, tbk], writes=["kro"])
                    for hh in range(8):
                        p.op("dve", lambda e, t0=t0, hh=hh: e.tensor_copy(out=KbT[64:96, hh, t0:t0 + 512],
                                                                          in_=kro[64:96, :]),
                             reads=["kro"], writes=[("KbT", b)])
                    if bstage <= 2:
                        continue
                    for j in range(4):
                        for what in ("r", "g"):
                            pj, pk = PJ.next()
                            c0, n = {"r": (640, 32), "g": (672, 512)}[what]
                            for kc in range(8):
                                p.op("pe", lambda e, kc=kc, pj=pj, j=j, c0=c0, n=n: e.matmul(
                                    pj[:, 0:n], lhsT=xnT[:, kc, j * 128:(j + 1) * 128], rhs=WB[:, kc, c0:c0 + n],
                                    start=(kc == 0), stop=(kc == 7)),
                                    reads=[("WB", kc), "xnT"], writes=[pk])
                            if what == "r":
                                p.op("act", lambda e, pj=pj, j=j: e.activation(
                                    out=sq32[:, 0:32], in_=pj[:, 0:32], func=AF.Square, accum_out=ssr[:, j:j + 1]),
                                    reads=[pk], writes=["sq32", ("ssr", j)])
                            else:
                                p.op("act", lambda e, pj=pj, j=j: e.activation(out=sgB[:, j, :], in_=pj[:, :],
                                                                               func=AF.Silu),
                                     reads=[pk], writes=[("sg", j)])
                    if bstage <= 3:
                        continue
                    jobs = []
                    for h in range(8):
                        def s0(st, h=h):
                            pj, pk = PJ.next()
                            st["pj"], st["pk"] = pj, pk
                            for c in range(3):
                                p.op("pe", lambda e, c=c, pj=pj, h=h: e.matmul(
                                    pj[0:96, :], lhsT=WUQ[:, c, h * 96:(h + 1) * 96], rhs=cqn[:, c, :],
                                    start=(c == 0), stop=(c == 2)),
                                    reads=[("WUQ", c), "cqn"], writes=[pk])

                        def s1(st):
                            pj, pk = st["pj"], st["pk"]
                            sq_, sqk = SQ.next()
                            ms_, msk = MS.next()
                            st["ms"], st["msk"] = ms_, msk
                            p.op("act", lambda e, pj=pj, sq_=sq_: e.activation(out=sq_[0:96, :], in_=pj[0:96, :],
                                                                               func=AF.Square),
                                 reads=[pk], writes=[sqk])
                            p.op("pe", lambda e, sq_=sq_, ms_=ms_: e.matmul(ms_[0:96, :], lhsT=ones[0:96, 0:96],
                                                                            rhs=sq_[0:96, :], start=True, stop=True),
                                 reads=[sqk, "mats"], writes=[msk])

                        def s2(st, h=h):
                            pj, pk, ms_, msk = st["pj"], st["pk"], st["ms"], st["msk"]
                            rs_, rsk = RS.next()
                            pr_, prk = PR.next()
                            ta_, tak = TA.next()
                            tb_, tbk = TB.next()
                            p.op("act", lambda e, ms_=ms_, rs_=rs_: e.activation(
                                out=rs_[0:96, :], in_=ms_[0:96, :], func=AF.Ln, bias=cols2[0:96, 6:7],
                                scale=1.0 / 96.0),
                                reads=[msk, "cols2"], writes=[rsk])
                            p.op("act", lambda e, rs_=rs_: e.activation(out=rs_[0:96, :], in_=rs_[0:96, :],
                                                                        func=AF.Exp, scale=-0.5),
                                 reads=[rsk], writes=[rsk])
                            yt, yk = YB.next()
                            p.op("dve", lambda e, pj=pj, yt=yt, rs_=rs_: e.scalar_tensor_tensor(
                                out=yt[0:96, :], in0=pj[0:96, :], scalar=cols2[0:96, 1:2], in1=rs_[0:96, :],
                                op0=ALU.mult, op1=ALU.mult),
                                reads=[pk, rsk, "cols2"], writes=[yk])
                            p.op("dve", lambda e, yt=yt, h=h: e.tensor_copy(out=QbT[0:64, h, :], in_=yt[0:64, :]),
                                 reads=[yk], writes=[("QbT", h)])
                            p.op("pe", lambda e, yt=yt, pr_=pr_: e.matmul(pr_[0:96, :], lhsT=permI[0:96, 0:96],
                                                                          rhs=yt[0:96, :], start=True, stop=True),
                                 reads=[yk, "mats"], writes=[prk])
                            p.op("pool", lambda e, yt=yt, ta_=ta_: e.tensor_tensor(
                                out=ta_[64:96, :], in0=yt[64:96, :], in1=tabs[64:96, 2, :], op=ALU.mult),
                                reads=[yk, ("tab", 2)], writes=[tak])
                            p.op("dve", lambda e, pr_=pr_, tb_=tb_: e.tensor_tensor(
                                out=tb_[64:96, :], in0=pr_[64:96, :], in1=tabs[64:96, 3, :], op=ALU.mult),
                                reads=[prk, ("tab", 3)], writes=[tbk])
                            p.op("pool", lambda e, h=h, ta_=ta_, tb_=tb_: e.tensor_tensor(
                                out=QbT[64:96, h, :], in0=ta_[64:96, :], in1=tb_[64:96, :], op=ALU.add),
                                reads=[tak, tbk], writes=[("QbT", h)])
                        jobs.append([s0, s1, s2])
                    run_pipeline(jobs)
                    if bstage <= 4:
                        continue
                    for j in range(4):
                        i = 4 * b + j
                        for hf in range(2):
                            pj, pk = PJ.next()
                            for c in range(2):
                                p.op("pe", lambda e, c=c, pj=pj, j=j, hf=hf: e.matmul(
                                    pj[:, :], lhsT=ckvn[:, c, j * 128:(j + 1) * 128],
                                    rhs=WUKV[:, c, hf * 512:(hf + 1) * 512], start=(c == 0), stop=(c == 1)),
                                    reads=[("WUKV", c), "ckvn"], writes=[pk])
                            pjv = pj[:, :].rearrange("p (h d) -> p h d", d=128)
                            p.op("act", lambda e, pjv=pjv, i=i, hf=hf: e.copy(
                                out=Vb[:, i, :].rearrange("p (h d) -> p h d", d=65)[:, hf * 4:hf * 4 + 4, 0:64],
                                in_=pjv[:, :, 64:128]),
                                reads=[pk], writes=[("V", b)])
                            p.op("act", lambda e, pjv=pjv: e.activation(
                                out=sq32[:].rearrange("p (h d) -> p h d", d=64), in_=pjv[:, :, 0:64], func=AF.Square),
                                reads=[pk], writes=["sq32"])
                            p.op("dve", lambda e, hf=hf: e.tensor_reduce(
                                out=ssn[:, hf * 4:hf * 4 + 4], in_=sq32[:].rearrange("p (h d) -> p h d", d=64),
                                axis=AX.X, op=ALU.add),
                                reads=["sq32"], writes=["ssn"])
                        p.op("dve", lambda e, j=j: e.tensor_scalar(out=t8[:, 0:8], in0=ssn[:, 0:8],
                                                                   scalar1=ssr[:, j:j + 1], scalar2=1.0 / 96.0,
                                                                   op0=ALU.add, op1=ALU.mult),
                             reads=["ssn", ("ssr", j)], writes=["t8"])
                        p.op("act", lambda e: e.activation(out=t8[:, 8:16], in_=t8[:, 0:8], func=AF.Ln,
                                                           bias=cols2[:, 6:7], scale=1.0),
                             reads=["t8", "cols2"], writes=["t8b"])
                        p.op("act", lambda e, i=i: e.activation(out=rk[:, i, :], in_=t8[:, 8:16], func=AF.Exp,
                                                                scale=-0.5),
                             reads=["t8b"], writes=[("rk", b)])
                    if bstage <= 5:
                        continue
                    for h in range(8):
                        pj, pk = PJ.next()
                        for c in range(2):
                            p.op("pe", lambda e, c=c, pj=pj, h=h: e.matmul(
                                pj[0:64, :], lhsT=WUKV[:, c, h * 128:h * 128 + 64], rhs=ckvn[:, c, :],
                                start=(c == 0), stop=(c == 1)),
                                reads=[("WUKV", c), "ckvn"], writes=[pk])
                        p.op("dve", lambda e, pj=pj, h=h, t0=t0: e.tensor_scalar(
                            out=KbT[0:64, h, t0:t0 + 512], in0=pj[0:64, :], scalar1=colap(C_BK, 0, 64),
                            scalar2=None, op0=ALU.mult),
                            reads=[pk, "cols"], writes=[("KbT", b)])

                    if bstage <= 6:
                        continue
                    PTs = Rot([(PTt[i], f"PT{i}") for i in range(3)])
                    Ops = Rot([(PS[4], "ps4"), (PS[5], "ps5"), (PS[6], "ps6")])

                    def qk_B(h, k, col0, lp, lk):
                        N = 512 - col0
                        p.op("pe", lambda e, lp=lp, h=h, k=k, col0=col0, N=N: e.matmul(
                            lp[:, 0:N], lhsT=KbT[0:96, h, k * 128:(k + 1) * 128], rhs=QbT[0:96, h, col0:512],
                            start=True, stop=True),
                            reads=[("KbT", k // 4), ("QbT", h)], writes=[lk])

                    def exp_B(h, k, col0, lp, lk, pt, pk, N):
                        p.op("act", lambda e, lp=lp, pt=pt, N=N, k=k, h=h: e.activation(
                            out=pt[:, 0:N], in_=lp[:, 0:N], func=AF.Exp, scale=rk[:, k, h:h + 1]),
                            reads=[lk, ("rk", k // 4)], writes=[pk])

                    def post_B(h, k, col0, pt, pk):
                        p.op("pool", lambda e, pt=pt: e.memset(pt[64:128, 0:64], 0.0), writes=[pk])

                    def fin_B(h, O, ok, b=b):
                        Ov = O[:, 0:260].rearrange("p (j d) -> p j d", d=65)
                        p.op("dve", lambda e, Ov=Ov: e.reciprocal(out=rinv[:, 0:4], in_=Ov[:, :, 64]),
                             reads=[ok], writes=["rinv"])
                        for jj in range(4):
                            p.op("dve", lambda e, Ov=Ov, jj=jj, h=h, b=b: e.scalar_tensor_tensor(
                                out=mixed[:, 4 * b + jj, 512 + h * 64:512 + (h + 1) * 64], in0=Ov[:, jj, 0:64],
                                scalar=rinv[:, jj:jj + 1], in1=sgB[:, jj, h * 64:(h + 1) * 64],
                                op0=ALU.mult, op1=ALU.mult),
                                reads=[ok, "rinv", ("sg", jj)], writes=[("mixed", 4 * b + jj)])

                    attention_units(p, b, 8, qk_B, exp_B, post_B,
                                    lambda k, h: Vb[:, k, h * 65:(h + 1) * 65], fin_B, PTs, Lps, Ops)
                if seq == 0 and "mixedB" in dbg_d:
                    flat = mixed[:].rearrange("p a b -> p (a b)")
                    for c0 in range(0, 16384, 2048):
                        p.dma("pool", "dbgp", lambda e, c0=c0: e.dma_start(
                            out=dbg_d["mixedB"][:, c0:c0 + 2048], in_=flat[:, c0:c0 + 2048]),
                            reads=list(p.lw.keys()))
                p.finish()
                p.emit()
            if stage <= 4:
                continue
            stO = ExitStack()
            with stO:
                p = Prog(nc, stO)
                Ot = lambda name, shape, dt: sb(f"O{seq}_{name}", shape, dt, stO)
                WO = Ot("WO", [128, 8, 1024], BF16)
                mT = [Ot(f"mT{i}", [128, 8, 128], BF16) for i in range(2)]
                xst = [Ot(f"xs{i}", [128, 1024], F32) for i in range(2)]
                ost = [Ot(f"os{i}", [128, 1024], F32) for i in range(2)]
                load_w(p, WO, wout_d[:, :], 8, "wo", "WO")
                PJ = Rot([(PS[i], f"ps{i}") for i in range(4)])
                for i in range(16):
                    mt, mk = mT[i % 2], f"mT{i % 2}"
                    xs, xk = xst[i % 2], f"xs{i % 2}"
                    os_, okk = ost[i % 2], f"os{i % 2}"
                    p.dma("sp", xk, lambda e, xs=xs, i=i: e.dma_start(out=xs[:], in_=x_d[seq, i * 128:(i + 1) * 128, :]),
                          writes=[xk])
                    for g in range(2):
                        for q in range(4):
                            kc = 4 * g + q
                            p.op("pe", lambda e, kc=kc, q=q, i=i: e.transpose(
                                out=PST[:, q * 128:(q + 1) * 128], in_=mixed[:, i, kc * 128:(kc + 1) * 128],
                                identity=ident),
                                reads=["mats"], writes=["pst"])
                        if g == 0:
                            p.op("act", lambda e, g=g, mt=mt: e.copy(
                                out=mt[:, 4 * g:4 * g + 4, :], in_=PST[:, 0:512].rearrange("p (q t) -> p q t", t=128)),
                                reads=["pst"], writes=[mk])
                        else:
                            p.op("dve", lambda e, g=g, mt=mt: e.tensor_copy(
                                out=mt[:, 4 * g:4 * g + 4, :], in_=PST[:, 0:512].rearrange("p (q t) -> p q t", t=128)),
                                reads=["pst"], writes=[mk])
                    for nh in range(2):
                        pj, pk = PJ.next()
                        for kc in range(8):
                            p.op("pe", lambda e, kc=kc, pj=pj, mt=mt, nh=nh: e.matmul(
                                pj[:, :], lhsT=mt[:, kc, :], rhs=WO[:, kc, nh * 512:(nh + 1) * 512],
                                start=(kc == 0), stop=(kc == 7)),
                                reads=[mk, ("WO", kc)], writes=[pk])
                        p.op("dve", lambda e, pj=pj, os_=os_, xs=xs, nh=nh: e.tensor_tensor(
                            out=os_[:, nh * 512:(nh + 1) * 512], in0=pj[:, :], in1=xs[:, nh * 512:(nh + 1) * 512],
                            op=ALU.add),
                            reads=[pk, xk], writes=[okk])
                    p.dma("sp", "o" + okk, lambda e, os_=os_, i=i: e.dma_start(
                        out=out_d[seq, i * 128:(i + 1) * 128, :], in_=os_[:]), reads=[okk])
                p.finish()
                p.emit()
    return nc


def host_consts(inp):
    cols = np.zeros((128, 32), np.float32)
    ng = inp["norm_gain"].reshape(1024)
    cols[:, 0:8] = ng.reshape(8, 128).T
    cols[:, 8] = np.tile(inp["a_q_norm"].reshape(64), 2)
    cols[:, 9] = np.tile(inp["a_k_norm"].reshape(64), 2)
    cols[:, 10:13] = inp["b_q_latent_norm"].reshape(3, 128).T
    cols[:, 13:15] = inp["b_kv_latent_norm"].reshape(2, 128).T
    cols[:96, 15] = inp["b_q_norm"].reshape(96)
    cols[:96, 16] = inp["b_k_norm"].reshape(96)
    pidx = np.arange(128)
    cols[:, 17] = np.power(np.float32(10000.0), -(pidx % 32).astype(np.float32) * np.float32(2.0) / np.float32(64))
    cols[:, 18] = np.power(np.float32(10000.0), -(pidx % 16).astype(np.float32) * np.float32(2.0) / np.float32(32))
    m64 = pidx % 64
    cols[:, 19] = np.where(m64 < 32, 1.0, -1.0)
    cols[:, 20] = np.where(m64 < 16, 1.0, np.where(m64 < 32, -1.0, 0.0))
    mI = (m64 < 32).astype(np.float32)
    cols[:, 21] = -mI
    cols[:, 22] = 1.0 - mI
    mats = np.zeros((128, 6, 128), np.float32)
    mats[:, 0, :] = np.eye(128)
    for m in range(128):
        pm = m + 32 if m64[m] < 32 else m - 32
        mats[pm, 1, m] = 1.0
        if m64[m] < 16:
            mats[m + 16, 2, m] = 1.0
        elif m64[m] < 32:
            mats[m - 16, 2, m] = 1.0
    mats[:, 3, :] = 1.0
    mats[:, 4, :] = (pidx[:, None] // 64 == pidx[None, :] // 64).astype(np.float32)
    pos = np.arange(S, dtype=np.float32)
    tabs = np.zeros((128, 4, S), np.float32)
    angA = pos[None, :] * cols[:, 17:18]
    angI = pos[None, :] * cols[:, 18:19]
    sgnA = np.where(m64 < 32, -1.0, 1.0)[:, None]
    sgnI = np.where(m64 < 16, -1.0, np.where(m64 < 32, 1.0, 0.0))[:, None]
    tabs[:, 0] = np.cos(angA)
    tabs[:, 1] = sgnA * np.sin(angA)
    tabs[:, 2] = np.where(mI[:, None] > 0, np.cos(angI), 1.0)
    tabs[:, 3] = sgnI * np.sin(angI)
    tabs = np.ascontiguousarray(tabs.reshape(128, 4, 4, 512).transpose(2, 0, 1, 3)).astype(np.float32)
    return cols, mats, tabs


def make_in_maps(inp):
    cols, mats, tabs = host_consts(inp)
    x = np.ascontiguousarray(inp["x"], dtype=np.float32)
    maps = []
    for c in range(NCORES):
        maps.append({
            "x": np.ascontiguousarray(x[c * SEQ_PER_CORE:(c + 1) * SEQ_PER_CORE]),
            "w_in": np.ascontiguousarray(inp["w_in"][0]),
            "w_uq": np.ascontiguousarray(inp["w_uq"][0]),
            "w_ukv": np.ascontiguousarray(inp["w_ukv"][0]),
            "w_out": np.ascontiguousarray(inp["w_out"][0]),
            "cols": cols, "mats": mats, "tabs": tabs,
        })
    return maps


def kernel(**inputs):
    inp = {k: np.asarray(v) for k, v in inputs.items()}
    nc = build_nc()
    res = run_bass_kernel_spmd(nc, make_in_maps(inp), core_ids=list(range(NCORES)))
    out = np.concatenate([np.asarray(r["out"]) for r in res.results], axis=0)
    return out.astype(np.float32)
```

```python
import numpy as np
import os
VAR = os.environ.get('KVAR', '')
from contextlib import ExitStack
import concourse.bass as bass
import concourse.mybir as mybir
from concourse.bass_utils import run_bass_kernel_spmd

F32 = mybir.dt.float32
BF16 = mybir.dt.bfloat16
I32 = mybir.dt.int32
ALU = mybir.AluOpType
AF = mybir.ActivationFunctionType
AX = mybir.AxisListType

S = 2048
DM = 1024
NCORES = 8
SEQ_PER_CORE = 2
EPS = 1e-6
NIT = 24
TOPK = 256
NEG = -30000.0
PI = float(np.pi)

ENGS = ("pe", "act", "dve", "pool", "sp")


class Prog:
    uid = 0

    def __init__(self, nc, st):
        Prog.uid += 1
        self.nc = nc
        self.st = st
        self.sems = {("eng", e): st.enter_context(nc.semaphore(f"se_{e}_{Prog.uid}")) for e in ENGS}
        self.cnt = {e: 0 for e in ENGS}
        self.ops = {e: [] for e in ENGS}
        self.waited = {e: {} for e in ENGS}
        self.lw = {}
        self.rd = {}
        self.chcnt = {}

    def _deps(self, e, reads, writes):
        deps = {}

        def add(s, v):
            if deps.get(s, 0) < v:
                deps[s] = v
        for k in reads:
            t = self.lw.get(k)
            if t:
                add(*t)
        for k in writes:
            t = self.lw.get(k)
            if t:
                add(*t)
            for s, v in self.rd.get(k, {}).items():
                add(s, v)
        out = []
        for s, v in deps.items():
            if e == "pe" and s == ("eng", "pe"):
                continue
            if self.waited[e].get(s, 0) >= v:
                continue
            self.waited[e][s] = v
            out.append((s, v))
        return out

    def _commit(self, tok, reads, writes):
        for k in reads:
            d = self.rd.setdefault(k, {})
            if d.get(tok[0], 0) < tok[1]:
                d[tok[0]] = tok[1]
        for k in writes:
            self.lw[k] = tok
            self.rd[k] = {}

    def op(self, e, fn, reads=(), writes=()):
        waits = self._deps(e, reads, writes)
        self.cnt[e] += 1
        tok = (("eng", e), self.cnt[e])
        self.ops[e].append((waits, fn, tok[0], 1))
        self._commit(tok, reads, writes)

    def dma(self, e, chan, fn, reads=(), writes=()):
        waits = self._deps(e, reads, writes)
        key = ("ch", chan)
        if key not in self.sems:
            self.sems[key] = self.st.enter_context(self.nc.semaphore(f"sc_{chan}_{Prog.uid}"))
            self.chcnt[key] = 0
        self.chcnt[key] += 16
        tok = (key, self.chcnt[key])
        self.ops[e].append((waits, fn, key, 16))
        self._commit(tok, reads, writes)

    def finish(self):
        waits = []
        for key, v in self.chcnt.items():
            if self.waited["sp"].get(key, 0) < v:
                waits.append((key, v))
        self.ops["sp"].append((waits, None, None, 0))

    def emit(self):
        nc = self.nc
        engobj = {"pe": nc.tensor, "act": nc.scalar, "dve": nc.vector, "pool": nc.gpsimd, "sp": nc.sync}
        with nc.Block() as block:
            def run(e):
                eng = engobj[e]
                for waits, fn, inc, amt in self.ops[e]:
                    for s, v in waits:
                        eng.wait_ge(self.sems[s], v)
                    if fn is not None:
                        ins = fn(eng)
                        ins.then_inc(self.sems[inc], amt)

            @block.tensor
            def _(t):
                run("pe")

            @block.scalar
            def _(t):
                run("act")

            @block.vector
            def _(t):
                run("dve")

            @block.gpsimd
            def _(t):
                run("pool")

            @block.sync
            def _(t):
                run("sp")


def run_pipeline(jobs):
    n = len(jobs)
    S = max(len(j) for j in jobs)
    states = [dict() for _ in jobs]
    for step in range(n + S - 1):
        for st_i in range(S):
            c = step - st_i
            if 0 <= c < n and st_i < len(jobs[c]):
                jobs[c][st_i](states[c])


class Rot:
    def __init__(self, items):
        self.items = list(items)
        self.i = 0

    def next(self):
        it = self.items[self.i % len(self.items)]
        self.i += 1
        return it


def build_nc(stage=99, dbg=None, bstage=99, skipA=False, nseq=SEQ_PER_CORE):
    dbg = dbg if dbg is not None else {}
    nc = bass.Bass("TRN2", target_bir_lowering=False)
    x_d = nc.dram_tensor("x", [SEQ_PER_CORE, S, DM], F32, kind="ExternalInput").ap()
    win_d = nc.dram_tensor("w_in", [DM, 3816], F32, kind="ExternalInput").ap()
    wuq_d = nc.dram_tensor("w_uq", [384, 768], F32, kind="ExternalInput").ap()
    wukv_d = nc.dram_tensor("w_ukv", [256, 1024], F32, kind="ExternalInput").ap()
    wout_d = nc.dram_tensor("w_out", [DM, DM], F32, kind="ExternalInput").ap()
    cols_d = nc.dram_tensor("cols", [128, 32], F32, kind="ExternalInput").ap()
    mats_d = nc.dram_tensor("mats", [128, 6, 128], F32, kind="ExternalInput").ap()
    tabs_d = nc.dram_tensor("tabs", [4, 128, 4, 512], F32, kind="ExternalInput").ap()
    out_d = nc.dram_tensor("out", [SEQ_PER_CORE, S, DM], F32, kind="ExternalOutput").ap()
    dbg_d = {}
    for name, shape in dbg.items():
        dbg_d[name] = nc.dram_tensor("dbg_" + name, list(shape), F32, kind="ExternalOutput").ap()

    top = ExitStack()
    with top:
        def sb(name, shape, dt, st=top):
            return st.enter_context(nc.sbuf_tensor("sb_" + name, list(shape), dt))

        cols = sb("cols", [128, 32], F32)
        cols2 = sb("cols2", [128, 8], F32)
        matsf = sb("matsf", [128, 6, 128], F32)
        mats = sb("matsb", [128, 6, 128], BF16)
        ident = mats[:, 0, :]
        permA = mats[:, 1, :]
        permI = mats[:, 2, :]
        ones = mats[:, 3, :]
        bdA = mats[:, 4, :]
        zeros = mats[:, 5, :]
        fin260 = mats[:].rearrange("p a b -> p (a b)")[:, 0:260]
        mixed = sb("mixed", [128, 16, 1024], BF16)
        PS = [top.enter_context(nc.psum_tensor(f"ps{i}", [128, 512], F32)) for i in range(7)]
        PST = top.enter_context(nc.psum_tensor("pst", [128, 1024], BF16))

        C_NG, C_AQ, C_AK, C_CQ, C_CKV, C_BQ, C_BK = 0, 8, 9, 10, 13, 15, 16
        C_FA, C_FI, C_NSA, C_NSI, C_NCMI, C_OMI = 17, 18, 19, 20, 21, 22

        st0 = ExitStack()
        with st0:
            p = Prog(nc, st0)
            p.dma("sp", "c0", lambda e: e.dma_start(out=cols[:], in_=cols_d[:, :]), writes=["cols"])
            p.dma("sp", "c1", lambda e: e.dma_start(out=matsf[:], in_=mats_d[:, :, :]), writes=["matsf"])
            p.op("dve", lambda e: e.tensor_copy(out=mats[:], in_=matsf[:]), reads=["matsf"], writes=["mats"])
            p.op("dve", lambda e: e.tensor_scalar(out=cols2[:, 0:1], in0=cols[:, C_AQ:C_AQ + 1], scalar1=0.125,
                                                  scalar2=None, op0=ALU.mult), reads=["cols"], writes=["cols2"])
            p.op("dve", lambda e: e.tensor_scalar(out=cols2[:, 1:2], in0=cols[:, C_BQ:C_BQ + 1],
                                                  scalar1=float(96.0 ** -0.5), scalar2=None, op0=ALU.mult),
                 reads=["cols"], writes=["cols2"])
            p.op("dve", lambda e: e.memset(cols2[:, 7:8], 1024.0 * EPS), writes=["cols2"])
            p.op("dve", lambda e: e.memset(cols2[:, 6:7], EPS), writes=["cols2"])
            p.finish()
            p.emit()

        def colap(i, lo=0, hi=128):
            return cols[lo:hi, i:i + 1]

        def common_p0(p, st, seq, b, xnT, xst_rot, tag):
            for j in range(4):
                i = 4 * b + j
                xs, xk, ch = xst_rot.next()
                p.dma("sp", ch, lambda e, xs=xs, i=i: e.dma_start(out=xs[:], in_=x_d[seq, i * 128:(i + 1) * 128, :]),
                      writes=[xk])
                p.op("act", lambda e, xs=xs: e.activation(out=p.junk[:, 0:1024], in_=xs[:], func=AF.Square,
                                                          accum_out=p.ss[:, 0:1]),
                     reads=[xk], writes=["junk", "ss"])
                p.op("act", lambda e: e.activation(out=p.rs[:, 1:2], in_=p.ss[:, 0:1], func=AF.Ln,
                                                   bias=cols2[:, 7:8], scale=1.0),
                     reads=["ss", "cols2"], writes=["rs1"])
                p.op("act", lambda e: e.activation(out=p.rs[:, 0:1], in_=p.rs[:, 1:2], func=AF.Exp, scale=-0.5),
                     reads=["rs1"], writes=["rs"])
                p.op("dve", lambda e, xs=xs: e.tensor_scalar(out=p.xb[:], in0=xs[:], scalar1=p.rs[:, 0:1],
                                                             scalar2=32.0, op0=ALU.mult, op1=ALU.mult),
                     reads=[xk, "rs"], writes=["xb"])
                for g in range(2):
                    for q in range(4):
                        kc = 4 * g + q
                        p.op("pe", lambda e, kc=kc, q=q: e.transpose(out=PST[:, q * 128:(q + 1) * 128],
                                                                      in_=p.xb[:, kc * 128:(kc + 1) * 128],
                                                                      identity=ident),
                             reads=["xb", "mats"], writes=["pst"])
                    eng = "act" if g == 0 else "dve"
                    if eng == "act":
                        p.op("act", lambda e, g=g, j=j: e.copy(
                            out=xnT[:, 4 * g:4 * g + 4, j * 128:(j + 1) * 128],
                            in_=PST[:, 0:512].rearrange("p (q t) -> p q t", t=128)),
                            reads=["pst"], writes=["xnT"])
                    else:
                        p.op("dve", lambda e, g=g, j=j: e.tensor_copy(
                            out=xnT[:, 4 * g:4 * g + 4, j * 128:(j + 1) * 128],
                            in_=PST[:, 0:512].rearrange("p (q t) -> p q t", t=128)),
                            reads=["pst"], writes=["xnT"])

        def make_tables(p, b, tabs, ntab=4):
            p.dma("sp", "tab", lambda e: e.dma_start(out=tabs[:, 0:ntab, :], in_=tabs_d[b, :, 0:ntab, :]),
                  writes=[("tab", ti) for ti in range(4)])

        def load_w(p, dst, src, nk, chan, key):
            for kc in range(nk):
                p.dma("pool", chan, lambda e, kc=kc: e.dma_start(
                    out=dst[:, kc, :], in_=src[kc * 128:(kc + 1) * 128, :], max_dma_last_dim=4096),
                    writes=[(key, kc)])
            tot = (("ch", chan), p.chcnt[("ch", chan)])
            for kc in range(nk):
                p.lw[(key, kc)] = tot

        def scale_rows(p, dst, nk, colbase, key, eng="dve"):
            for kc in range(nk):
                p.op(eng, lambda e, kc=kc: e.tensor_scalar(out=dst[:, kc, :], in0=dst[:, kc, :],
                                                           scalar1=colap(colbase + kc), scalar2=None, op0=ALU.mult),
                     reads=[(key, kc), "cols"], writes=[(key, kc)])

        def attention_units(p, b, nheads, qk_fn, exp_fn, post_fn, v_ap_fn, fin_fn, PTs, Lps, Ops):
            for _ in attention_units_gen(p, b, nheads, qk_fn, exp_fn, post_fn, v_ap_fn, fin_fn, PTs, Lps, Ops):
                pass

        def attention_units_gen(p, b, nheads, qk_fn, exp_fn, post_fn, v_ap_fn, fin_fn, PTs, Lps, Ops):
            units = [(h, k) for h in range(nheads) for k in range(4 * b + 4)]
            LOOK = min(2, max(1, len(Lps.items) - 1))
            state = {}

            def issue_qk(u):
                h, k = units[u]
                kk = k - 4 * b
                col0 = 128 * max(kk, 0)
                lp, lk = Lps.next()
                state[u] = (lp, lk, col0)
                qk_fn(h, k, col0, lp, lk)

            def issue_rest(u):
                h, k = units[u]
                lp, lk, col0 = state.pop(u)
                kk = k - 4 * b
                pt, pk = PTs.next()
                N = 512 - col0
                exp_fn(h, k, col0, lp, lk, pt, pk, N)
                if post_fn is not None and kk >= 0:
                    post_fn(h, k, col0, pt, pk)
                if k == 0:
                    state[("O", h)] = Ops.next()
                    O0, ok0 = state[("O", h)]
                    p.op("pe", lambda e, O0=O0: e.matmul(O0[:, 0:260], lhsT=zeros, rhs=fin260, start=True, stop=False),
                         reads=["mats"], writes=[ok0])
                O, ok = state[("O", h)]
                for jj in range(max(kk, 0), 4):
                    c0 = jj * 128 - col0
                    p.op("pe", lambda e, O=O, jj=jj, pt=pt, c0=c0, h=h, k=k: e.matmul(
                        O[:, jj * 65:jj * 65 + 65], lhsT=pt[:, c0:c0 + 128], rhs=v_ap_fn(k, h),
                        start=False, stop=(k == 4 * b + 3 and jj == 3)),
                        reads=[pk, ("V", k // 4)], writes=[ok])
                if k == 4 * b + 3:
                    fin_fn(h, O, ok)

            n = len(units)
            for u in range(min(LOOK, n)):
                issue_qk(u)
            for u in range(n):
                if u + LOOK < n:
                    issue_qk(u + LOOK)
                issue_rest(u)
                yield 1.3

        for seq in range(nseq):
            stA = ExitStack()
            if not skipA:
              with stA:
                  p = Prog(nc, stA)
                  A = lambda name, shape, dt: sb(f"A{seq}_{name}", shape, dt, stA)
                  WA = A("WA", [128, 8, 2632], BF16)
                  WKI2 = A("WKI2", [128, 8, 128], BF16)
                  SC = [A(f"scores{i}", [128, 2048], F32) for i in range(2)]
                  xnT = SC[0][:].bitcast(BF16).rearrange("p (k t) -> p k t", t=512)
                  sc1b = SC[1][:].bitcast(BF16)
                  xst_rot = Rot([(SC[1][:, 0:1024], "xs0", "xs0")])
                  p.junk = sc1b[:, 3072:4096]
                  p.ss = A("ss", [128, 1], F32)
                  p.rs = A("rs", [128, 2], F32)
                  p.xb = sc1b[:, 2048:3072]
                  tabs = A("tabs", [128, 4, 512], F32)
                  KaT = A("KaT", [128, 4, 2048], BF16)
                  KiT = A("KiT", [128, 2048], BF16)
                  Va = A("Va", [128, 16, 8 * 65], BF16)
                  QaTb = [A(f"QaT{i}", [128, 4, 512], BF16) for i in range(2)]
                  QiT = A("QiT", [128, 4, 512], BF16)
                  sgAb = [A(f"sgA{i}", [128, 4, 512], BF16) for i in range(2)]
                  wi = A("wi", [128, 4, 8], F32)
                  mbuf = [A(f"mb{i}", [128, 4, 2048], BF16) for i in range(2)]
                  PTt = [A(f"PT{i}", [128, 512], BF16) for i in range(3)]
                  Rt = [A(f"R{i}", [128, 512], BF16) for i in range(4)]
                  Ra = [A(f"Ra{i}", [128, 512], BF16) for i in range(2)]
                  Dg = A("Dg", [128, 4, 128], BF16)
                  sqb = [A(f"sqb{i}", [128, 512], BF16) for i in range(2)]
                  yb = [A(f"yb{i}", [128, 512], BF16) for i in range(3)]
                  bis = A("bis", [128, 16], F32)
                  wab = A("wab", [128, 4, 8], F32)
                  wsg = A("wsg", [128, 4, 8], F32)
                  rinv = A("rinv", [128, 4], F32)
                  fdum = A("fdum", [128, 2], F32)
                  PSTf = PST[:].bitcast(F32)

                  def fence(bi):
                      keys = ["sc0", "sc1", "xnT", "xs0", "xb", "junk", "ta0", "ta1", "tb0", "tb1", "rst0", "rst1"]
                      keys += [("mb", bi, jq) for jq in range(4)]
                      p.op("dve", lambda e: e.memset(fdum[:, 0:1], 0.0), writes=keys)

                  load_w(p, WA, win_d[:, 0:2632], 8, "wa", "WA")
                  scale_rows(p, WA, 8, C_NG, "WA")
                  for kc in range(8):
                      p.op("pool", lambda e, kc=kc: e.tensor_copy(
                          out=WKI2[:, kc, :].rearrange("p (r c) -> p r c", r=2),
                          in_=WA[:, kc, 2560:2624].unsqueeze(1).to_broadcast([128, 2, 64])),
                          reads=[("WA", kc)], writes=[("WKI2", kc)])
                  p.op("pool", lambda e: e.memset(Va[:].rearrange("p i (h d) -> p (i h) d", d=65)[:, :, 64:65], 1.0),
                       writes=[("V", q) for q in range(4)])

                  Lps = Rot([(PS[0], "ps0"), (PS[1], "ps1")])

                  def blockA_p1(b):
                      t0 = 512 * b
                      bi = b % 2
                      QaT, sgA = QaTb[bi], sgAb[bi]
                      f32v = mbuf[bi][:].rearrange("p a c -> p (a c)").bitcast(F32)
                      ta = [f32v[:, 0:512], f32v[:, 512:1024]]
                      tb = [f32v[:, 1024:1536], f32v[:, 1536:2048]]
                      rst = [f32v[:, 2048:2560], f32v[:, 2560:3072]]
                      fence(bi)
                      common_p0(p, stA, seq, b, xnT, xst_rot, "A")
                      make_tables(p, b, tabs)
                      PJ = Rot([(PS[0], "ps0"), (PS[1], "ps1"), (PS[6], "ps6")])
                      YB = Rot([(yb[0], "yb0"), (yb[1], "yb1"), (yb[2], "yb2")])
                      SQ = Rot([(sqb[0], "sqb0"), (sqb[1], "sqb1")])
                      MS = Rot([(PS[2], "ps2"), (PS[4], "ps4")])
                      PR = Rot([(PS[3], "ps3"), (PS[5], "ps5")])
                      RS = Rot([(rst[0], "rst0"), (rst[1], "rst1")])
                      TA = Rot([(ta[0], "ta0"), (ta[1], "ta1")])
                      TB = Rot([(tb[0], "tb0"), (tb[1], "tb1")])

                      def proj_fm(col, wt=WA, wkey="WA", ncol=128):
                          pj, pk = PJ.next()
                          for kc in range(8):
                              p.op("pe", lambda e, kc=kc, pj=pj: e.matmul(
                                  pj[0:ncol, :], lhsT=wt[:, kc, col:col + ncol], rhs=xnT[:, kc, :],
                                  start=(kc == 0), stop=(kc == 7)),
                                  reads=[(wkey, kc), "xnT"], writes=[pk])
                          return pj, pk

                      def rope_finish(yt, yk, perm, tcos, tsin, dst_fn, dkey):
                          pr_, prk = PR.next()
                          ta_, tak = TA.next()
                          tb_, tbk = TB.next()
                          p.op("pe", lambda e, yt=yt, pr_=pr_: e.matmul(pr_[:, :], lhsT=perm, rhs=yt[:], start=True, stop=True),
                               reads=[yk, "mats"], writes=[prk])
                          p.op("dve", lambda e, yt=yt, ta_=ta_: e.tensor_tensor(out=ta_[:], in0=yt[:], in1=tabs[:, tcos, :],
                                                                                op=ALU.mult),
                               reads=[yk, ("tab", tcos)], writes=[tak])
                          p.op("dve", lambda e, pr_=pr_, tb_=tb_: e.tensor_tensor(out=tb_[:], in0=pr_[:, :], in1=tabs[:, tsin, :],
                                                                                  op=ALU.mult),
                               reads=[prk, ("tab", tsin)], writes=[tbk])
                          p.op("pool", lambda e, ta_=ta_, tb_=tb_: e.tensor_tensor(out=dst_fn(), in0=ta_[:], in1=tb_[:], op=ALU.add),
                               reads=[tak, tbk], writes=[dkey])

                      jobs = []
                      for kind in ("q", "k"):
                          for c in range(4):
                              col = (0 if kind == "q" else 512) + c * 128
                              gcol = cols2[:, 0:1] if kind == "q" else colap(C_AK)
                              if kind == "q":
                                  dst_fn, dkey = (lambda c=c: QaT[:, c, :]), ("QaT", bi, c)
                              else:
                                  dst_fn, dkey = (lambda c=c, t0=t0: KaT[:, c, t0:t0 + 512]), ("KaT", b)

                              def s0(st, col=col):
                                  st["pj"], st["pk"] = proj_fm(col)

                              def s1(st):
                                  pj, pk = st["pj"], st["pk"]
                                  sq_, sqk = SQ.next()
                                  ms_, msk = MS.next()
                                  st["ms"], st["msk"] = ms_, msk
                                  p.op("act", lambda e, pj=pj, sq_=sq_: e.activation(out=sq_[:], in_=pj[:, :],
                                                                                     func=AF.Square),
                                       reads=[pk], writes=[sqk])
                                  p.op("pe", lambda e, sq_=sq_, ms_=ms_: e.matmul(ms_[:, :], lhsT=bdA, rhs=sq_[:],
                                                                                  start=True, stop=True),
                                       reads=[sqk, "mats"], writes=[msk])

                              def s2(st, gcol=gcol, dst_fn=dst_fn, dkey=dkey):
                                  pj, pk, ms_, msk = st["pj"], st["pk"], st["ms"], st["msk"]
                                  rs_, rsk = RS.next()
                                  p.op("act", lambda e, ms_=ms_, rs_=rs_: e.activation(
                                      out=rs_[:], in_=ms_[:, :], func=AF.Ln, bias=cols2[:, 6:7], scale=1.0 / 64.0),
                                      reads=[msk, "cols2"], writes=[rsk])
                                  p.op("act", lambda e, rs_=rs_: e.activation(out=rs_[:], in_=rs_[:], func=AF.Exp,
                                                                              scale=-0.5),
                                       reads=[rsk], writes=[rsk])
                                  yt, yk = YB.next()
                                  p.op("dve", lambda e, pj=pj, yt=yt, gcol=gcol, rs_=rs_: e.scalar_tensor_tensor(
                                      out=yt[:], in0=pj[:, :], scalar=gcol, in1=rs_[:], op0=ALU.mult, op1=ALU.mult),
                                      reads=[pk, rsk, "cols", "cols2"], writes=[yk])
                                  rope_finish(yt, yk, permA, 0, 1, dst_fn, dkey)
                              jobs.append([s0, s1, s2])
                      for c in range(5):
                          if c < 4:
                              dst_fn, dkey = (lambda c=c: QiT[:, c, :]), ("QiT", c)
                          else:
                              dst_fn, dkey = (lambda t0=t0: KiT[:, t0:t0 + 512]), ("KiT", b)

                          def s0(st, c=c):
                              if c < 4:
                                  st["pj"], st["pk"] = proj_fm(2048 + c * 128)
                              else:
                                  st["pj"], st["pk"] = proj_fm(0, wt=WKI2, wkey="WKI2")

                          def s1(st):
                              pj, pk = st["pj"], st["pk"]
                              yt, yk = YB.next()
                              st["yt"], st["yk"] = yt, yk
                              p.op("act", lambda e, pj=pj, yt=yt: e.copy(out=yt[:], in_=pj[:, :]),
                                   reads=[pk], writes=[yk])

                          def s2(st, dst_fn=dst_fn, dkey=dkey):
                              rope_finish(st["yt"], st["yk"], permI, 2, 3, dst_fn, dkey)
                          jobs.append([s0, s1, s2])
                      run_pipeline(jobs)
                      for j in range(4):
                          i = 4 * b + j
                          for what in ("v", "g", "w"):
                              pj, pk = PJ.next()
                              c0, n = {"v": (1024, 512), "g": (1536, 512), "w": (2624, 8)}[what]
                              for kc in range(8):
                                  p.op("pe", lambda e, kc=kc, pj=pj, j=j, c0=c0, n=n: e.matmul(
                                      pj[:, 0:n], lhsT=xnT[:, kc, j * 128:(j + 1) * 128], rhs=WA[:, kc, c0:c0 + n],
                                      start=(kc == 0), stop=(kc == 7)),
                                      reads=[("WA", kc), "xnT"], writes=[pk])
                              if what == "v":
                                  p.op("dve", lambda e, pj=pj, i=i: e.tensor_copy(
                                      out=Va[:, i, :].rearrange("p (h d) -> p h d", d=65)[:, :, 0:64],
                                      in_=pj[:, :].rearrange("p (h d) -> p h d", d=64)),
                                      reads=[pk], writes=[("V", b)])
                              elif what == "g":
                                  p.op("act", lambda e, pj=pj, j=j: e.activation(out=sgA[:, j, :], in_=pj[:, :],
                                                                                 func=AF.Silu),
                                       reads=[pk], writes=[("sg", bi, j)])
                              else:
                                  p.op("dve", lambda e, pj=pj, j=j: e.tensor_copy(out=wi[:, j, :], in_=pj[:, 0:8]),
                                       reads=[pk], writes=[("wi", j)])
                                  p.op("dve", lambda e, j=j: e.tensor_scalar(out=wab[:, j, :], in0=wi[:, j, :], scalar1=-1.0,
                                                                             scalar2=None, op0=ALU.mult),
                                       reads=[("wi", j)], writes=[("wab", j)])
                                  p.op("dve", lambda e, j=j: e.tensor_tensor(out=wab[:, j, :], in0=wab[:, j, :],
                                                                             in1=wi[:, j, :], op=ALU.max),
                                       reads=[("wi", j), ("wab", j)], writes=[("wab", j)])
                                  p.op("dve", lambda e, j=j: e.tensor_scalar(out=wsg[:, j, :], in0=wi[:, j, :], scalar1=0.0,
                                                                             scalar2=2.0, op0=ALU.is_ge, op1=ALU.mult),
                                       reads=[("wi", j)], writes=[("wsg", j)])
                                  p.op("dve", lambda e, j=j: e.tensor_scalar(out=wsg[:, j, :], in0=wsg[:, j, :], scalar1=-1.0,
                                                                             scalar2=None, op0=ALU.add),
                                       reads=[("wsg", j)], writes=[("wsg", j)])

                      fence(bi)

                  def blockA_idx(b):
                      bi = b % 2
                      mb = mbuf[bi]
                      XP = Rot([(PS[i], f"ps{i}") for i in range(3, 7)])
                      ACC = Rot([(PSTf, "pst")])
                      RR = Rot([(Rt[i], f"R{i}") for i in range(4)])
                      RA = Rot([(Ra[i], f"Ra{i}") for i in range(2)])
                      for pr in range(2):
                          tiles = []
                          for q in range(2):
                              j = 2 * pr + q
                              i = 4 * b + j
                              L2 = 128 * (i + 1)
                              L1 = L2 - 64
                              sc, sck = SC[q], f"sc{q}"
                              tiles.append((j, i, L1, L2, sc, sck))
                              for dq, hh in enumerate((2, 3, 6, 7)):
                                  p.op("dve", lambda e, dq=dq, hh=hh, j=j: e.tensor_scalar(
                                      out=Dg[:, dq, :], in0=ident, scalar1=wsg[:, j, hh:hh + 1], scalar2=None,
                                      op0=ALU.mult),
                                      reads=["mats", ("wsg", j)], writes=[("Dg", dq)])
                              for sbk in range(b + 1):
                                  wd = 512 if sbk < b else 128 * (j + 1)
                                  acc, acck = ACC.next()
                                  pend = []

                                  def issue_x(h, j=j, sbk=sbk, wd=wd):
                                      c, base = h // 2, (h % 2) * 64
                                      xp, xk = XP.next()
                                      p.op("pe", lambda e, xp=xp, c=c, base=base: e.matmul(
                                          xp[:, 0:wd], lhsT=QiT[base:base + 64, c, j * 128:(j + 1) * 128],
                                          rhs=KiT[base:base + 64, sbk * 512:sbk * 512 + wd], start=True, stop=True),
                                          reads=[("QiT", c), ("KiT", sbk)], writes=[xk])
                                      return xp, xk
                                  for h0 in range(4):
                                      pend.append(issue_x(h0))
                                  nacc = 0
                                  for m in range(4):
                                      xpe, xke = pend.pop(0)
                                      xpo, xko = pend.pop(0)
                                      he, ho = 2 * m, 2 * m + 1
                                      if m % 2 == 0:
                                          r, rk = RR.next()
                                          r2, rk2 = RR.next()
                                          for (xp_, xk_, r_, rk_, h_) in ((xpe, xke, r, rk, he), (xpo, xko, r2, rk2, ho)):
                                              p.op("dve", lambda e, xp=xp_, r=r_, h=h_, j=j, wd=wd: e.tensor_scalar(
                                                  out=r[:, 0:wd], in0=xp[:, 0:wd], scalar1=0.0, scalar2=wi[:, j, h:h + 1],
                                                  op0=ALU.max, op1=ALU.mult),
                                                  reads=[xk_, ("wi", j)], writes=[rk_])
                                          if 2 * m + 4 < 8:
                                              pend.append(issue_x(2 * m + 4))
                                              pend.append(issue_x(2 * m + 5))
                                          p.op("pool", lambda e, r=r, r2=r2, wd=wd: e.tensor_tensor(
                                              out=r[:, 0:wd], in0=r[:, 0:wd], in1=r2[:, 0:wd], op=ALU.add),
                                              reads=[rk, rk2], writes=[rk])
                                          p.op("pe", lambda e, acc=acc, r=r, nacc=nacc, wd=wd: e.matmul(
                                              acc[:, 0:wd], lhsT=ident, rhs=r[:, 0:wd], start=(nacc == 0), stop=False),
                                              reads=[rk, "mats"], writes=[acck])
                                          nacc += 1
                                      else:
                                          r, rk = RA.next()
                                          r2, rk2 = RA.next()
                                          for (xp_, xk_, r_, rk_, h_) in ((xpe, xke, r, rk, he), (xpo, xko, r2, rk2, ho)):
                                              p.op("act", lambda e, xp=xp_, r=r_, h=h_, j=j, wd=wd: e.activation(
                                                  out=r[:, 0:wd], in_=xp[:, 0:wd], func=AF.Relu, scale=wab[:, j, h:h + 1]),
                                                  reads=[xk_, ("wab", j)], writes=[rk_])
                                          if 2 * m + 4 < 8:
                                              pend.append(issue_x(2 * m + 4))
                                              pend.append(issue_x(2 * m + 5))
                                          for (r_, rk_, h_) in ((r, rk, he), (r2, rk2, ho)):
                                              dq = (h_ // 4) * 2 + (h_ % 2)
                                              p.op("pe", lambda e, acc=acc, r=r_, nacc=nacc, wd=wd, dq=dq, m=m, h=h_: e.matmul(
                                                  acc[:, 0:wd], lhsT=Dg[:, dq, :], rhs=r[:, 0:wd], start=(nacc == 0),
                                                  stop=(h == 7)),
                                                  reads=[rk_, ("Dg", dq)], writes=[acck])
                                              nacc += 1
                                  p.op("act", lambda e, acc=acc, sbk=sbk, wd=wd, sc=sc: e.copy(
                                      out=sc[:, sbk * 512:sbk * 512 + wd], in_=acc[:, 0:wd]),
                                      reads=[acck], writes=[sck])
                                  yield ("relu", 7.0 * wd / 512.0)
                              p.op("pool", lambda e, L1=L1, L2=L2, sc=sc: e.memset(sc[0:64, L1:L2], -1e30),
                                   reads=[], writes=[sck])
                              if "scores" in dbg_d and seq == 0:
                                  p.dma("sp", "dbg", lambda e, i=i, L2=L2, sc=sc: e.dma_start(
                                      out=dbg_d["scores"][i, :, 0:L2], in_=sc[:, 0:L2]), reads=[sck])
                          LO2, HI2, RNG2, MID2, CNT2, GE2, T2 = [bis[:, 2 * q:2 * q + 2] for q in range(7)]
                          SBc = bis[:, 14:15]
                          if tiles[0][1] < 2:
                              p.op("dve", lambda e: e.memset(LO2, -5e29), writes=["lo0", "lo1"])
                          else:
                              for q, (j, i, L1, L2, sc, sck) in enumerate(tiles):
                                  p.op("dve", lambda e, L1=L1, sc=sc, q=q: e.tensor_reduce(
                                      out=LO2[:, q:q + 1], in_=sc[:, 0:L1], axis=AX.X, op=ALU.min),
                                      reads=[sck], writes=[f"lo{q}"])
                                  p.op("dve", lambda e, L2=L2, sc=sc, q=q: e.tensor_reduce(
                                      out=HI2[:, q:q + 1], in_=sc[:, 0:L2], axis=AX.X, op=ALU.max),
                                      reads=[sck], writes=["hi"])
                              p.op("dve", lambda e: e.tensor_tensor(out=RNG2, in0=HI2, in1=LO2, op=ALU.subtract),
                                   reads=["hi", "lo0", "lo1"], writes=["rng"])
                              (jA, iA, L1A, L2A, scA, sckA), (jB, iB, L1B, L2B, scB, sckB) = tiles
                              for it in range(1, NIT + 1):
                                  cst = float(2.0 ** (-it))
                                  p.op("dve", lambda e, cst=cst: e.scalar_tensor_tensor(
                                      out=MID2, in0=RNG2, scalar=cst, in1=LO2, op0=ALU.mult, op1=ALU.add),
                                      reads=["rng", "lo0", "lo1"], writes=["mid"])
                                  p.op("act", lambda e, jB=jB, L2B=L2B, scB=scB: e.activation(
                                      out=mb[:, jB, 0:L2B], in_=scB[:, 0:L2B], func=AF.Sign, scale=-1.0,
                                      bias=MID2[:, 1:2], accum_out=SBc),
                                      reads=[sckB, "mid"], writes=[("mb", bi, jB), "sb"])
                                  p.op("dve", lambda e, jA=jA, L2A=L2A, scA=scA: e.tensor_scalar(
                                      out=mb[:, jA, 0:L2A], in0=scA[:, 0:L2A], scalar1=MID2[:, 0:1], scalar2=0.0,
                                      op0=ALU.is_ge, op1=ALU.add, accum_out=CNT2[:, 0:1]),
                                      reads=[sckA, "mid"], writes=[("mb", bi, jA), "cnt0"])
                                  p.op("dve", lambda e, cst=cst: e.tensor_scalar(
                                      out=GE2[:, 0:1], in0=CNT2[:, 0:1], scalar1=float(TOPK) - 0.75, scalar2=cst,
                                      op0=ALU.is_ge, op1=ALU.mult),
                                      reads=["cnt0"], writes=["ge0"])
                                  p.op("dve", lambda e, cst=cst, L2B=L2B: e.tensor_scalar(
                                      out=GE2[:, 1:2], in0=SBc, scalar1=float(L2B - 2 * TOPK) + 1.5, scalar2=cst,
                                      op0=ALU.is_le, op1=ALU.mult),
                                      reads=["sb"], writes=["ge1"])
                                  p.op("dve", lambda e: e.scalar_tensor_tensor(
                                      out=LO2[:, 0:1], in0=RNG2[:, 0:1], scalar=GE2[:, 0:1], in1=LO2[:, 0:1],
                                      op0=ALU.mult, op1=ALU.add),
                                      reads=["rng", "ge0", "lo0"], writes=["lo0"])
                                  p.op("dve", lambda e: e.scalar_tensor_tensor(
                                      out=LO2[:, 1:2], in0=RNG2[:, 1:2], scalar=GE2[:, 1:2], in1=LO2[:, 1:2],
                                      op0=ALU.mult, op1=ALU.add),
                                      reads=["rng", "ge1", "lo1"], writes=["lo1"])
                                  yield ("bis", 1.2 + L2B / 800.0)
                          for q, (j, i, L1, L2, sc, sck) in enumerate(tiles):
                              p.op("dve", lambda e, j=j, L2=L2, sc=sc, q=q: e.tensor_scalar(
                                  out=mb[:, j, 0:L2], in0=sc[:, 0:L2], scalar1=LO2[:, q:q + 1], scalar2=NEG,
                                  op0=ALU.is_lt, op1=ALU.mult),
                                  reads=[sck, f"lo{q}"], writes=[("mb", bi, j)])
                              if "thr" in dbg_d and seq == 0:
                                  p.dma("sp", "dbg", lambda e, i=i, q=q: e.dma_start(
                                      out=dbg_d["thr"][i, :, 0:1], in_=LO2[:, q:q + 1]), reads=[f"lo{q}"])

                          yield ("end", 0.5)

                  def blockA_att(b):
                      bi = b % 2
                      mb, QaT, sgA = mbuf[bi], QaTb[bi], sgAb[bi]
                      PTs = Rot([(PTt[i], f"PT{i}") for i in range(3)])
                      Ops = Rot([(PS[2], "ps2")])

                      def qk_A(h, k, col0, lp, lk):
                          c, base = h // 2, (h % 2) * 64
                          kk = k - 4 * b
                          N = 512 - col0
                          for jj in range(max(kk, 0), 4):
                              c0 = jj * 128 - col0
                              p.op("pe", lambda e, lp=lp, jj=jj, k=k, c0=c0, kk=kk: e.matmul(
                                  lp[:, c0:c0 + 128], lhsT=mb[:, jj, k * 128:(k + 1) * 128], rhs=ident,
                                  start=(jj == max(kk, 0)), stop=False),
                                  reads=[("mb", bi, jj), "mats"], writes=[lk])
                          p.op("pe", lambda e, lp=lp, k=k, c=c, base=base, col0=col0, N=N: e.matmul(
                              lp[:, 0:N], lhsT=KaT[base:base + 64, c, k * 128:(k + 1) * 128],
                              rhs=QaT[base:base + 64, c, col0:512], start=False, stop=True),
                              reads=[("KaT", k // 4), ("QaT", bi, c)], writes=[lk])

                      def exp_A(h, k, col0, lp, lk, pt, pk, N):
                          p.op("act", lambda e, lp=lp, pt=pt, N=N: e.activation(out=pt[:, 0:N], in_=lp[:, 0:N],
                                                                                 func=AF.Exp),
                               reads=[lk], writes=[pk])

                      def fin_A(h, O, ok):
                          Ov = O[:, 0:260].rearrange("p (j d) -> p j d", d=65)
                          p.op("dve", lambda e, Ov=Ov: e.reciprocal(out=rinv[:, 0:4], in_=Ov[:, :, 64]),
                               reads=[ok], writes=["rinv"])
                          for jj in range(4):
                              p.op("dve", lambda e, Ov=Ov, jj=jj, h=h, b=b: e.scalar_tensor_tensor(
                                  out=mixed[:, 4 * b + jj, h * 64:(h + 1) * 64], in0=Ov[:, jj, 0:64],
                                  scalar=rinv[:, jj:jj + 1], in1=sgA[:, jj, h * 64:(h + 1) * 64],
                                  op0=ALU.mult, op1=ALU.mult),
                                  reads=[ok, "rinv", ("sg", bi, jj)], writes=[("mixed", 4 * b + jj)])

                      yield from attention_units_gen(p, b, 8, qk_A, exp_A, None,
                                                     lambda k, h: Va[:, k, h * 65:(h + 1) * 65], fin_A, PTs, Lps, Ops)

                  def merge(ga, gi, n_att, n_bis):
                      done = 0
                      seen = 0
                      alive = True
                      for tag, _c in gi:
                          if tag == "bis" and alive:
                              seen += 1
                              target = (n_att * seen + n_bis - 1) // n_bis
                              while done < target:
                                  try:
                                      next(ga)
                                      done += 1
                                  except StopIteration:
                                      alive = False
                                      break
                      for _ in ga:
                          pass

                  def drain(g):
                      for _ in g:
                          pass

                  blockA_p1(0)
                  if stage >= 2:
                      drain(blockA_idx(0))
                  for b in range(4):
                      if b + 1 < 4:
                          blockA_p1(b + 1)
                      if stage >= 3:
                          if b + 1 < 4:
                              merge(blockA_att(b), blockA_idx(b + 1), 8 * (4 * b + 4), 2 * NIT)
                          else:
                              drain(blockA_att(b))
                      elif stage >= 2 and b + 1 < 4:
                          drain(blockA_idx(b + 1))

                  if seq == 0:
                      QaT, sgA = QaTb[1], sgAb[1]
                      dump = {"KaT": (KaT, [128, 4 * 2048]), "KiT": (KiT, [128, 2048]), "QaT": (QaT, [128, 4 * 512]),
                              "QiT": (QiT, [128, 4 * 512]), "sgA": (sgA, [128, 4 * 512]), "Va": (Va, [128, 16 * 520]),
                              "mixedA": (mixed, [128, 16 * 1024])}
                      for name, (tl, shp) in dump.items():
                          if name in dbg_d:
                              flat = tl[:] if len(tl.shape) == 2 else tl[:].rearrange(
                                  "p a b -> p (a b)") if len(tl.shape) == 3 else tl[:]
                              nfree = shp[1]
                              for c0 in range(0, nfree, 2048):
                                  c1 = min(nfree, c0 + 2048)
                                  p.dma("pool", "dbgp", lambda e, flat=flat, c0=c0, c1=c1, name=name: e.dma_start(
                                      out=dbg_d[name][:, c0:c1], in_=flat[:, c0:c1]),
                                      reads=list(p.lw.keys()))
                  p.finish()
                  p.emit()
            if stage <= 3:
                continue
            stB = ExitStack()
            with stB:
                p = Prog(nc, stB)
                Bt = lambda name, shape, dt: sb(f"B{seq}_{name}", shape, dt, stB)
                WB = Bt("WB", [128, 8, 1184], BF16)
                WUQ = Bt("WUQ", [128, 3, 768], BF16)
                WUKV = Bt("WUKV", [128, 2, 1024], BF16)
                xnT = Bt("xnT", [128, 8, 512], BF16)
                xst = [Bt(f"xs{i}", [128, 1024], F32) for i in range(2)]
                xst_rot = Rot([(xst[i], f"xs{i}", f"xs{i}") for i in range(2)])
                p.junk = Bt("junk", [128, 1024], BF16)
                p.ss = Bt("ss", [128, 1], F32)
                p.rs = Bt("rs", [128, 2], F32)
                p.xb = Bt("xb", [128, 1024], BF16)
                tabs = Bt("tabs", [128, 4, 512], F32)
                KbT = Bt("KbT", [128, 8, 2048], BF16)
                Vb = Bt("Vb", [128, 16, 8 * 65], BF16)
                rk = Bt("rk", [128, 16, 8], F32)
                QbT = Bt("QbT", [128, 8, 512], BF16)
                sgB = Bt("sgB", [128, 4, 512], BF16)
                cqs = Bt("cqs", [128, 3, 512], BF16)
                cqn = Bt("cqn", [128, 3, 512], BF16)
                ckvs = Bt("ckvs", [128, 2, 512], BF16)
                ckvn = Bt("ckvn", [128, 2, 512], BF16)
                sqb = [Bt(f"sqb{i}", [128, 512], BF16) for i in range(2)]
                rst = [Bt(f"rst{i}", [128, 512], F32) for i in range(2)]
                ta = [Bt(f"ta{i}", [128, 512], F32) for i in range(2)]
                tb = [Bt(f"tb{i}", [128, 512], F32) for i in range(2)]
                yb = [Bt(f"yb{i}", [128, 512], BF16) for i in range(3)]
                krs = Bt("krs", [128, 512], BF16)
                kro = Bt("kro", [128, 512], BF16)
                PTt = [Bt(f"PT{i}", [128, 512], BF16) for i in range(3)]
                ssr = Bt("ssr", [128, 4], F32)
                ssn = Bt("ssn", [128, 8], F32)
                t8 = Bt("t8", [128, 16], F32)
                sq32 = Bt("sq32", [128, 256], F32)
                rinv = Bt("rinv", [128, 4], F32)

                load_w(p, WB, win_d[:, 2632:3816], 8, "wb", "WB")
                scale_rows(p, WB, 8, C_NG, "WB")
                load_w(p, WUQ, wuq_d[:, :], 3, "wuq", "WUQ")
                scale_rows(p, WUQ, 3, C_CQ, "WUQ")
                load_w(p, WUKV, wukv_d[:, :], 2, "wukv", "WUKV")
                scale_rows(p, WUKV, 2, C_CKV, "WUKV")
                p.op("pool", lambda e: e.memset(Vb[:].rearrange("p i (h d) -> p (i h) d", d=65)[:, :, 64:65], 1.0),
                     writes=[("V", q) for q in range(4)])
                p.op("pool", lambda e: e.memset(krs[:], 0.0), writes=["krs"])

                Lps = Rot([(PS[i], f"ps{i}") for i in range(4)])
                for b in range(4):
                    t0 = 512 * b
                    if bstage <= -1:
                        continue
                    common_p0(p, stB, seq, b, xnT, xst_rot, "B")
                    if bstage <= 0:
                        continue
                    if 'notab' not in VAR:
                        p.dma("sp", "tab", lambda e, b=b: e.dma_start(out=tabs[:, 2:4, :], in_=tabs_d[b, :, 2:4, :]),
                              writes=[("tab", 2), ("tab", 3)])
                    PJ = Rot([(PS[0], "ps0"), (PS[1], "ps1"), (PS[6], "ps6")])
                    YB = Rot([(yb[0], "yb0"), (yb[1], "yb1"), (yb[2], "yb2")])
                    SQ = Rot([(sqb[0], "sqb0"), (sqb[1], "sqb1")])
                    MS = Rot([(PS[2], "ps2"), (PS[4], "ps4")])
                    PR = Rot([(PS[3], "ps3"), (PS[5], "ps5")])
                    RS = Rot([(rst[0], "rst0"), (rst[1], "rst1")])
                    TA = Rot([(ta[0], "ta0"), (ta[1], "ta1")])
                    TB = Rot([(tb[0], "tb0"), (tb[1], "tb1")])

                    def proj_fm(col, ncol=128):
                        pj, pk = PJ.next()
                        for kc in range(8):
                            p.op("pe", lambda e, kc=kc, pj=pj: e.matmul(
                                pj[0:ncol, :], lhsT=WB[:, kc, col:col + ncol], rhs=xnT[:, kc, :],
                                start=(kc == 0), stop=(kc == 7)),
                                reads=[("WB", kc), "xnT"], writes=[pk])
                        return pj, pk

                    jobs = []
                    for (nch, cbase, raw, nrm, rawk, nrmk, dim) in ((3, 0, cqs, cqn, "cqs", "cqn", 384.0),
                                                                    (2, 384, ckvs, ckvn, "ckvs", "ckvn", 256.0)):
                        grp = {}
                        for c in range(nch):
                            def s0(st, c=c, cbase=cbase):
                                st["pj"], st["pk"] = proj_fm(cbase + c * 128)

                            def s1(st, c=c, nch=nch, grp=grp, raw=raw, rawk=rawk):
                                pj, pk = st["pj"], st["pk"]
                                if c == 0:
                                    grp["ms"], grp["msk"] = MS.next()
                                ms_, msk = grp["ms"], grp["msk"]
                                sq_, sqk = SQ.next()
                                p.op("act", lambda e, pj=pj, sq_=sq_: e.activation(out=sq_[:], in_=pj[:, :],
                                                                                   func=AF.Square),
                                     reads=[pk], writes=[sqk])
                                p.op("pe", lambda e, c=c, nch=nch, ms_=ms_, sq_=sq_: e.matmul(
                                    ms_[:, :], lhsT=ones, rhs=sq_[:], start=(c == 0), stop=(c == nch - 1)),
                                    reads=[sqk, "mats"], writes=[msk])
                                p.op("dve", lambda e, pj=pj, raw=raw, c=c: e.tensor_copy(out=raw[:, c, :], in_=pj[:, :]),
                                     reads=[pk, sqk], writes=[(rawk, c)])

                            def s2(st, nch=nch, grp=grp, raw=raw, nrm=nrm, rawk=rawk, nrmk=nrmk, dim=dim):
                                ms_, msk = grp["ms"], grp["msk"]
                                rs_, rsk = RS.next()
                                p.op("act", lambda e, dim=dim, ms_=ms_, rs_=rs_: e.activation(
                                    out=rs_[:], in_=ms_[:, :], func=AF.Ln, bias=cols2[:, 6:7], scale=1.0 / dim),
                                    reads=[msk, "cols2"], writes=[rsk])
                                p.op("act", lambda e, rs_=rs_: e.activation(out=rs_[:], in_=rs_[:], func=AF.Exp,
                                                                            scale=-0.5),
                                     reads=[rsk], writes=[rsk])
                                for cc in range(nch):
                                    p.op("pool", lambda e, raw=raw, nrm=nrm, cc=cc, rs_=rs_: e.tensor_tensor(
                                        out=nrm[:, cc, :], in0=raw[:, cc, :], in1=rs_[:], op=ALU.mult),
                                        reads=[(rawk, cc), rsk], writes=[nrmk])
                            jobs.append([s0, s1, s2] if c == nch - 1 else [s0, s1])
                    run_pipeline(jobs)
                    if bstage <= 1:
                        continue
                    pj, pk = proj_fm(576, ncol=96)
                    p.op("dve", lambda e, pj=pj: e.tensor_scalar(out=krs[64:96, :], in0=pj[64:96, :],
                                                                 scalar1=colap(C_BK, 64, 96), scalar2=None,
                                                                 op0=ALU.mult),
                         reads=[pk, "cols"], writes=["krs"])
                    pr_, prk = PR.next()
                    ta_, tak = TA.next()
                    tb_, tbk = TB.next()
                    p.op("pe", lambda e, pr_=pr_: e.matmul(pr_[0:96, :], lhsT=permI[0:96, 0:96], rhs=krs[0:96, :],
                                                           start=True, stop=True),
                         reads=["krs", "mats"], writes=[prk])
                    p.op("pool", lambda e, ta_=ta_: e.tensor_tensor(out=ta_[64:96, :], in0=krs[64:96, :],
                                                                    in1=tabs[64:96, 2, :], op=ALU.mult),
                         reads=["krs", ("tab", 2)], writes=[tak])
                    p.op("dve", lambda e, pr_=pr_, tb_=tb_: e.tensor_tensor(out=tb_[64:96, :], in0=pr_[64:96, :],
                                                                            in1=tabs[64:96, 3, :], op=ALU.mult),
                         reads=[prk, ("tab", 3)], writes=[tbk])
                    p.op("pool", lambda e, ta_=ta_, tb_=tb_: e.tensor_tensor(out=kro[64:96, :], in0=ta_[64:96, :],
                                                                             in1=tb_[64:96, :], op=ALU.add),
                         reads=[tak, tbk], writes=["kro"])
                    for hh in range(8):
                        p.op("dve", lambda e, t0=t0, hh=hh: e.tensor_copy(out=KbT[64:96, hh, t0:t0 + 512],
                                                                          in_=kro[64:96, :]),
                             reads=["kro"], writes=[("KbT", b)])
                    if bstage <= 2:
                        continue
                    for j in range(4):
                        for what in ("r", "g"):
                            pj, pk = PJ.next()
                            c0, n = {"r": (640, 32), "g": (672, 512)}[what]
                            for kc in range(8):
                                p.op("pe", lambda e, kc=kc, pj=pj, j=j, c0=c0, n=n: e.matmul(
                                    pj[:, 0:n], lhsT=xnT[:, kc, j * 128:(j + 1) * 128], rhs=WB[:, kc, c0:c0 + n],
                                    start=(kc == 0), stop=(kc == 7)),
                                    reads=[("WB", kc), "xnT"], writes=[pk])
                            if what == "r":
                                p.op("act", lambda e, pj=pj, j=j: e.activation(
                                    out=sq32[:, 0:32], in_=pj[:, 0:32], func=AF.Square, accum_out=ssr[:, j:j + 1]),
                                    reads=[pk], writes=["sq32", ("ssr", j)])
                            else:
                                p.op("act", lambda e, pj=pj, j=j: e.activation(out=sgB[:, j, :], in_=pj[:, :],
                                                                               func=AF.Silu),
                                     reads=[pk], writes=[("sg", j)])
                    if bstage <= 3:
                        continue
                    jobs = []
                    for h in range(8):
                        def s0(st, h=h):
                            pj, pk = PJ.next()
                            st["pj"], st["pk"] = pj, pk
                            for c in range(3):
                                p.op("pe", lambda e, c=c, pj=pj, h=h: e.matmul(
                                    pj[0:96, :], lhsT=WUQ[:, c, h * 96:(h + 1) * 96], rhs=cqn[:, c, :],
                                    start=(c == 0), stop=(c == 2)),
                                    reads=[("WUQ", c), "cqn"], writes=[pk])

                        def s1(st):
                            pj, pk = st["pj"], st["pk"]
                            sq_, sqk = SQ.next()
                            ms_, msk = MS.next()
                            st["ms"], st["msk"] = ms_, msk
                            p.op("act", lambda e, pj=pj, sq_=sq_: e.activation(out=sq_[0:96, :], in_=pj[0:96, :],
                                                                               func=AF.Square),
                                 reads=[pk], writes=[sqk])
                            p.op("pe", lambda e, sq_=sq_, ms_=ms_: e.matmul(ms_[0:96, :], lhsT=ones[0:96, 0:96],
                                                                            rhs=sq_[0:96, :], start=True, stop=True),
                                 reads=[sqk, "mats"], writes=[msk])

                        def s2(st, h=h):
                            pj, pk, ms_, msk = st["pj"], st["pk"], st["ms"], st["msk"]
                            rs_, rsk = RS.next()
                            pr_, prk = PR.next()
                            ta_, tak = TA.next()
                            tb_, tbk = TB.next()
                            p.op("act", lambda e, ms_=ms_, rs_=rs_: e.activation(
                                out=rs_[0:96, :], in_=ms_[0:96, :], func=AF.Ln, bias=cols2[0:96, 6:7],
                                scale=1.0 / 96.0),
                                reads=[msk, "cols2"], writes=[rsk])
                            p.op("act", lambda e, rs_=rs_: e.activation(out=rs_[0:96, :], in_=rs_[0:96, :],
                                                                        func=AF.Exp, scale=-0.5),
                                 reads=[rsk], writes=[rsk])
                            yt, yk = YB.next()
                            p.op("dve", lambda e, pj=pj, yt=yt, rs_=rs_: e.scalar_tensor_tensor(
                                out=yt[0:96, :], in0=pj[0:96, :], scalar=cols2[0:96, 1:2], in1=rs_[0:96, :],
                                op0=ALU.mult, op1=ALU.mult),
                                reads=[pk, rsk, "cols2"], writes=[yk])
                            p.op("dve", lambda e, yt=yt, h=h: e.tensor_copy(out=QbT[0:64, h, :], in_=yt[0:64, :]),
                                 reads=[yk], writes=[("QbT", h)])
                            p.op("pe", lambda e, yt=yt, pr_=pr_: e.matmul(pr_[0:96, :], lhsT=permI[0:96, 0:96],
                                                                          rhs=yt[0:96, :], start=True, stop=True),
                                 reads=[yk, "mats"], writes=[prk])
                            p.op("pool", lambda e, yt=yt, ta_=ta_: e.tensor_tensor(
                                out=ta_[64:96, :], in0=yt[64:96, :], in1=tabs[64:96, 2, :], op=ALU.mult),
                                reads=[yk, ("tab", 2)], writes=[tak])
                            p.op("dve", lambda e, pr_=pr_, tb_=tb_: e.tensor_tensor(
                                out=tb_[64:96, :], in0=pr_[64:96, :], in1=tabs[64:96, 3, :], op=ALU.mult),
                                reads=[prk, ("tab", 3)], writes=[tbk])
                            p.op("pool", lambda e, h=h, ta_=ta_, tb_=tb_: e.tensor_tensor(
                                out=QbT[64:96, h, :], in0=ta_[64:96, :], in1=tb_[64:96, :], op=ALU.add),
                                reads=[tak, tbk], writes=[("QbT", h)])
                        jobs.append([s0, s1, s2])
                    run_pipeline(jobs)
                    if bstage <= 4:
                        continue
                    for j in range(4):
                        i = 4 * b + j
                        for hf in range(2):
                            pj, pk = PJ.next()
                            for c in range(2):
                                p.op("pe", lambda e, c=c, pj=pj, j=j, hf=hf: e.matmul(
                                    pj[:, :], lhsT=ckvn[:, c, j * 128:(j + 1) * 128],
                                    rhs=WUKV[:, c, hf * 512:(hf + 1) * 512], start=(c == 0), stop=(c == 1)),
                                    reads=[("WUKV", c), "ckvn"], writes=[pk])
                            pjv = pj[:, :].rearrange("p (h d) -> p h d", d=128)
                            p.op("act", lambda e, pjv=pjv, i=i, hf=hf: e.copy(
                                out=Vb[:, i, :].rearrange("p (h d) -> p h d", d=65)[:, hf * 4:hf * 4 + 4, 0:64],
                                in_=pjv[:, :, 64:128]),
                                reads=[pk], writes=[("V", b)])
                            p.op("act", lambda e, pjv=pjv: e.activation(
                                out=sq32[:].rearrange("p (h d) -> p h d", d=64), in_=pjv[:, :, 0:64], func=AF.Square),
                                reads=[pk], writes=["sq32"])
                            p.op("dve", lambda e, hf=hf: e.tensor_reduce(
                                out=ssn[:, hf * 4:hf * 4 + 4], in_=sq32[:].rearrange("p (h d) -> p h d", d=64),
                                axis=AX.X, op=ALU.add),
                                reads=["sq32"], writes=["ssn"])
                        p.op("dve", lambda e, j=j: e.tensor_scalar(out=t8[:, 0:8], in0=ssn[:, 0:8],
                                                                   scalar1=ssr[:, j:j + 1], scalar2=1.0 / 96.0,
                                                                   op0=ALU.add, op1=ALU.mult),
                             reads=["ssn", ("ssr", j)], writes=["t8"])
                        p.op("act", lambda e: e.activation(out=t8[:, 8:16], in_=t8[:, 0:8], func=AF.Ln,
                                                           bias=cols2[:, 6:7], scale=1.0),
                             reads=["t8", "cols2"], writes=["t8b"])
                        p.op("act", lambda e, i=i: e.activation(out=rk[:, i, :], in_=t8[:, 8:16], func=AF.Exp,
                                                                scale=-0.5),
                             reads=["t8b"], writes=[("rk", b)])
                    if bstage <= 5:
                        continue
                    for h in range(8):
                        pj, pk = PJ.next()
                        for c in range(2):
                            p.op("pe", lambda e, c=c, pj=pj, h=h: e.matmul(
                                pj[0:64, :], lhsT=WUKV[:, c, h * 128:h * 128 + 64], rhs=ckvn[:, c, :],
                                start=(c == 0), stop=(c == 1)),
                                reads=[("WUKV", c), "ckvn"], writes=[pk])
                        p.op("dve", lambda e, pj=pj, h=h, t0=t0: e.tensor_scalar(
                            out=KbT[0:64, h, t0:t0 + 512], in0=pj[0:64, :], scalar1=colap(C_BK, 0, 64),
                            scalar2=None, op0=ALU.mult),
                            reads=[pk, "cols"], writes=[("KbT", b)])

                    if bstage <= 6:
                        continue
                    PTs = Rot([(PTt[i], f"PT{i}") for i in range(3)])
                    Ops = Rot([(PS[4], "ps4"), (PS[5], "ps5"), (PS[6], "ps6")])

                    def qk_B(h, k, col0, lp, lk):
                        N = 512 - col0
                        p.op("pe", lambda e, lp=lp, h=h, k=k, col0=col0, N=N: e.matmul(
                            lp[:, 0:N], lhsT=KbT[0:96, h, k * 128:(k + 1) * 128], rhs=QbT[0:96, h, col0:512],
                            start=True, stop=True),
                            reads=[("KbT", k // 4), ("QbT", h)], writes=[lk])

                    def exp_B(h, k, col0, lp, lk, pt, pk, N):
                        p.op("act", lambda e, lp=lp, pt=pt, N=N, k=k, h=h: e.activation(
                            out=pt[:, 0:N], in_=lp[:, 0:N], func=AF.Exp, scale=rk[:, k, h:h + 1]),
                            reads=[lk, ("rk", k // 4)], writes=[pk])

                    def post_B(h, k, col0, pt, pk):
                        p.op("pool", lambda e, pt=pt: e.memset(pt[64:128, 0:64], 0.0), writes=[pk])

                    def fin_B(h, O, ok, b=b):
                        Ov = O[:, 0:260].rearrange("p (j d) -> p j d", d=65)
                        p.op("dve", lambda e, Ov=Ov: e.reciprocal(out=rinv[:, 0:4], in_=Ov[:, :, 64]),
                             reads=[ok], writes=["rinv"])
                        for jj in range(4):
                            p.op("dve", lambda e, Ov=Ov, jj=jj, h=h, b=b: e.scalar_tensor_tensor(
                                out=mixed[:, 4 * b + jj, 512 + h * 64:512 + (h + 1) * 64], in0=Ov[:, jj, 0:64],
                                scalar=rinv[:, jj:jj + 1], in1=sgB[:, jj, h * 64:(h + 1) * 64],
                                op0=ALU.mult, op1=ALU.mult),
                                reads=[ok, "rinv", ("sg", jj)], writes=[("mixed", 4 * b + jj)])

                    attention_units(p, b, 8, qk_B, exp_B, post_B,
                                    lambda k, h: Vb[:, k, h * 65:(h + 1) * 65], fin_B, PTs, Lps, Ops)
                if seq == 0 and "mixedB" in dbg_d:
                    flat = mixed[:].rearrange("p a b -> p (a b)")
                    for c0 in range(0, 16384, 2048):
                        p.dma("pool", "dbgp", lambda e, c0=c0: e.dma_start(
                            out=dbg_d["mixedB"][:, c0:c0 + 2048], in_=flat[:, c0:c0 + 2048]),
                            reads=list(p.lw.keys()))
                p.finish()
                p.emit()
            if stage <= 4:
                continue
            stO = ExitStack()
            with stO:
                p = Prog(nc, stO)
                Ot = lambda name, shape, dt: sb(f"O{seq}_{name}", shape, dt, stO)
                WO = Ot("WO", [128, 8, 1024], BF16)
                mT = [Ot(f"mT{i}", [128, 8, 128], BF16) for i in range(2)]
                xst = [Ot(f"xs{i}", [128, 1024], F32) for i in range(2)]
                ost = [Ot(f"os{i}", [128, 1024], F32) for i in range(2)]
                load_w(p, WO, wout_d[:, :], 8, "wo", "WO")
                PJ = Rot([(PS[i], f"ps{i}") for i in range(4)])
                for i in range(16):
                    mt, mk = mT[i % 2], f"mT{i % 2}"
                    xs, xk = xst[i % 2], f"xs{i % 2}"
                    os_, okk = ost[i % 2], f"os{i % 2}"
                    p.dma("sp", xk, lambda e, xs=xs, i=i: e.dma_start(out=xs[:], in_=x_d[seq, i * 128:(i + 1) * 128, :]),
                          writes=[xk])
                    for g in range(2):
                        for q in range(4):
                            kc = 4 * g + q
                            p.op("pe", lambda e, kc=kc, q=q, i=i: e.transpose(
                                out=PST[:, q * 128:(q + 1) * 128], in_=mixed[:, i, kc * 128:(kc + 1) * 128],
                                identity=ident),
                                reads=["mats"], writes=["pst"])
                        if g == 0:
                            p.op("act", lambda e, g=g, mt=mt: e.copy(
                                out=mt[:, 4 * g:4 * g + 4, :], in_=PST[:, 0:512].rearrange("p (q t) -> p q t", t=128)),
                                reads=["pst"], writes=[mk])
                        else:
                            p.op("dve", lambda e, g=g, mt=mt: e.tensor_copy(
                                out=mt[:, 4 * g:4 * g + 4, :], in_=PST[:, 0:512].rearrange("p (q t) -> p q t", t=128)),
                                reads=["pst"], writes=[mk])
                    for nh in range(2):
                        pj, pk = PJ.next()
                        for kc in range(8):
                            p.op("pe", lambda e, kc=kc, pj=pj, mt=mt, nh=nh: e.matmul(
                                pj[:, :], lhsT=mt[:, kc, :], rhs=WO[:, kc, nh * 512:(nh + 1) * 512],
                                start=(kc == 0), stop=(kc == 7)),
                                reads=[mk, ("WO", kc)], writes=[pk])
                        p.op("dve", lambda e, pj=pj, os_=os_, xs=xs, nh=nh: e.tensor_tensor(
                            out=os_[:, nh * 512:(nh + 1) * 512], in0=pj[:, :], in1=xs[:, nh * 512:(nh + 1) * 512],
                            op=ALU.add),
                            reads=[pk, xk], writes=[okk])
                    p.dma("sp", "o" + okk, lambda e, os_=os_, i=i: e.dma_start(
                        out=out_d[seq, i * 128:(i + 1) * 128, :], in_=os_[:]), reads=[okk])
                p.finish()
                p.emit()
    return nc


def host_consts(inp):
    cols = np.zeros((128, 32), np.float32)
    ng = inp["norm_gain"].reshape(1024)
    cols[:, 0:8] = ng.reshape(8, 128).T
    cols[:, 8] = np.tile(inp["a_q_norm"].reshape(64), 2)
    cols[:, 9] = np.tile(inp["a_k_norm"].reshape(64), 2)
    cols[:, 10:13] = inp["b_q_latent_norm"].reshape(3, 128).T
    cols[:, 13:15] = inp["b_kv_latent_norm"].reshape(2, 128).T
    cols[:96, 15] = inp["b_q_norm"].reshape(96)
    cols[:96, 16] = inp["b_k_norm"].reshape(96)
    pidx = np.arange(128)
    cols[:, 17] = np.power(np.float32(10000.0), -(pidx % 32).astype(np.float32) * np.float32(2.0) / np.float32(64))
    cols[:, 18] = np.power(np.float32(10000.0), -(pidx % 16).astype(np.float32) * np.float32(2.0) / np.float32(32))
    m64 = pidx % 64
    cols[:, 19] = np.where(m64 < 32, 1.0, -1.0)
    cols[:, 20] = np.where(m64 < 16, 1.0, np.where(m64 < 32, -1.0, 0.0))
    mI = (m64 < 32).astype(np.float32)
    cols[:, 21] = -mI
    cols[:, 22] = 1.0 - mI
    mats = np.zeros((128, 6, 128), np.float32)
    mats[:, 0, :] = np.eye(128)
    for m in range(128):
        pm = m + 32 if m64[m] < 32 else m - 32
        mats[pm, 1, m] = 1.0
        if m64[m] < 16:
            mats[m + 16, 2, m] = 1.0
        elif m64[m] < 32:
            mats[m - 16, 2, m] = 1.0
    mats[:, 3, :] = 1.0
    mats[:, 4, :] = (pidx[:, None] // 64 == pidx[None, :] // 64).astype(np.float32)
    pos = np.arange(S, dtype=np.float32)
    tabs = np.zeros((128, 4, S), np.float32)
    angA = pos[None, :] * cols[:, 17:18]
    angI = pos[None, :] * cols[:, 18:19]
    sgnA = np.where(m64 < 32, -1.0, 1.0)[:, None]
    sgnI = np.where(m64 < 16, -1.0, np.where(m64 < 32, 1.0, 0.0))[:, None]
    tabs[:, 0] = np.cos(angA)
    tabs[:, 1] = sgnA * np.sin(angA)
    tabs[:, 2] = np.where(mI[:, None] > 0, np.cos(angI), 1.0)
    tabs[:, 3] = sgnI * np.sin(angI)
    tabs = np.ascontiguousarray(tabs.reshape(128, 4, 4, 512).transpose(2, 0, 1, 3)).astype(np.float32)
    return cols, mats, tabs


def make_in_maps(inp):
    cols, mats, tabs = host_consts(inp)
    x = np.ascontiguousarray(inp["x"], dtype=np.float32)
    maps = []
    for c in range(NCORES):
        maps.append({
            "x": np.ascontiguousarray(x[c * SEQ_PER_CORE:(c + 1) * SEQ_PER_CORE]),
            "w_in": np.ascontiguousarray(inp["w_in"][0]),
            "w_uq": np.ascontiguousarray(inp["w_uq"][0]),
            "w_ukv": np.ascontiguousarray(inp["w_ukv"][0]),
            "w_out": np.ascontiguousarray(inp["w_out"][0]),
            "cols": cols, "mats": mats, "tabs": tabs,
        })
    return maps


def kernel(**inputs):
    inp = {k: np.asarray(v) for k, v in inputs.items()}
    nc = build_nc()
    res = run_bass_kernel_spmd(nc, make_in_maps(inp), core_ids=list(range(NCORES)))
    out = np.concatenate([np.asarray(r["out"]) for r in res.results], axis=0)
    return out.astype(np.float32)
```

```python
import numpy as np
import os
VAR = os.environ.get('KVAR', '')
from contextlib import ExitStack
import concourse.bass as bass
import concourse.mybir as mybir
from concourse.bass_utils import run_bass_kernel_spmd

F32 = mybir.dt.float32
BF16 = mybir.dt.bfloat16
I32 = mybir.dt.int32
ALU = mybir.AluOpType
AF = mybir.ActivationFunctionType
AX = mybir.AxisListType

S = 2048
DM = 1024
NCORES = 8
SEQ_PER_CORE = 2
EPS = 1e-6
NIT = 24
TOPK = 256
NEG = -30000.0
PI = float(np.pi)

ENGS = ("pe", "act", "dve", "pool", "sp")


class Prog:
    uid = 0

    def __init__(self, nc, st):
        Prog.uid += 1
        self.nc = nc
        self.st = st
        self.sems = {("eng", e): st.enter_context(nc.semaphore(f"se_{e}_{Prog.uid}")) for e in ENGS}
        self.cnt = {e: 0 for e in ENGS}
        self.ops = {e: [] for e in ENGS}
        self.waited = {e: {} for e in ENGS}
        self.lw = {}
        self.rd = {}
        self.chcnt = {}

    def _deps(self, e, reads, writes):
        deps = {}

        def add(s, v):
            if deps.get(s, 0) < v:
                deps[s] = v
        for k in reads:
            t = self.lw.get(k)
            if t:
                add(*t)
        for k in writes:
            t = self.lw.get(k)
            if t:
                add(*t)
            for s, v in self.rd.get(k, {}).items():
                add(s, v)
        out = []
        for s, v in deps.items():
            if e == "pe" and s == ("eng", "pe"):
                continue
            if self.waited[e].get(s, 0) >= v:
                continue
            self.waited[e][s] = v
            out.append((s, v))
        return out

    def _commit(self, tok, reads, writes):
        for k in reads:
            d = self.rd.setdefault(k, {})
            if d.get(tok[0], 0) < tok[1]:
                d[tok[0]] = tok[1]
        for k in writes:
            self.lw[k] = tok
            self.rd[k] = {}

    def op(self, e, fn, reads=(), writes=()):
        waits = self._deps(e, reads, writes)
        self.cnt[e] += 1
        tok = (("eng", e), self.cnt[e])
        self.ops[e].append((waits, fn, tok[0], 1))
        self._commit(tok, reads, writes)

    def dma(self, e, chan, fn, reads=(), writes=()):
        waits = self._deps(e, reads, writes)
        key = ("ch", chan)
        if key not in self.sems:
            self.sems[key] = self.st.enter_context(self.nc.semaphore(f"sc_{chan}_{Prog.uid}"))
            self.chcnt[key] = 0
        self.chcnt[key] += 16
        tok = (key, self.chcnt[key])
        self.ops[e].append((waits, fn, key, 16))
        self._commit(tok, reads, writes)

    def finish(self):
        waits = []
        for key, v in self.chcnt.items():
            if self.waited["sp"].get(key, 0) < v:
                waits.append((key, v))
        self.ops["sp"].append((waits, None, None, 0))

    def emit(self):
        nc = self.nc
        engobj = {"pe": nc.tensor, "act": nc.scalar, "dve": nc.vector, "pool": nc.gpsimd, "sp": nc.sync}
        with nc.Block() as block:
            def run(e):
                eng = engobj[e]
                for waits, fn, inc, amt in self.ops[e]:
                    for s, v in waits:
                        eng.wait_ge(self.sems[s], v)
                    if fn is not None:
                        ins = fn(eng)
                        ins.then_inc(self.sems[inc], amt)

            @block.tensor
            def _(t):
                run("pe")

            @block.scalar
            def _(t):
                run("act")

            @block.vector
            def _(t):
                run("dve")

            @block.gpsimd
            def _(t):
                run("pool")

            @block.sync
            def _(t):
                run("sp")


def run_pipeline(jobs):
    n = len(jobs)
    S = max(len(j) for j in jobs)
    states = [dict() for _ in jobs]
    for step in range(n + S - 1):
        for st_i in range(S):
            c = step - st_i
            if 0 <= c < n and st_i < len(jobs[c]):
                jobs[c][st_i](states[c])


class Rot:
    def __init__(self, items):
        self.items = list(items)
        self.i = 0

    def next(self):
        it = self.items[self.i % len(self.items)]
        self.i += 1
        return it


def build_nc(stage=99, dbg=None, bstage=99, skipA=False, nseq=SEQ_PER_CORE):
    dbg = dbg if dbg is not None else {}
    nc = bass.Bass("TRN2", target_bir_lowering=False)
    x_d = nc.dram_tensor("x", [SEQ_PER_CORE, S, DM], F32, kind="ExternalInput").ap()
    win_d = nc.dram_tensor("w_in", [DM, 3816], F32, kind="ExternalInput").ap()
    wuq_d = nc.dram_tensor("w_uq", [384, 768], F32, kind="ExternalInput").ap()
    wukv_d = nc.dram_tensor("w_ukv", [256, 1024], F32, kind="ExternalInput").ap()
    wout_d = nc.dram_tensor("w_out", [DM, DM], F32, kind="ExternalInput").ap()
    cols_d = nc.dram_tensor("cols", [128, 32], F32, kind="ExternalInput").ap()
    mats_d = nc.dram_tensor("mats", [128, 6, 128], F32, kind="ExternalInput").ap()
    tabs_d = nc.dram_tensor("tabs", [4, 128, 4, 512], F32, kind="ExternalInput").ap()
    out_d = nc.dram_tensor("out", [SEQ_PER_CORE, S, DM], F32, kind="ExternalOutput").ap()
    dbg_d = {}
    for name, shape in dbg.items():
        dbg_d[name] = nc.dram_tensor("dbg_" + name, list(shape), F32, kind="ExternalOutput").ap()

    top = ExitStack()
    with top:
        def sb(name, shape, dt, st=top):
            return st.enter_context(nc.sbuf_tensor("sb_" + name, list(shape), dt))

        cols = sb("cols", [128, 32], F32)
        cols2 = sb("cols2", [128, 8], F32)
        matsf = sb("matsf", [128, 6, 128], F32)
        mats = sb("matsb", [128, 6, 128], BF16)
        ident = mats[:, 0, :]
        permA = mats[:, 1, :]
        permI = mats[:, 2, :]
        ones = mats[:, 3, :]
        bdA = mats[:, 4, :]
        zeros = mats[:, 5, :]
        fin260 = mats[:].rearrange("p a b -> p (a b)")[:, 0:260]
        mixed = sb("mixed", [128, 16, 1024], BF16)
        PS = [top.enter_context(nc.psum_tensor(f"ps{i}", [128, 512], F32)) for i in range(7)]
        PST = top.enter_context(nc.psum_tensor("pst", [128, 1024], BF16))

        C_NG, C_AQ, C_AK, C_CQ, C_CKV, C_BQ, C_BK = 0, 8, 9, 10, 13, 15, 16
        C_FA, C_FI, C_NSA, C_NSI, C_NCMI, C_OMI = 17, 18, 19, 20, 21, 22

        st0 = ExitStack()
        with st0:
            p = Prog(nc, st0)
            p.dma("sp", "c0", lambda e: e.dma_start(out=cols[:], in_=cols_d[:, :]), writes=["cols"])
            p.dma("sp", "c1", lambda e: e.dma_start(out=matsf[:], in_=mats_d[:, :, :]), writes=["matsf"])
            p.op("dve", lambda e: e.tensor_copy(out=mats[:], in_=matsf[:]), reads=["matsf"], writes=["mats"])
            p.op("dve", lambda e: e.tensor_scalar(out=cols2[:, 0:1], in0=cols[:, C_AQ:C_AQ + 1], scalar1=0.125,
                                                  scalar2=None, op0=ALU.mult), reads=["cols"], writes=["cols2"])
            p.op("dve", lambda e: e.tensor_scalar(out=cols2[:, 1:2], in0=cols[:, C_BQ:C_BQ + 1],
                                                  scalar1=float(96.0 ** -0.5), scalar2=None, op0=ALU.mult),
                 reads=["cols"], writes=["cols2"])
            p.op("dve", lambda e: e.memset(cols2[:, 7:8], 1024.0 * EPS), writes=["cols2"])
            p.op("dve", lambda e: e.memset(cols2[:, 6:7], EPS), writes=["cols2"])
            p.finish()
            p.emit()

        def colap(i, lo=0, hi=128):
            return cols[lo:hi, i:i + 1]

        def common_p0(p, st, seq, b, xnT, xst_rot, tag):
            for j in range(4):
                i = 4 * b + j
                xs, xk, ch = xst_rot.next()
                p.dma("sp", ch, lambda e, xs=xs, i=i: e.dma_start(out=xs[:], in_=x_d[seq, i * 128:(i + 1) * 128, :]),
                      writes=[xk])
                p.op("act", lambda e, xs=xs: e.activation(out=p.junk[:, 0:1024], in_=xs[:], func=AF.Square,
                                                          accum_out=p.ss[:, 0:1]),
                     reads=[xk], writes=["junk", "ss"])
                p.op("act", lambda e: e.activation(out=p.rs[:, 1:2], in_=p.ss[:, 0:1], func=AF.Ln,
                                                   bias=cols2[:, 7:8], scale=1.0),
                     reads=["ss", "cols2"], writes=["rs1"])
                p.op("act", lambda e: e.activation(out=p.rs[:, 0:1], in_=p.rs[:, 1:2], func=AF.Exp, scale=-0.5),
                     reads=["rs1"], writes=["rs"])
                p.op("dve", lambda e, xs=xs: e.tensor_scalar(out=p.xb[:], in0=xs[:], scalar1=p.rs[:, 0:1],
                                                             scalar2=32.0, op0=ALU.mult, op1=ALU.mult),
                     reads=[xk, "rs"], writes=["xb"])
                for g in range(2):
                    for q in range(4):
                        kc = 4 * g + q
                        p.op("pe", lambda e, kc=kc, q=q: e.transpose(out=PST[:, q * 128:(q + 1) * 128],
                                                                      in_=p.xb[:, kc * 128:(kc + 1) * 128],
                                                                      identity=ident),
                             reads=["xb", "mats"], writes=["pst"])
                    eng = "act" if g == 0 else "dve"
                    if eng == "act":
                        p.op("act", lambda e, g=g, j=j: e.copy(
                            out=xnT[:, 4 * g:4 * g + 4, j * 128:(j + 1) * 128],
                            in_=PST[:, 0:512].rearrange("p (q t) -> p q t", t=128)),
                            reads=["pst"], writes=["xnT"])
                    else:
                        p.op("dve", lambda e, g=g, j=j: e.tensor_copy(
                            out=xnT[:, 4 * g:4 * g + 4, j * 128:(j + 1) * 128],
                            in_=PST[:, 0:512].rearrange("p (q t) -> p q t", t=128)),
                            reads=["pst"], writes=["xnT"])

        def make_tables(p, b, tabs, ntab=4):
            p.dma("sp", "tab", lambda e: e.dma_start(out=tabs[:, 0:ntab, :], in_=tabs_d[b, :, 0:ntab, :]),
                  writes=[("tab", ti) for ti in range(4)])

        def load_w(p, dst, src, nk, chan, key):
            for kc in range(nk):
                p.dma("pool", chan, lambda e, kc=kc: e.dma_start(
                    out=dst[:, kc, :], in_=src[kc * 128:(kc + 1) * 128, :], max_dma_last_dim=4096),
                    writes=[(key, kc)])
            tot = (("ch", chan), p.chcnt[("ch", chan)])
            for kc in range(nk):
                p.lw[(key, kc)] = tot

        def scale_rows(p, dst, nk, colbase, key, eng="dve"):
            for kc in range(nk):
                p.op(eng, lambda e, kc=kc: e.tensor_scalar(out=dst[:, kc, :], in0=dst[:, kc, :],
                                                           scalar1=colap(colbase + kc), scalar2=None, op0=ALU.mult),
                     reads=[(key, kc), "cols"], writes=[(key, kc)])

        def attention_units(p, b, nheads, qk_fn, exp_fn, post_fn, v_ap_fn, fin_fn, PTs, Lps, Ops):
            for _ in attention_units_gen(p, b, nheads, qk_fn, exp_fn, post_fn, v_ap_fn, fin_fn, PTs, Lps, Ops):
                pass

        def attention_units_gen(p, b, nheads, qk_fn, exp_fn, post_fn, v_ap_fn, fin_fn, PTs, Lps, Ops):
            units = [(h, k) for h in range(nheads) for k in range(4 * b + 4)]
            LOOK = min(2, max(1, len(Lps.items) - 1))
            state = {}

            def issue_qk(u):
                h, k = units[u]
                kk = k - 4 * b
                col0 = 128 * max(kk, 0)
                lp, lk = Lps.next()
                state[u] = (lp, lk, col0)
                qk_fn(h, k, col0, lp, lk)

            def issue_rest(u):
                h, k = units[u]
                lp, lk, col0 = state.pop(u)
                kk = k - 4 * b
                pt, pk = PTs.next()
                N = 512 - col0
                exp_fn(h, k, col0, lp, lk, pt, pk, N)
                if post_fn is not None and kk >= 0:
                    post_fn(h, k, col0, pt, pk)
                if k == 0:
                    state[("O", h)] = Ops.next()
                    O0, ok0 = state[("O", h)]
                    p.op("pe", lambda e, O0=O0: e.matmul(O0[:, 0:260], lhsT=zeros, rhs=fin260, start=True, stop=False),
                         reads=["mats"], writes=[ok0])
                O, ok = state[("O", h)]
                for jj in range(max(kk, 0), 4):
                    c0 = jj * 128 - col0
                    p.op("pe", lambda e, O=O, jj=jj, pt=pt, c0=c0, h=h, k=k: e.matmul(
                        O[:, jj * 65:jj * 65 + 65], lhsT=pt[:, c0:c0 + 128], rhs=v_ap_fn(k, h),
                        start=False, stop=(k == 4 * b + 3 and jj == 3)),
                        reads=[pk, ("V", k // 4)], writes=[ok])
                if k == 4 * b + 3:
                    fin_fn(h, O, ok)

            n = len(units)
            for u in range(min(LOOK, n)):
                issue_qk(u)
            for u in range(n):
                if u + LOOK < n:
                    issue_qk(u + LOOK)
                issue_rest(u)
                yield 1.3

        for seq in range(nseq):
            stA = ExitStack()
            if not skipA:
              with stA:
                  p = Prog(nc, stA)
                  A = lambda name, shape, dt: sb(f"A{seq}_{name}", shape, dt, stA)
                  WA = A("WA", [128, 8, 2632], BF16)
                  WKI2 = A("WKI2", [128, 8, 128], BF16)
                  SC = [A(f"scores{i}", [128, 2048], F32) for i in range(2)]
                  xnT = SC[0][:].bitcast(BF16).rearrange("p (k t) -> p k t", t=512)
                  sc1b = SC[1][:].bitcast(BF16)
                  xst_rot = Rot([(SC[1][:, 0:1024], "xs0", "xs0")])
                  p.junk = sc1b[:, 3072:4096]
                  p.ss = A("ss", [128, 1], F32)
                  p.rs = A("rs", [128, 2], F32)
                  p.xb = sc1b[:, 2048:3072]
                  tabs = A("tabs", [128, 4, 512], F32)
                  KaT = A("KaT", [128, 4, 2048], BF16)
                  KiT = A("KiT", [128, 2048], BF16)
                  Va = A("Va", [128, 16, 8 * 65], BF16)
                  QaTb = [A(f"QaT{i}", [128, 4, 512], BF16) for i in range(2)]
                  QiT = A("QiT", [128, 4, 512], BF16)
                  sgAb = [A(f"sgA{i}", [128, 4, 512], BF16) for i in range(2)]
                  wi = A("wi", [128, 4, 8], F32)
                  mbuf = [A(f"mb{i}", [128, 4, 2048], BF16) for i in range(2)]
                  PTt = [A(f"PT{i}", [128, 512], BF16) for i in range(3)]
                  Rt = [A(f"R{i}", [128, 512], BF16) for i in range(4)]
                  Ra = [A(f"Ra{i}", [128, 512], BF16) for i in range(2)]
                  Dg = A("Dg", [128, 4, 128], BF16)
                  sqb = [A(f"sqb{i}", [128, 512], BF16) for i in range(2)]
                  yb = [A(f"yb{i}", [128, 512], BF16) for i in range(3)]
                  bis = A("bis", [128, 16], F32)
                  wab = A("wab", [128, 4, 8], F32)
                  wsg = A("wsg", [128, 4, 8], F32)
                  rinv = A("rinv", [128, 4], F32)
                  fdum = A("fdum", [128, 2], F32)
                  PSTf = PST[:].bitcast(F32)

                  def fence(bi):
                      keys = ["sc0", "sc1", "xnT", "xs0", "xb", "junk", "ta0", "ta1", "tb0", "tb1", "rst0", "rst1"]
                      keys += [("mb", bi, jq) for jq in range(4)]
                      p.op("dve", lambda e: e.memset(fdum[:, 0:1], 0.0), writes=keys)

                  load_w(p, WA, win_d[:, 0:2632], 8, "wa", "WA")
                  scale_rows(p, WA, 8, C_NG, "WA")
                  for kc in range(8):
                      p.op("pool", lambda e, kc=kc: e.tensor_copy(
                          out=WKI2[:, kc, :].rearrange("p (r c) -> p r c", r=2),
                          in_=WA[:, kc, 2560:2624].unsqueeze(1).to_broadcast([128, 2, 64])),
                          reads=[("WA", kc)], writes=[("WKI2", kc)])
                  p.op("pool", lambda e: e.memset(Va[:].rearrange("p i (h d) -> p (i h) d", d=65)[:, :, 64:65], 1.0),
                       writes=[("V", q) for q in range(4)])

                  Lps = Rot([(PS[0], "ps0"), (PS[1], "ps1")])

                  def blockA_p1(b):
                      t0 = 512 * b
                      bi = b % 2
                      QaT, sgA = QaTb[bi], sgAb[bi]
                      f32v = mbuf[bi][:].rearrange("p a c -> p (a c)").bitcast(F32)
                      ta = [f32v[:, 0:512], f32v[:, 512:1024]]
                      tb = [f32v[:, 1024:1536], f32v[:, 1536:2048]]
                      rst = [f32v[:, 2048:2560], f32v[:, 2560:3072]]
                      fence(bi)
                      common_p0(p, stA, seq, b, xnT, xst_rot, "A")
                      make_tables(p, b, tabs)
                      PJ = Rot([(PS[0], "ps0"), (PS[1], "ps1"), (PS[6], "ps6")])
                      YB = Rot([(yb[0], "yb0"), (yb[1], "yb1"), (yb[2], "yb2")])
                      SQ = Rot([(sqb[0], "sqb0"), (sqb[1], "sqb1")])
                      MS = Rot([(PS[2], "ps2"), (PS[4], "ps4")])
                      PR = Rot([(PS[3], "ps3"), (PS[5], "ps5")])
                      RS = Rot([(rst[0], "rst0"), (rst[1], "rst1")])
                      TA = Rot([(ta[0], "ta0"), (ta[1], "ta1")])
                      TB = Rot([(tb[0], "tb0"), (tb[1], "tb1")])

                      def proj_fm(col, wt=WA, wkey="WA", ncol=128):
                          pj, pk = PJ.next()
                          for kc in range(8):
                              p.op("pe", lambda e, kc=kc, pj=pj: e.matmul(
                                  pj[0:ncol, :], lhsT=wt[:, kc, col:col + ncol], rhs=xnT[:, kc, :],
                                  start=(kc == 0), stop=(kc == 7)),
                                  reads=[(wkey, kc), "xnT"], writes=[pk])
                          return pj, pk

                      def rope_finish(yt, yk, perm, tcos, tsin, dst_fn, dkey):
                          pr_, prk = PR.next()
                          ta_, tak = TA.next()
                          tb_, tbk = TB.next()
                          p.op("pe", lambda e, yt=yt, pr_=pr_: e.matmul(pr_[:, :], lhsT=perm, rhs=yt[:], start=True, stop=True),
                               reads=[yk, "mats"], writes=[prk])
                          p.op("dve", lambda e, yt=yt, ta_=ta_: e.tensor_tensor(out=ta_[:], in0=yt[:], in1=tabs[:, tcos, :],
                                                                                op=ALU.mult),
                               reads=[yk, ("tab", tcos)], writes=[tak])
                          p.op("dve", lambda e, pr_=pr_, tb_=tb_: e.tensor_tensor(out=tb_[:], in0=pr_[:, :], in1=tabs[:, tsin, :],
                                                                                  op=ALU.mult),
                               reads=[prk, ("tab", tsin)], writes=[tbk])
                          p.op("pool", lambda e, ta_=ta_, tb_=tb_: e.tensor_tensor(out=dst_fn(), in0=ta_[:], in1=tb_[:], op=ALU.add),
                               reads=[tak, tbk], writes=[dkey])

                      jobs = []
                      for kind in ("q", "k"):
                          for c in range(4):
                              col = (0 if kind == "q" else 512) + c * 128
                              gcol = cols2[:, 0:1] if kind == "q" else colap(C_AK)
                              if kind == "q":
                                  dst_fn, dkey = (lambda c=c: QaT[:, c, :]), ("QaT", bi, c)
                              else:
                                  dst_fn, dkey = (lambda c=c, t0=t0: KaT[:, c, t0:t0 + 512]), ("KaT", b)

                              def s0(st, col=col):
                                  st["pj"], st["pk"] = proj_fm(col)

                              def s1(st):
                                  pj, pk = st["pj"], st["pk"]
                                  sq_, sqk = SQ.next()
                                  ms_, msk = MS.next()
                                  st["ms"], st["msk"] = ms_, msk
                                  p.op("act", lambda e, pj=pj, sq_=sq_: e.activation(out=sq_[:], in_=pj[:, :],
                                                                                     func=AF.Square),
                                       reads=[pk], writes=[sqk])
                                  p.op("pe", lambda e, sq_=sq_, ms_=ms_: e.matmul(ms_[:, :], lhsT=bdA, rhs=sq_[:],
                                                                                  start=True, stop=True),
                                       reads=[sqk, "mats"], writes=[msk])

                              def s2(st, gcol=gcol, dst_fn=dst_fn, dkey=dkey):
                                  pj, pk, ms_, msk = st["pj"], st["pk"], st["ms"], st["msk"]
                                  rs_, rsk = RS.next()
                                  p.op("act", lambda e, ms_=ms_, rs_=rs_: e.activation(
                                      out=rs_[:], in_=ms_[:, :], func=AF.Ln, bias=cols2[:, 6:7], scale=1.0 / 64.0),
                                      reads=[msk, "cols2"], writes=[rsk])
                                  p.op("act", lambda e, rs_=rs_: e.activation(out=rs_[:], in_=rs_[:], func=AF.Exp,
                                                                              scale=-0.5),
                                       reads=[rsk], writes=[rsk])
                                  yt, yk = YB.next()
                                  p.op("dve", lambda e, pj=pj, yt=yt, gcol=gcol, rs_=rs_: e.scalar_tensor_tensor(
                                      out=yt[:], in0=pj[:, :], scalar=gcol, in1=rs_[:], op0=ALU.mult, op1=ALU.mult),
                                      reads=[pk, rsk, "cols", "cols2"], writes=[yk])
                                  rope_finish(yt, yk, permA, 0, 1, dst_fn, dkey)
                              jobs.append([s0, s1, s2])
                      for c in range(5):
                          if c < 4:
                              dst_fn, dkey = (lambda c=c: QiT[:, c, :]), ("QiT", c)
                          else:
                              dst_fn, dkey = (lambda t0=t0: KiT[:, t0:t0 + 512]), ("KiT", b)

                          def s0(st, c=c):
                              if c < 4:
                                  st["pj"], st["pk"] = proj_fm(2048 + c * 128)
                              else:
                                  st["pj"], st["pk"] = proj_fm(0, wt=WKI2, wkey="WKI2")

                          def s1(st):
                              pj, pk = st["pj"], st["pk"]
                              yt, yk = YB.next()
                              st["yt"], st["yk"] = yt, yk
                              p.op("act", lambda e, pj=pj, yt=yt: e.copy(out=yt[:], in_=pj[:, :]),
                                   reads=[pk], writes=[yk])

                          def s2(st, dst_fn=dst_fn, dkey=dkey):
                              rope_finish(st["yt"], st["yk"], permI, 2, 3, dst_fn, dkey)
                          jobs.append([s0, s1, s2])
                      run_pipeline(jobs)
                      for j in range(4):
                          i = 4 * b + j
                          for what in ("v", "g", "w"):
                              pj, pk = PJ.next()
                              c0, n = {"v": (1024, 512), "g": (1536, 512), "w": (2624, 8)}[what]
                              for kc in range(8):
                                  p.op("pe", lambda e, kc=kc, pj=pj, j=j, c0=c0, n=n: e.matmul(
                                      pj[:, 0:n], lhsT=xnT[:, kc, j * 128:(j + 1) * 128], rhs=WA[:, kc, c0:c0 + n],
                                      start=(kc == 0), stop=(kc == 7)),
                                      reads=[("WA", kc), "xnT"], writes=[pk])
                              if what == "v":
                                  p.op("dve", lambda e, pj=pj, i=i: e.tensor_copy(
                                      out=Va[:, i, :].rearrange("p (h d) -> p h d", d=65)[:, :, 0:64],
                                      in_=pj[:, :].rearrange("p (h d) -> p h d", d=64)),
                                      reads=[pk], writes=[("V", b)])
                              elif what == "g":
                                  p.op("act", lambda e, pj=pj, j=j: e.activation(out=sgA[:, j, :], in_=pj[:, :],
                                                                                 func=AF.Silu),
                                       reads=[pk], writes=[("sg", bi, j)])
                              else:
                                  p.op("dve", lambda e, pj=pj, j=j: e.tensor_copy(out=wi[:, j, :], in_=pj[:, 0:8]),
                                       reads=[pk], writes=[("wi", j)])
                                  p.op("dve", lambda e, j=j: e.tensor_scalar(out=wab[:, j, :], in0=wi[:, j, :], scalar1=-1.0,
                                                                             scalar2=None, op0=ALU.mult),
                                       reads=[("wi", j)], writes=[("wab", j)])
                                  p.op("dve", lambda e, j=j: e.tensor_tensor(out=wab[:, j, :], in0=wab[:, j, :],
                                                                             in1=wi[:, j, :], op=ALU.max),
                                       reads=[("wi", j), ("wab", j)], writes=[("wab", j)])
                                  p.op("dve", lambda e, j=j: e.tensor_scalar(out=wsg[:, j, :], in0=wi[:, j, :], scalar1=0.0,
                                                                             scalar2=2.0, op0=ALU.is_ge, op1=ALU.mult),
                                       reads=[("wi", j)], writes=[("wsg", j)])
                                  p.op("dve", lambda e, j=j: e.tensor_scalar(out=wsg[:, j, :], in0=wsg[:, j, :], scalar1=-1.0,
                                                                             scalar2=None, op0=ALU.add),
                                       reads=[("wsg", j)], writes=[("wsg", j)])

                      fence(bi)

                  def blockA_idx(b):
                      bi = b % 2
                      mb = mbuf[bi]
                      XP = Rot([(PS[i], f"ps{i}") for i in range(3, 7)])
                      ACC = Rot([(PSTf, "pst")])
                      RR = Rot([(Rt[i], f"R{i}") for i in range(4)])
                      RA = Rot([(Ra[i], f"Ra{i}") for i in range(2)])
                      for pr in range(2):
                          tiles = []
                          for q in range(2):
                              j = 2 * pr + q
                              i = 4 * b + j
                              L2 = 128 * (i + 1)
                              L1 = L2 - 64
                              sc, sck = SC[q], f"sc{q}"
                              tiles.append((j, i, L1, L2, sc, sck))
                              for dq, hh in enumerate((2, 3, 6, 7)):
                                  p.op("dve", lambda e, dq=dq, hh=hh, j=j: e.tensor_scalar(
                                      out=Dg[:, dq, :], in0=ident, scalar1=wsg[:, j, hh:hh + 1], scalar2=None,
                                      op0=ALU.mult),
                                      reads=["mats", ("wsg", j)], writes=[("Dg", dq)])
                              for sbk in range(b + 1):
                                  wd = 512 if sbk < b else 128 * (j + 1)
                                  acc, acck = ACC.next()
                                  pend = []

                                  def issue_x(h, j=j, sbk=sbk, wd=wd):
                                      c, base = h // 2, (h % 2) * 64
                                      xp, xk = XP.next()
                                      p.op("pe", lambda e, xp=xp, c=c, base=base: e.matmul(
                                          xp[:, 0:wd], lhsT=QiT[base:base + 64, c, j * 128:(j + 1) * 128],
                                          rhs=KiT[base:base + 64, sbk * 512:sbk * 512 + wd], start=True, stop=True),
                                          reads=[("QiT", c), ("KiT", sbk)], writes=[xk])
                                      return xp, xk
                                  for h0 in range(4):
                                      pend.append(issue_x(h0))
                                  nacc = 0
                                  for m in range(4):
                                      xpe, xke = pend.pop(0)
                                      xpo, xko = pend.pop(0)
                                      he, ho = 2 * m, 2 * m + 1
                                      if m % 2 == 0:
                                          r, rk = RR.next()
                                          r2, rk2 = RR.next()
                                          for (xp_, xk_, r_, rk_, h_) in ((xpe, xke, r, rk, he), (xpo, xko, r2, rk2, ho)):
                                              p.op("dve", lambda e, xp=xp_, r=r_, h=h_, j=j, wd=wd: e.tensor_scalar(
                                                  out=r[:, 0:wd], in0=xp[:, 0:wd], scalar1=0.0, scalar2=wi[:, j, h:h + 1],
                                                  op0=ALU.max, op1=ALU.mult),
                                                  reads=[xk_, ("wi", j)], writes=[rk_])
                                          if 2 * m + 4 < 8:
                                              pend.append(issue_x(2 * m + 4))
                                              pend.append(issue_x(2 * m + 5))
                                          p.op("pool", lambda e, r=r, r2=r2, wd=wd: e.tensor_tensor(
                                              out=r[:, 0:wd], in0=r[:, 0:wd], in1=r2[:, 0:wd], op=ALU.add),
                                              reads=[rk, rk2], writes=[rk])
                                          p.op("pe", lambda e, acc=acc, r=r, nacc=nacc, wd=wd: e.matmul(
                                              acc[:, 0:wd], lhsT=ident, rhs=r[:, 0:wd], start=(nacc == 0), stop=False),
                                              reads=[rk, "mats"], writes=[acck])
                                          nacc += 1
                                      else:
                                          r, rk = RA.next()
                                          r2, rk2 = RA.next()
                                          for (xp_, xk_, r_, rk_, h_) in ((xpe, xke, r, rk, he), (xpo, xko, r2, rk2, ho)):
                                              p.op("act", lambda e, xp=xp_, r=r_, h=h_, j=j, wd=wd: e.activation(
                                                  out=r[:, 0:wd], in_=xp[:, 0:wd], func=AF.Relu, scale=wab[:, j, h:h + 1]),
                                                  reads=[xk_, ("wab", j)], writes=[rk_])
                                          if 2 * m + 4 < 8:
                                              pend.append(issue_x(2 * m + 4))
                                              pend.append(issue_x(2 * m + 5))
                                          for (r_, rk_, h_) in ((r, rk, he), (r2, rk2, ho)):
                                              dq = (h_ // 4) * 2 + (h_ % 2)
                                              p.op("pe", lambda e, acc=acc, r=r_, nacc=nacc, wd=wd, dq=dq, m=m, h=h_: e.matmul(
                                                  acc[:, 0:wd], lhsT=Dg[:, dq, :], rhs=r[:, 0:wd], start=(nacc == 0),
                                                  stop=(h == 7)),
                                                  reads=[rk_, ("Dg", dq)], writes=[acck])
                                              nacc += 1
                                  p.op("act", lambda e, acc=acc, sbk=sbk, wd=wd, sc=sc: e.copy(
                                      out=sc[:, sbk * 512:sbk * 512 + wd], in_=acc[:, 0:wd]),
                                      reads=[acck], writes=[sck])
                                  yield ("relu", 7.0 * wd / 512.0)
                              p.op("pool", lambda e, L1=L1, L2=L2, sc=sc: e.memset(sc[0:64, L1:L2], -1e30),
                                   reads=[], writes=[sck])
                              if "scores" in dbg_d and seq == 0:
                                  p.dma("sp", "dbg", lambda e, i=i, L2=L2, sc=sc: e.dma_start(
                                      out=dbg_d["scores"][i, :, 0:L2], in_=sc[:, 0:L2]), reads=[sck])
                          LO2, HI2, RNG2, MID2, CNT2, GE2, T2 = [bis[:, 2 * q:2 * q + 2] for q in range(7)]
                          SBc = bis[:, 14:15]
                          if tiles[0][1] < 2:
                              p.op("dve", lambda e: e.memset(LO2, -5e29), writes=["lo0", "lo1"])
                          else:
                              for q, (j, i, L1, L2, sc, sck) in enumerate(tiles):
                                  p.op("dve", lambda e, L1=L1, sc=sc, q=q: e.tensor_reduce(
                                      out=LO2[:, q:q + 1], in_=sc[:, 0:L1], axis=AX.X, op=ALU.min),
                                      reads=[sck], writes=[f"lo{q}"])
                                  p.op("dve", lambda e, L2=L2, sc=sc, q=q: e.tensor_reduce(
                                      out=HI2[:, q:q + 1], in_=sc[:, 0:L2], axis=AX.X, op=ALU.max),
                                      reads=[sck], writes=["hi"])
                              p.op("dve", lambda e: e.tensor_tensor(out=RNG2, in0=HI2, in1=LO2, op=ALU.subtract),
                                   reads=["hi", "lo0", "lo1"], writes=["rng"])
                              (jA, iA, L1A, L2A, scA, sckA), (jB, iB, L1B, L2B, scB, sckB) = tiles
                              for it in range(1, NIT + 1):
                                  cst = float(2.0 ** (-it))
                                  p.op("dve", lambda e, cst=cst: e.scalar_tensor_tensor(
                                      out=MID2, in0=RNG2, scalar=cst, in1=LO2, op0=ALU.mult, op1=ALU.add),
                                      reads=["rng", "lo0", "lo1"], writes=["mid"])
                                  p.op("act", lambda e, jB=jB, L2B=L2B, scB=scB: e.activation(
                                      out=mb[:, jB, 0:L2B], in_=scB[:, 0:L2B], func=AF.Sign, scale=-1.0,
                                      bias=MID2[:, 1:2], accum_out=SBc),
                                      reads=[sckB, "mid"], writes=[("mb", bi, jB), "sb"])
                                  p.op("dve", lambda e, jA=jA, L2A=L2A, scA=scA: e.tensor_scalar(
                                      out=mb[:, jA, 0:L2A], in0=scA[:, 0:L2A], scalar1=MID2[:, 0:1], scalar2=0.0,
                                      op0=ALU.is_ge, op1=ALU.add, accum_out=CNT2[:, 0:1]),
                                      reads=[sckA, "mid"], writes=[("mb", bi, jA), "cnt0"])
                                  p.op("dve", lambda e, cst=cst: e.tensor_scalar(
                                      out=GE2[:, 0:1], in0=CNT2[:, 0:1], scalar1=float(TOPK) - 0.75, scalar2=cst,
                                      op0=ALU.is_ge, op1=ALU.mult),
                                      reads=["cnt0"], writes=["ge0"])
                                  p.op("dve", lambda e, cst=cst, L2B=L2B: e.tensor_scalar(
                                      out=GE2[:, 1:2], in0=SBc, scalar1=float(L2B - 2 * TOPK) + 1.5, scalar2=cst,
                                      op0=ALU.is_le, op1=ALU.mult),
                                      reads=["sb"], writes=["ge1"])
                                  p.op("dve", lambda e: e.scalar_tensor_tensor(
                                      out=LO2[:, 0:1], in0=RNG2[:, 0:1], scalar=GE2[:, 0:1], in1=LO2[:, 0:1],
                                      op0=ALU.mult, op1=ALU.add),
                                      reads=["rng", "ge0", "lo0"], writes=["lo0"])
                                  p.op("dve", lambda e: e.scalar_tensor_tensor(
                                      out=LO2[:, 1:2], in0=RNG2[:, 1:2], scalar=GE2[:, 1:2], in1=LO2[:, 1:2],
                                      op0=ALU.mult, op1=ALU.add),
                                      reads=["rng", "ge1", "lo1"], writes=["lo1"])
                                  yield ("bis", 1.2 + L2B / 800.0)
                          for q, (j, i, L1, L2, sc, sck) in enumerate(tiles):
                              p.op("dve", lambda e, j=j, L2=L2, sc=sc, q=q: e.tensor_scalar(
                                  out=mb[:, j, 0:L2], in0=sc[:, 0:L2], scalar1=LO2[:, q:q + 1], scalar2=NEG,
                                  op0=ALU.is_lt, op1=ALU.mult),
                                  reads=[sck, f"lo{q}"], writes=[("mb", bi, j)])
                              if "thr" in dbg_d and seq == 0:
                                  p.dma("sp", "dbg", lambda e, i=i, q=q: e.dma_start(
                                      out=dbg_d["thr"][i, :, 0:1], in_=LO2[:, q:q + 1]), reads=[f"lo{q}"])

                          yield ("end", 0.5)

                  def blockA_att(b, alone=False):
                      bi = b % 2
                      mb, QaT, sgA = mbuf[bi], QaTb[bi], sgAb[bi]
                      PTs = Rot([(PTt[i], f"PT{i}") for i in range(3)])
                      Ops = Rot([(PS[2], "ps2")])
                      LpsL = Lps
                      if alone:
                          LpsL = Rot([(PS[0], "ps0"), (PS[1], "ps1"), (PS[3], "ps3"), (PS[4], "ps4")])
                          Ops = Rot([(PS[2], "ps2"), (PS[5], "ps5"), (PS[6], "ps6")])

                      def qk_A(h, k, col0, lp, lk):
                          c, base = h // 2, (h % 2) * 64
                          kk = k - 4 * b
                          N = 512 - col0
                          for jj in range(max(kk, 0), 4):
                              c0 = jj * 128 - col0
                              p.op("pe", lambda e, lp=lp, jj=jj, k=k, c0=c0, kk=kk: e.matmul(
                                  lp[:, c0:c0 + 128], lhsT=mb[:, jj, k * 128:(k + 1) * 128], rhs=ident,
                                  start=(jj == max(kk, 0)), stop=False),
                                  reads=[("mb", bi, jj), "mats"], writes=[lk])
                          p.op("pe", lambda e, lp=lp, k=k, c=c, base=base, col0=col0, N=N: e.matmul(
                              lp[:, 0:N], lhsT=KaT[base:base + 64, c, k * 128:(k + 1) * 128],
                              rhs=QaT[base:base + 64, c, col0:512], start=False, stop=True),
                              reads=[("KaT", k // 4), ("QaT", bi, c)], writes=[lk])

                      def exp_A(h, k, col0, lp, lk, pt, pk, N):
                          p.op("act", lambda e, lp=lp, pt=pt, N=N: e.activation(out=pt[:, 0:N], in_=lp[:, 0:N],
                                                                                 func=AF.Exp),
                               reads=[lk], writes=[pk])

                      def fin_A(h, O, ok):
                          Ov = O[:, 0:260].rearrange("p (j d) -> p j d", d=65)
                          p.op("dve", lambda e, Ov=Ov: e.reciprocal(out=rinv[:, 0:4], in_=Ov[:, :, 64]),
                               reads=[ok], writes=["rinv"])
                          for jj in range(4):
                              p.op("dve", lambda e, Ov=Ov, jj=jj, h=h, b=b: e.scalar_tensor_tensor(
                                  out=mixed[:, 4 * b + jj, h * 64:(h + 1) * 64], in0=Ov[:, jj, 0:64],
                                  scalar=rinv[:, jj:jj + 1], in1=sgA[:, jj, h * 64:(h + 1) * 64],
                                  op0=ALU.mult, op1=ALU.mult),
                                  reads=[ok, "rinv", ("sg", bi, jj)], writes=[("mixed", 4 * b + jj)])

                      yield from attention_units_gen(p, b, 8, qk_A, exp_A, None,
                                                     lambda k, h: Va[:, k, h * 65:(h + 1) * 65], fin_A, PTs, LpsL, Ops)

                  def merge(ga, gi, n_att, n_bis):
                      done = 0
                      seen = 0
                      alive = True
                      for tag, _c in gi:
                          if tag == "bis" and alive:
                              seen += 1
                              target = (n_att * seen + n_bis - 1) // n_bis
                              while done < target:
                                  try:
                                      next(ga)
                                      done += 1
                                  except StopIteration:
                                      alive = False
                                      break
                      for _ in ga:
                          pass

                  def drain(g):
                      for _ in g:
                          pass

                  blockA_p1(0)
                  if stage >= 2:
                      drain(blockA_idx(0))
                  for b in range(4):
                      if b + 1 < 4:
                          blockA_p1(b + 1)
                      if stage >= 3:
                          if b + 1 < 4:
                              merge(blockA_att(b), blockA_idx(b + 1), 8 * (4 * b + 4), 2 * NIT)
                          else:
                              drain(blockA_att(b, alone=True))
                      elif stage >= 2 and b + 1 < 4:
                          drain(blockA_idx(b + 1))

                  if seq == 0:
                      QaT, sgA = QaTb[1], sgAb[1]
                      dump = {"KaT": (KaT, [128, 4 * 2048]), "KiT": (KiT, [128, 2048]), "QaT": (QaT, [128, 4 * 512]),
                              "QiT": (QiT, [128, 4 * 512]), "sgA": (sgA, [128, 4 * 512]), "Va": (Va, [128, 16 * 520]),
                              "mixedA": (mixed, [128, 16 * 1024])}
                      for name, (tl, shp) in dump.items():
                          if name in dbg_d:
                              flat = tl[:] if len(tl.shape) == 2 else tl[:].rearrange(
                                  "p a b -> p (a b)") if len(tl.shape) == 3 else tl[:]
                              nfree = shp[1]
                              for c0 in range(0, nfree, 2048):
                                  c1 = min(nfree, c0 + 2048)
                                  p.dma("pool", "dbgp", lambda e, flat=flat, c0=c0, c1=c1, name=name: e.dma_start(
                                      out=dbg_d[name][:, c0:c1], in_=flat[:, c0:c1]),
                                      reads=list(p.lw.keys()))
                  p.finish()
                  p.emit()
            if stage <= 3:
                continue
            stB = ExitStack()
            with stB:
                p = Prog(nc, stB)
                Bt = lambda name, shape, dt: sb(f"B{seq}_{name}", shape, dt, stB)
                WB = Bt("WB", [128, 8, 1184], BF16)
                WUQ = Bt("WUQ", [128, 3, 768], BF16)
                WUKV = Bt("WUKV", [128, 2, 1024], BF16)
                xnT = Bt("xnT", [128, 8, 512], BF16)
                xst = [Bt(f"xs{i}", [128, 1024], F32) for i in range(2)]
                xst_rot = Rot([(xst[i], f"xs{i}", f"xs{i}") for i in range(2)])
                p.junk = Bt("junk", [128, 1024], BF16)
                p.ss = Bt("ss", [128, 1], F32)
                p.rs = Bt("rs", [128, 2], F32)
                p.xb = Bt("xb", [128, 1024], BF16)
                tabs = Bt("tabs", [128, 4, 512], F32)
                KbT = Bt("KbT", [128, 8, 2048], BF16)
                Vb = Bt("Vb", [128, 16, 8 * 65], BF16)
                rk = Bt("rk", [128, 16, 8], F32)
                QbT = Bt("QbT", [128, 8, 512], BF16)
                sgB = Bt("sgB", [128, 4, 512], BF16)
                cqs = Bt("cqs", [128, 3, 512], BF16)
                cqn = Bt("cqn", [128, 3, 512], BF16)
                ckvs = Bt("ckvs", [128, 2, 512], BF16)
                ckvn = Bt("ckvn", [128, 2, 512], BF16)
                sqb = [Bt(f"sqb{i}", [128, 512], BF16) for i in range(2)]
                rst = [Bt(f"rst{i}", [128, 512], F32) for i in range(2)]
                ta = [Bt(f"ta{i}", [128, 512], F32) for i in range(2)]
                tb = [Bt(f"tb{i}", [128, 512], F32) for i in range(2)]
                yb = [Bt(f"yb{i}", [128, 512], BF16) for i in range(3)]
                krs = Bt("krs", [128, 512], BF16)
                kro = Bt("kro", [128, 512], BF16)
                PTt = [Bt(f"PT{i}", [128, 512], BF16) for i in range(3)]
                ssr = Bt("ssr", [128, 4], F32)
                ssn = Bt("ssn", [128, 8], F32)
                t8 = Bt("t8", [128, 16], F32)
                sq32 = Bt("sq32", [128, 256], F32)
                rinv = Bt("rinv", [128, 4], F32)

                load_w(p, WB, win_d[:, 2632:3816], 8, "wb", "WB")
                scale_rows(p, WB, 8, C_NG, "WB")
                load_w(p, WUQ, wuq_d[:, :], 3, "wuq", "WUQ")
                scale_rows(p, WUQ, 3, C_CQ, "WUQ")
                load_w(p, WUKV, wukv_d[:, :], 2, "wukv", "WUKV")
                scale_rows(p, WUKV, 2, C_CKV, "WUKV")
                p.op("pool", lambda e: e.memset(Vb[:].rearrange("p i (h d) -> p (i h) d", d=65)[:, :, 64:65], 1.0),
                     writes=[("V", q) for q in range(4)])
                p.op("pool", lambda e: e.memset(krs[:], 0.0), writes=["krs"])

                Lps = Rot([(PS[i], f"ps{i}") for i in range(4)])
                for b in range(4):
                    t0 = 512 * b
                    if bstage <= -1:
                        continue
                    common_p0(p, stB, seq, b, xnT, xst_rot, "B")
                    if bstage <= 0:
                        continue
                    if 'notab' not in VAR:
                        p.dma("sp", "tab", lambda e, b=b: e.dma_start(out=tabs[:, 2:4, :], in_=tabs_d[b, :, 2:4, :]),
                              writes=[("tab", 2), ("tab", 3)])
                    PJ = Rot([(PS[0], "ps0"), (PS[1], "ps1"), (PS[6], "ps6")])
                    YB = Rot([(yb[0], "yb0"), (yb[1], "yb1"), (yb[2], "yb2")])
                    SQ = Rot([(sqb[0], "sqb0"), (sqb[1], "sqb1")])
                    MS = Rot([(PS[2], "ps2"), (PS[4], "ps4")])
                    PR = Rot([(PS[3], "ps3"), (PS[5], "ps5")])
                    RS = Rot([(rst[0], "rst0"), (rst[1], "rst1")])
                    TA = Rot([(ta[0], "ta0"), (ta[1], "ta1")])
                    TB = Rot([(tb[0], "tb0"), (tb[1], "tb1")])

                    def proj_fm(col, ncol=128):
                        pj, pk = PJ.next()
                        for kc in range(8):
                            p.op("pe", lambda e, kc=kc, pj=pj: e.matmul(
                                pj[0:ncol, :], lhsT=WB[:, kc, col:col + ncol], rhs=xnT[:, kc, :],
                                start=(kc == 0), stop=(kc == 7)),
                                reads=[("WB", kc), "xnT"], writes=[pk])
                        return pj, pk

                    jobs = []
                    for (nch, cbase, raw, nrm, rawk, nrmk, dim) in ((3, 0, cqs, cqn, "cqs", "cqn", 384.0),
                                                                    (2, 384, ckvs, ckvn, "ckvs", "ckvn", 256.0)):
                        grp = {}
                        for c in range(nch):
                            def s0(st, c=c, cbase=cbase):
                                st["pj"], st["pk"] = proj_fm(cbase + c * 128)

                            def s1(st, c=c, nch=nch, grp=grp, raw=raw, rawk=rawk):
                                pj, pk = st["pj"], st["pk"]
                                if c == 0:
                                    grp["ms"], grp["msk"] = MS.next()
                                ms_, msk = grp["ms"], grp["msk"]
                                sq_, sqk = SQ.next()
                                p.op("act", lambda e, pj=pj, sq_=sq_: e.activation(out=sq_[:], in_=pj[:, :],
                                                                                   func=AF.Square),
                                     reads=[pk], writes=[sqk])
                                p.op("pe", lambda e, c=c, nch=nch, ms_=ms_, sq_=sq_: e.matmul(
                                    ms_[:, :], lhsT=ones, rhs=sq_[:], start=(c == 0), stop=(c == nch - 1)),
                                    reads=[sqk, "mats"], writes=[msk])
                                p.op("dve", lambda e, pj=pj, raw=raw, c=c: e.tensor_copy(out=raw[:, c, :], in_=pj[:, :]),
                                     reads=[pk, sqk], writes=[(rawk, c)])

                            def s2(st, nch=nch, grp=grp, raw=raw, nrm=nrm, rawk=rawk, nrmk=nrmk, dim=dim):
                                ms_, msk = grp["ms"], grp["msk"]
                                rs_, rsk = RS.next()
                                p.op("act", lambda e, dim=dim, ms_=ms_, rs_=rs_: e.activation(
                                    out=rs_[:], in_=ms_[:, :], func=AF.Ln, bias=cols2[:, 6:7], scale=1.0 / dim),
                                    reads=[msk, "cols2"], writes=[rsk])
                                p.op("act", lambda e, rs_=rs_: e.activation(out=rs_[:], in_=rs_[:], func=AF.Exp,
                                                                            scale=-0.5),
                                     reads=[rsk], writes=[rsk])
                                for cc in range(nch):
                                    p.op("pool", lambda e, raw=raw, nrm=nrm, cc=cc, rs_=rs_: e.tensor_tensor(
                                        out=nrm[:, cc, :], in0=raw[:, cc, :], in1=rs_[:], op=ALU.mult),
                                        reads=[(rawk, cc), rsk], writes=[nrmk])
                            jobs.append([s0, s1, s2] if c == nch - 1 else [s0, s1])
                    run_pipeline(jobs)
                    if bstage <= 1:
                        continue
                    pj, pk = proj_fm(576, ncol=96)
                    p.op("dve", lambda e, pj=pj: e.tensor_scalar(out=krs[64:96, :], in0=pj[64:96, :],
                                                                 scalar1=colap(C_BK, 64, 96), scalar2=None,
                                                                 op0=ALU.mult),
                         reads=[pk, "cols"], writes=["krs"])
                    pr_, prk = PR.next()
                    ta_, tak = TA.next()
                    tb_, tbk = TB.next()
                    p.op("pe", lambda e, pr_=pr_: e.matmul(pr_[0:96, :], lhsT=permI[0:96, 0:96], rhs=krs[0:96, :],
                                                           start=True, stop=True),
                         reads=["krs", "mats"], writes=[prk])
                    p.op("pool", lambda e, ta_=ta_: e.tensor_tensor(out=ta_[64:96, :], in0=krs[64:96, :],
                                                                    in1=tabs[64:96, 2, :], op=ALU.mult),
                         reads=["krs", ("tab", 2)], writes=[tak])
                    p.op("dve", lambda e, pr_=pr_, tb_=tb_: e.tensor_tensor(out=tb_[64:96, :], in0=pr_[64:96, :],
                                                                            in1=tabs[64:96, 3, :], op=ALU.mult),
                         reads=[prk, ("tab", 3)], writes=[tbk])
                    p.op("pool", lambda e, ta_=ta_, tb_=tb_: e.tensor_tensor(out=kro[64:96, :], in0=ta_[64:96, :],
                                                                             in1=tb_[64:96, :], op=ALU.add),
                         reads=[tak, tbk], writes=["kro"])
                    for hh in range(8):
                        p.op("dve", lambda e, t0=t0, hh=hh: e.tensor_copy(out=KbT[64:96, hh, t0:t0 + 512],
                                                                          in_=kro[64:96, :]),
                             reads=["kro"], writes=[("KbT", b)])
                    if bstage <= 2:
                        continue
                    for j in range(4):
                        for what in ("r", "g"):
                            pj, pk = PJ.next()
                            c0, n = {"r": (640, 32), "g": (672, 512)}[what]
                            for kc in range(8):
                                p.op("pe", lambda e, kc=kc, pj=pj, j=j, c0=c0, n=n: e.matmul(
                                    pj[:, 0:n], lhsT=xnT[:, kc, j * 128:(j + 1) * 128], rhs=WB[:, kc, c0:c0 + n],
                                    start=(kc == 0), stop=(kc == 7)),
                                    reads=[("WB", kc), "xnT"], writes=[pk])
                            if what == "r":
                                p.op("act", lambda e, pj=pj, j=j: e.activation(
                                    out=sq32[:, 0:32], in_=pj[:, 0:32], func=AF.Square, accum_out=ssr[:, j:j + 1]),
                                    reads=[pk], writes=["sq32", ("ssr", j)])
                            else:
                                p.op("act", lambda e, pj=pj, j=j: e.activation(out=sgB[:, j, :], in_=pj[:, :],
                                                                               func=AF.Silu),
                                     reads=[pk], writes=[("sg", j)])
                    if bstage <= 3:
                        continue
                    jobs = []
                    for h in range(8):
                        def s0(st, h=h):
                            pj, pk = PJ.next()
                            st["pj"], st["pk"] = pj, pk
                            for c in range(3):
                                p.op("pe", lambda e, c=c, pj=pj, h=h: e.matmul(
                                    pj[0:96, :], lhsT=WUQ[:, c, h * 96:(h + 1) * 96], rhs=cqn[:, c, :],
                                    start=(c == 0), stop=(c == 2)),
                                    reads=[("WUQ", c), "cqn"], writes=[pk])

                        def s1(st):
                            pj, pk = st["pj"], st["pk"]
                            sq_, sqk = SQ.next()
                            ms_, msk = MS.next()
                            st["ms"], st["msk"] = ms_, msk
                            p.op("act", lambda e, pj=pj, sq_=sq_: e.activation(out=sq_[0:96, :], in_=pj[0:96, :],
                                                                               func=AF.Square),
                                 reads=[pk], writes=[sqk])
                            p.op("pe", lambda e, sq_=sq_, ms_=ms_: e.matmul(ms_[0:96, :], lhsT=ones[0:96, 0:96],
                                                                            rhs=sq_[0:96, :], start=True, stop=True),
                                 reads=[sqk, "mats"], writes=[msk])

                        def s2(st, h=h):
                            pj, pk, ms_, msk = st["pj"], st["pk"], st["ms"], st["msk"]
                            rs_, rsk = RS.next()
                            pr_, prk = PR.next()
                            ta_, tak = TA.next()
                            tb_, tbk = TB.next()
                            p.op("act", lambda e, ms_=ms_, rs_=rs_: e.activation(
                                out=rs_[0:96, :], in_=ms_[0:96, :], func=AF.Ln, bias=cols2[0:96, 6:7],
                                scale=1.0 / 96.0),
                                reads=[msk, "cols2"], writes=[rsk])
                            p.op("act", lambda e, rs_=rs_: e.activation(out=rs_[0:96, :], in_=rs_[0:96, :],
                                                                        func=AF.Exp, scale=-0.5),
                                 reads=[rsk], writes=[rsk])
                            yt, yk = YB.next()
                            p.op("dve", lambda e, pj=pj, yt=yt, rs_=rs_: e.scalar_tensor_tensor(
                                out=yt[0:96, :], in0=pj[0:96, :], scalar=cols2[0:96, 1:2], in1=rs_[0:96, :],
                                op0=ALU.mult, op1=ALU.mult),
                                reads=[pk, rsk, "cols2"], writes=[yk])
                            p.op("dve", lambda e, yt=yt, h=h: e.tensor_copy(out=QbT[0:64, h, :], in_=yt[0:64, :]),
                                 reads=[yk], writes=[("QbT", h)])
                            p.op("pe", lambda e, yt=yt, pr_=pr_: e.matmul(pr_[0:96, :], lhsT=permI[0:96, 0:96],
                                                                          rhs=yt[0:96, :], start=True, stop=True),
                                 reads=[yk, "mats"], writes=[prk])
                            p.op("pool", lambda e, yt=yt, ta_=ta_: e.tensor_tensor(
                                out=ta_[64:96, :], in0=yt[64:96, :], in1=tabs[64:96, 2, :], op=ALU.mult),
                                reads=[yk, ("tab", 2)], writes=[tak])
                            p.op("dve", lambda e, pr_=pr_, tb_=tb_: e.tensor_tensor(
                                out=tb_[64:96, :], in0=pr_[64:96, :], in1=tabs[64:96, 3, :], op=ALU.mult),
                                reads=[prk, ("tab", 3)], writes=[tbk])
                            p.op("pool", lambda e, h=h, ta_=ta_, tb_=tb_: e.tensor_tensor(
                                out=QbT[64:96, h, :], in0=ta_[64:96, :], in1=tb_[64:96, :], op=ALU.add),
                                reads=[tak, tbk], writes=[("QbT", h)])
                        jobs.append([s0, s1, s2])
                    run_pipeline(jobs)
                    if bstage <= 4:
                        continue
                    for j in range(4):
                        i = 4 * b + j
                        for hf in range(2):
                            pj, pk = PJ.next()
                            for c in range(2):
                                p.op("pe", lambda e, c=c, pj=pj, j=j, hf=hf: e.matmul(
                                    pj[:, :], lhsT=ckvn[:, c, j * 128:(j + 1) * 128],
                                    rhs=WUKV[:, c, hf * 512:(hf + 1) * 512], start=(c == 0), stop=(c == 1)),
                                    reads=[("WUKV", c), "ckvn"], writes=[pk])
                            pjv = pj[:, :].rearrange("p (h d) -> p h d", d=128)
                            p.op("act", lambda e, pjv=pjv, i=i, hf=hf: e.copy(
                                out=Vb[:, i, :].rearrange("p (h d) -> p h d", d=65)[:, hf * 4:hf * 4 + 4, 0:64],
                                in_=pjv[:, :, 64:128]),
                                reads=[pk], writes=[("V", b)])
                            p.op("act", lambda e, pjv=pjv: e.activation(
                                out=sq32[:].rearrange("p (h d) -> p h d", d=64), in_=pjv[:, :, 0:64], func=AF.Square),
                                reads=[pk], writes=["sq32"])
                            p.op("dve", lambda e, hf=hf: e.tensor_reduce(
                                out=ssn[:, hf * 4:hf * 4 + 4], in_=sq32[:].rearrange("p (h d) -> p h d", d=64),
                                axis=AX.X, op=ALU.add),
                                reads=["sq32"], writes=["ssn"])
                        p.op("dve", lambda e, j=j: e.tensor_scalar(out=t8[:, 0:8], in0=ssn[:, 0:8],
                                                                   scalar1=ssr[:, j:j + 1], scalar2=1.0 / 96.0,
                                                                   op0=ALU.add, op1=ALU.mult),
                             reads=["ssn", ("ssr", j)], writes=["t8"])
                        p.op("act", lambda e: e.activation(out=t8[:, 8:16], in_=t8[:, 0:8], func=AF.Ln,
                                                           bias=cols2[:, 6:7], scale=1.0),
                             reads=["t8", "cols2"], writes=["t8b"])
                        p.op("act", lambda e, i=i: e.activation(out=rk[:, i, :], in_=t8[:, 8:16], func=AF.Exp,
                                                                scale=-0.5),
                             reads=["t8b"], writes=[("rk", b)])
                    if bstage <= 5:
                        continue
                    for h in range(8):
                        pj, pk = PJ.next()
                        for c in range(2):
                            p.op("pe", lambda e, c=c, pj=pj, h=h: e.matmul(
                                pj[0:64, :], lhsT=WUKV[:, c, h * 128:h * 128 + 64], rhs=ckvn[:, c, :],
                                start=(c == 0), stop=(c == 1)),
                                reads=[("WUKV", c), "ckvn"], writes=[pk])
                        p.op("dve", lambda e, pj=pj, h=h, t0=t0: e.tensor_scalar(
                            out=KbT[0:64, h, t0:t0 + 512], in0=pj[0:64, :], scalar1=colap(C_BK, 0, 64),
                            scalar2=None, op0=ALU.mult),
                            reads=[pk, "cols"], writes=[("KbT", b)])

                    if bstage <= 6:
                        continue
                    PTs = Rot([(PTt[i], f"PT{i}") for i in range(3)])
                    Ops = Rot([(PS[4], "ps4"), (PS[5], "ps5"), (PS[6], "ps6")])

                    def qk_B(h, k, col0, lp, lk):
                        N = 512 - col0
                        p.op("pe", lambda e, lp=lp, h=h, k=k, col0=col0, N=N: e.matmul(
                            lp[:, 0:N], lhsT=KbT[0:96, h, k * 128:(k + 1) * 128], rhs=QbT[0:96, h, col0:512],
                            start=True, stop=True),
                            reads=[("KbT", k // 4), ("QbT", h)], writes=[lk])

                    def exp_B(h, k, col0, lp, lk, pt, pk, N):
                        p.op("act", lambda e, lp=lp, pt=pt, N=N, k=k, h=h: e.activation(
                            out=pt[:, 0:N], in_=lp[:, 0:N], func=AF.Exp, scale=rk[:, k, h:h + 1]),
                            reads=[lk, ("rk", k // 4)], writes=[pk])

                    def post_B(h, k, col0, pt, pk):
                        p.op("pool", lambda e, pt=pt: e.memset(pt[64:128, 0:64], 0.0), writes=[pk])

                    def fin_B(h, O, ok, b=b):
                        Ov = O[:, 0:260].rearrange("p (j d) -> p j d", d=65)
                        p.op("dve", lambda e, Ov=Ov: e.reciprocal(out=rinv[:, 0:4], in_=Ov[:, :, 64]),
                             reads=[ok], writes=["rinv"])
                        for jj in range(4):
                            p.op("dve", lambda e, Ov=Ov, jj=jj, h=h, b=b: e.scalar_tensor_tensor(
                                out=mixed[:, 4 * b + jj, 512 + h * 64:512 + (h + 1) * 64], in0=Ov[:, jj, 0:64],
                                scalar=rinv[:, jj:jj + 1], in1=sgB[:, jj, h * 64:(h + 1) * 64],
                                op0=ALU.mult, op1=ALU.mult),
                                reads=[ok, "rinv", ("sg", jj)], writes=[("mixed", 4 * b + jj)])

                    attention_units(p, b, 8, qk_B, exp_B, post_B,
                                    lambda k, h: Vb[:, k, h * 65:(h + 1) * 65], fin_B, PTs, Lps, Ops)
                if seq == 0 and "mixedB" in dbg_d:
                    flat = mixed[:].rearrange("p a b -> p (a b)")
                    for c0 in range(0, 16384, 2048):
                        p.dma("pool", "dbgp", lambda e, c0=c0: e.dma_start(
                            out=dbg_d["mixedB"][:, c0:c0 + 2048], in_=flat[:, c0:c0 + 2048]),
                            reads=list(p.lw.keys()))
                p.finish()
                p.emit()
            if stage <= 4:
                continue
            stO = ExitStack()
            with stO:
                p = Prog(nc, stO)
                Ot = lambda name, shape, dt: sb(f"O{seq}_{name}", shape, dt, stO)
                WO = Ot("WO", [128, 8, 1024], BF16)
                mT = [Ot(f"mT{i}", [128, 8, 128], BF16) for i in range(2)]
                xst = [Ot(f"xs{i}", [128, 1024], F32) for i in range(2)]
                ost = [Ot(f"os{i}", [128, 1024], F32) for i in range(2)]
                load_w(p, WO, wout_d[:, :], 8, "wo", "WO")
                PJ = Rot([(PS[i], f"ps{i}") for i in range(4)])
                for i in range(16):
                    mt, mk = mT[i % 2], f"mT{i % 2}"
                    xs, xk = xst[i % 2], f"xs{i % 2}"
                    os_, okk = ost[i % 2], f"os{i % 2}"
                    p.dma("sp", xk, lambda e, xs=xs, i=i: e.dma_start(out=xs[:], in_=x_d[seq, i * 128:(i + 1) * 128, :]),
                          writes=[xk])
                    for g in range(2):
                        for q in range(4):
                            kc = 4 * g + q
                            p.op("pe", lambda e, kc=kc, q=q, i=i: e.transpose(
                                out=PST[:, q * 128:(q + 1) * 128], in_=mixed[:, i, kc * 128:(kc + 1) * 128],
                                identity=ident),
                                reads=["mats"], writes=["pst"])
                        if g == 0:
                            p.op("act", lambda e, g=g, mt=mt: e.copy(
                                out=mt[:, 4 * g:4 * g + 4, :], in_=PST[:, 0:512].rearrange("p (q t) -> p q t", t=128)),
                                reads=["pst"], writes=[mk])
                        else:
                            p.op("dve", lambda e, g=g, mt=mt: e.tensor_copy(
                                out=mt[:, 4 * g:4 * g + 4, :], in_=PST[:, 0:512].rearrange("p (q t) -> p q t", t=128)),
                                reads=["pst"], writes=[mk])
                    for nh in range(2):
                        pj, pk = PJ.next()
                        for kc in range(8):
                            p.op("pe", lambda e, kc=kc, pj=pj, mt=mt, nh=nh: e.matmul(
                                pj[:, :], lhsT=mt[:, kc, :], rhs=WO[:, kc, nh * 512:(nh + 1) * 512],
                                start=(kc == 0), stop=(kc == 7)),
                                reads=[mk, ("WO", kc)], writes=[pk])
                        p.op("dve", lambda e, pj=pj, os_=os_, xs=xs, nh=nh: e.tensor_tensor(
                            out=os_[:, nh * 512:(nh + 1) * 512], in0=pj[:, :], in1=xs[:, nh * 512:(nh + 1) * 512],
                            op=ALU.add),
                            reads=[pk, xk], writes=[okk])
                    p.dma("sp", "o" + okk, lambda e, os_=os_, i=i: e.dma_start(
                        out=out_d[seq, i * 128:(i + 1) * 128, :], in_=os_[:]), reads=[okk])
                p.finish()
                p.emit()
    return nc


def host_consts(inp):
    cols = np.zeros((128, 32), np.float32)
    ng = inp["norm_gain"].reshape(1024)
    cols[:, 0:8] = ng.reshape(8, 128).T
    cols[:, 8] = np.tile(inp["a_q_norm"].reshape(64), 2)
    cols[:, 9] = np.tile(inp["a_k_norm"].reshape(64), 2)
    cols[:, 10:13] = inp["b_q_latent_norm"].reshape(3, 128).T
    cols[:, 13:15] = inp["b_kv_latent_norm"].reshape(2, 128).T
    cols[:96, 15] = inp["b_q_norm"].reshape(96)
    cols[:96, 16] = inp["b_k_norm"].reshape(96)
    pidx = np.arange(128)
    cols[:, 17] = np.power(np.float32(10000.0), -(pidx % 32).astype(np.float32) * np.float32(2.0) / np.float32(64))
    cols[:, 18] = np.power(np.float32(10000.0), -(pidx % 16).astype(np.float32) * np.float32(2.0) / np.float32(32))
    m64 = pidx % 64
    cols[:, 19] = np.where(m64 < 32, 1.0, -1.0)
    cols[:, 20] = np.where(m64 < 16, 1.0, np.where(m64 < 32, -1.0, 0.0))
    mI = (m64 < 32).astype(np.float32)
    cols[:, 21] = -mI
    cols[:, 22] = 1.0 - mI
    mats = np.zeros((128, 6, 128), np.float32)
    mats[:, 0, :] = np.eye(128)
    for m in range(128):
        pm = m + 32 if m64[m] < 32 else m - 32
        mats[pm, 1, m] = 1.0
        if m64[m] < 16:
            mats[m + 16, 2, m] = 1.0
        elif m64[m] < 32:
            mats[m - 16, 2, m] = 1.0
    mats[:, 3, :] = 1.0
    mats[:, 4, :] = (pidx[:, None] // 64 == pidx[None, :] // 64).astype(np.float32)
    pos = np.arange(S, dtype=np.float32)
    tabs = np.zeros((128, 4, S), np.float32)
    angA = pos[None, :] * cols[:, 17:18]
    angI = pos[None, :] * cols[:, 18:19]
    sgnA = np.where(m64 < 32, -1.0, 1.0)[:, None]
    sgnI = np.where(m64 < 16, -1.0, np.where(m64 < 32, 1.0, 0.0))[:, None]
    tabs[:, 0] = np.cos(angA)
    tabs[:, 1] = sgnA * np.sin(angA)
    tabs[:, 2] = np.where(mI[:, None] > 0, np.cos(angI), 1.0)
    tabs[:, 3] = sgnI * np.sin(angI)
    tabs = np.ascontiguousarray(tabs.reshape(128, 4, 4, 512).transpose(2, 0, 1, 3)).astype(np.float32)
    return cols, mats, tabs


def make_in_maps(inp):
    cols, mats, tabs = host_consts(inp)
    x = np.ascontiguousarray(inp["x"], dtype=np.float32)
    maps = []
    for c in range(NCORES):
        maps.append({
            "x": np.ascontiguousarray(x[c * SEQ_PER_CORE:(c + 1) * SEQ_PER_CORE]),
            "w_in": np.ascontiguousarray(inp["w_in"][0]),
            "w_uq": np.ascontiguousarray(inp["w_uq"][0]),
            "w_ukv": np.ascontiguousarray(inp["w_ukv"][0]),
            "w_out": np.ascontiguousarray(inp["w_out"][0]),
            "cols": cols, "mats": mats, "tabs": tabs,
        })
    return maps


def kernel(**inputs):
    inp = {k: np.asarray(v) for k, v in inputs.items()}
    nc = build_nc()
    res = run_bass_kernel_spmd(nc, make_in_maps(inp), core_ids=list(range(NCORES)))
    out = np.concatenate([np.asarray(r["out"]) for r in res.results], axis=0)
    return out.astype(np.float32)
```
